# Optimizing a Trainium2 kernel written in Bass

```python
import jax, jax.numpy as jnp
from jax import lax
import numpy as np

D_MODEL = 1024
BATCH = 8
SEQ = 2048
DEPTH = 1
DEC_BATCH = 4
DEC_SEQ = 8192
PAST_LEN = 128

N_META = 16
GRID_W = 64
EXPAND = 2
D_MIX = EXPAND * D_MODEL
D_POOL = D_MIX // 2
D_NA = D_MIX - D_POOL
POOL_WINDOWS = (2, 4, 8, 16)
N_POOL_GROUPS = len(POOL_WINDOWS)
POOL_GROUP_W = D_POOL // N_POOL_GROUPS
NA_HEAD_DIM = 64
NA_HEADS = D_NA // NA_HEAD_DIM
NA_KR_MAX = 8
NA_KC = 16
RMS_EPS = 1e-6
D_IN = 2 * D_POOL + 4 * D_NA

kernel_name = "hymba_pool_natten_encoder"


def rms_norm(x, g):
    xf = x.astype(jnp.float32)
    y = xf * lax.rsqrt(jnp.mean(xf * xf, axis=-1, keepdims=True) + RMS_EPS)
    return (y * g.astype(jnp.float32)).astype(x.dtype)


def pool_mixer(u, w_pool, pool_scale):
    B, L, _ = u.shape
    ug = u.reshape(B, L, N_POOL_GROUPS, POOL_GROUP_W)
    ugf = ug.astype(jnp.float32)
    cs = jnp.concatenate([jnp.zeros((B, 1, N_POOL_GROUPS, POOL_GROUP_W), jnp.float32),
                          jnp.cumsum(ugf, axis=1)], axis=1)
    w = jnp.array(POOL_WINDOWS, jnp.int32)
    t = jnp.arange(L, dtype=jnp.int32)
    start = t[:, None] - w[None, :] // 2
    lo = jnp.clip(start, 0, L)
    hi = jnp.clip(start + w[None, :], 0, L)
    gidx = jnp.arange(N_POOL_GROUPS)[None, :]
    win_sum = cs[:, hi, gidx, :] - cs[:, lo, gidx, :]
    count = (hi - lo).astype(jnp.float32)[None, :, :, None]
    pooled = (win_sum / count - ugf).astype(u.dtype)
    mixed = jnp.einsum('blgc,gcd->blgd', pooled, w_pool).reshape(B, L, D_POOL)
    return mixed * pool_scale


def neighborhood_attention(q, k, v, rpb, meta_bias):
    B, L, H, hd = q.shape
    T = L - N_META
    rows = T // GRID_W
    kr_n = min(NA_KR_MAX, rows)
    scale = hd ** -0.5
    qm, km, vm = q[:, :N_META], k[:, :N_META], v[:, :N_META]
    qg = q[:, N_META:].reshape(B, rows, GRID_W, H, hd)
    kg = k[:, N_META:].reshape(B, rows, GRID_W, H, hd)
    vg = v[:, N_META:].reshape(B, rows, GRID_W, H, hd)
    cols = np.arange(GRID_W)
    cstart = np.clip(cols - NA_KC // 2, 0, GRID_W - NA_KC)
    colidx = cstart[:, None] + np.arange(NA_KC)[None, :]
    coloff = colidx - cols[:, None] + (NA_KC - 1)
    mbias = meta_bias.astype(jnp.float32)[None, :, None, :]

    def row_block(r):
        rs = jnp.clip(r - kr_n // 2, 0, rows - kr_n)
        q_r = lax.dynamic_index_in_dim(qg, r, axis=1, keepdims=False)
        k_r = lax.dynamic_slice_in_dim(kg, rs, kr_n, axis=1)[:, :, colidx]
        v_r = lax.dynamic_slice_in_dim(vg, rs, kr_n, axis=1)[:, :, colidx]
        rowoff = rs + jnp.arange(kr_n) - r + (NA_KR_MAX - 1)
        bias = rpb[:, rowoff][:, :, coloff]
        bias = bias.transpose(0, 2, 1, 3).astype(jnp.float32)[None]
        s_grid = jnp.einsum('bqhd,bkqjhd->bhqkj', q_r, k_r).astype(jnp.float32) * scale + bias
        s_grid = s_grid.reshape(B, H, GRID_W, kr_n * NA_KC)
        s_meta = jnp.einsum('bqhd,bmhd->bhqm', q_r, km).astype(jnp.float32) * scale + mbias
        p = jax.nn.softmax(jnp.concatenate([s_meta, s_grid], axis=-1), axis=-1).astype(v.dtype)
        p_m = p[..., :N_META]
        p_g = p[..., N_META:].reshape(B, H, GRID_W, kr_n, NA_KC)
        return (jnp.einsum('bhqm,bmhd->bqhd', p_m, vm)
                + jnp.einsum('bhqkj,bkqjhd->bqhd', p_g, v_r))

    out_g = lax.map(row_block, jnp.arange(rows))
    out_g = out_g.transpose(1, 0, 2, 3, 4).reshape(B, T, H, hd)
    s_mm = jnp.einsum('bqhd,bmhd->bhqm', qm, km).astype(jnp.float32) * scale + mbias
    p_mm = jax.nn.softmax(s_mm, axis=-1).astype(v.dtype)
    out_m = jnp.einsum('bhqm,bmhd->bqhd', p_mm, vm)
    return jnp.concatenate([out_m, out_g], axis=1).reshape(B, L, H * hd)


def mixer_layer(x, norm_g, w_in, w_pool, pool_scale, rpb, meta_bias, w_out):
    B, L, _ = x.shape
    h = rms_norm(x, norm_g)
    proj = h @ w_in
    splits = [D_POOL, 2 * D_POOL, 2 * D_POOL + D_NA, 2 * D_POOL + 2 * D_NA, 2 * D_POOL + 3 * D_NA]
    u, g_pool, q, k, v, g_na = jnp.split(proj, splits, axis=-1)
    pool_out = pool_mixer(u, w_pool, pool_scale) * jax.nn.silu(g_pool)
    shp = (B, L, NA_HEADS, NA_HEAD_DIM)
    na_out = neighborhood_attention(q.reshape(shp), k.reshape(shp), v.reshape(shp),
                                    rpb, meta_bias) * jax.nn.silu(g_na)
    return x + jnp.concatenate([pool_out, na_out], axis=-1) @ w_out


def encode(x, meta_tokens, norm_g, w_in, w_pool, pool_scale, rpb, meta_bias, w_out, final_g):
    B = x.shape[0]
    meta = jnp.broadcast_to(meta_tokens[None].astype(x.dtype), (B, N_META, x.shape[-1]))
    h = jnp.concatenate([meta, x], axis=1)
    for l in range(DEPTH):
        h = mixer_layer(h, norm_g[l], w_in[l], w_pool[l], pool_scale[l], rpb[l],
                        meta_bias[l], w_out[l])
    return rms_norm(h, final_g)[:, N_META:]


def setup_inputs(seed: int = 0) -> dict:
    key = jax.random.key(seed)
    ks = jax.random.split(key, 12)
    f32 = jnp.float32
    return {
        "x_prompt": jax.random.normal(ks[0], (BATCH, SEQ, D_MODEL), f32),
        "x_sample": jax.random.normal(ks[1], (DEC_BATCH, DEC_SEQ, D_MODEL), f32),
        "meta_tokens": jax.random.normal(ks[2], (N_META, D_MODEL), f32),
        "norm_g": 1.0 + 0.02 * jax.random.normal(ks[3], (DEPTH, D_MODEL), f32),
        "w_in": jax.random.normal(ks[4], (DEPTH, D_MODEL, D_IN), f32) * D_MODEL ** -0.5,
        "w_pool": jax.random.normal(ks[5], (DEPTH, N_POOL_GROUPS, POOL_GROUP_W, POOL_GROUP_W), f32) * POOL_GROUP_W ** -0.5,
        "pool_scale": 1.0 + 0.02 * jax.random.normal(ks[6], (DEPTH, D_POOL), f32),
        "rpb": 0.02 * jax.random.normal(ks[7], (DEPTH, NA_HEADS, 2 * NA_KR_MAX - 1, 2 * NA_KC - 1), f32),
        "meta_bias": 0.02 * jax.random.normal(ks[8], (DEPTH, NA_HEADS, N_META), f32),
        "w_out": jax.random.normal(ks[9], (DEPTH, D_MIX, D_MODEL), f32) * D_MIX ** -0.5,
        "final_g": 1.0 + 0.02 * jax.random.normal(ks[10], (D_MODEL,), f32),
    }


def reference(x_prompt, x_sample, meta_tokens, norm_g, w_in, w_pool, pool_scale, rpb,
              meta_bias, w_out, final_g):
    y_prompt = encode(x_prompt, meta_tokens, norm_g, w_in, w_pool, pool_scale, rpb,
                      meta_bias, w_out, final_g)
    y_sample = encode(x_sample, meta_tokens, norm_g, w_in, w_pool, pool_scale, rpb,
                      meta_bias, w_out, final_g)
    return (y_prompt, y_sample)
```

```python
import re
import numpy as np
from contextlib import ExitStack
import concourse.bass as bass
import concourse.mybir as mybir
from concourse.bass_utils import run_bass_kernel_spmd

F32 = mybir.dt.float32
BF16 = mybir.dt.bfloat16
AF = mybir.ActivationFunctionType
ALU = mybir.AluOpType

NCORES = 8
D = 1024
KC = 8
NH = 16
HD = 64
VW = NH * 65
TA = 16
TB = 34
NT = TA + TB + 1
MT = TA + TB
EPS = 1e-6
POOL_W = (2, 4, 8, 16)
ENGS = ['sync', 'gpsimd', 'scalar', 'vector', 'tensor']


class Op:
    __slots__ = ('eng', 'fn', 'deps', 'needed', 'val', 'is_dma', 'sem', 'idx', 'ent')


class Sched:
    def __init__(self, nc, es):
        self.nc = nc
        self.esem = {e: es.enter_context(nc.semaphore('s_' + e)) for e in ENGS}
        self.ecnt = {e: 0 for e in ENGS}
        self.dsem = {}
        self.es = es
        self.reset()

    def reset(self):
        self.q = {e: [] for e in ENGS}
        self.lastw = {}
        self.readers = {}
        self.n = 0

    def _tok(self, o):
        return o.sem if o.is_dma else o.eng

    def op(self, eng, fn, reads=(), writes=(), dma=None):
        o = Op()
        o.eng = eng; o.fn = fn; o.needed = False; o.val = None; o.idx = self.n
        self.n += 1
        o.is_dma = dma is not None
        o.sem = None
        if o.is_dma:
            if dma not in self.dsem:
                self.dsem[dma] = [self.es.enter_context(self.nc.semaphore('d_' + re.sub('[^0-9a-zA-Z]+', '_', str(dma)))), 0]
            ent = self.dsem[dma]
            o.sem = ent[0]; o.ent = ent
        deps = {}

        def add(d):
            if d is None:
                return
            if (not d.is_dma) and (not o.is_dma) and d.eng == 'tensor' and eng == 'tensor':
                return
            k = id(d.sem) if d.is_dma else d.eng
            cur = deps.get(k)
            if cur is None or cur.idx < d.idx:
                deps[k] = d
        for r in reads:
            add(self.lastw.get(r))
        for r in writes:
            add(self.lastw.get(r))
            for d in self.readers.get(r, {}).values():
                add(d)
        o.deps = [(d, (d.ent[1] if d.is_dma else None)) for d in deps.values()]
        if o.is_dma:
            o.ent[1] += 16
            o.val = o.ent[1]
        for d, _ in o.deps:
            d.needed = True
        for r in reads:
            self.readers.setdefault(r, {})[(id(o.sem) if o.is_dma else eng)] = o
        for r in writes:
            self.lastw[r] = o
            self.readers[r] = {}
        self.q[eng].append(o)
        return o

    def flush(self):
        nc = self.nc
        for e in ENGS:
            for o in self.q[e]:
                if (not o.is_dma) and o.needed:
                    self.ecnt[e] += 1
                    o.val = self.ecnt[e]
        final_dma = [(ent[0], ent[1]) for ent in self.dsem.values() if ent[1] > 0]
        with nc.Block(no_gpsimd_drain=True) as block:
            for e in ENGS:
                ops = self.q[e]

                def body(eng, ops=ops, e=e):
                    waited = {}
                    for o in ops:
                        for d, dv in o.deps:
                            sem = d.sem if d.is_dma else self.esem[d.eng]
                            val = dv if d.is_dma else d.val
                            if waited.get(id(sem), 0) >= val:
                                continue
                            eng.wait_ge(sem, val)
                            waited[id(sem)] = val
                        ins = o.fn(eng)
                        if o.is_dma:
                            ins.then_inc(o.sem, 16)
                        elif o.needed:
                            ins.then_inc(self.esem[e], 1)
                    if e == 'sync':
                        for sem, val in final_dma:
                            if waited.get(id(sem), 0) < val:
                                eng.wait_ge(sem, val)
                getattr(block, e)(body)
        self.reset()


def seq_chunks(t0, ntiles):
    out = []
    t = 0
    while t < ntiles:
        n = min(4, ntiles - t)
        out.append((t0 + t, n))
        t += n
    return out


def build_program():
    nc = bass.Bass('TRN2', target_bir_lowering=False)
    dt = lambda name, shape, dtype, kind: nc.dram_tensor(name, shape, dtype, kind=kind).ap()
    xall = dt('xall', [NT * 128, D], F32, 'ExternalInput')
    w_in = dt('w_in', [D, 6144], F32, 'ExternalInput')
    w_pool = dt('w_pool', [4, 256, 256], F32, 'ExternalInput')
    w_out = dt('w_out', [2048, D], F32, 'ExternalInput')
    gT_d = dt('gT', [128, KC], F32, 'ExternalInput')
    psT_d = dt('psT', [128, KC], F32, 'ExternalInput')
    fgb_d = dt('fgb', [128, D], F32, 'ExternalInput')
    mbT_d = dt('mbT', [16, NH], F32, 'ExternalInput')
    b0_d = dt('b0', [64, NH, 16, 64], F32, 'ExternalInput')
    mask_d = dt('mask01', [64, 64], F32, 'ExternalInput')
    ident_d = dt('ident', [128, 128], F32, 'ExternalInput')
    bandw_d = dt('bandw', [128, 4, 112], F32, 'ExternalInput')
    bandl64_d = dt('bandl64', [128, 4, 64], F32, 'ExternalInput')
    bandl32_d = dt('bandl32', [128, 4, 32], F32, 'ExternalInput')
    invl64_d = dt('invl64', [128, 4, 64], F32, 'ExternalInput')
    invl32_d = dt('invl32', [128, 4, 32], F32, 'ExternalInput')
    y_d = dt('y', [(TA + TB) * 128, D], F32, 'ExternalOutput')
    hT_d = dt('hT_s', [NT * 128 * D], BF16, 'Internal')
    kT_d = dt('kT_s', [NT * 128, D], BF16, 'Internal')
    v_d = dt('v_s', [NT * 128, VW], BF16, 'Internal')
    u_d = dt('u_s', [NT * 128, D], BF16, 'Internal')
    at_d = dt('at_s', [NT * 128, D], BF16, 'Internal')
    w_in_v = w_in.rearrange('(kc p) n -> p kc n', p=128)
    w_out_v = w_out.rearrange('(kc p) n -> p kc n', p=128)
    w_pool_v = w_pool.rearrange('g (k p) d -> p (g k) d', p=128)
    kT_v = lambda t: kT_d[t * 128:(t + 1) * 128, :].rearrange('p (a b) -> p a b', a=KC)
    hT_v = lambda t0, nt: hT_d[t0 * 128 * D:(t0 + nt) * 128 * D].rearrange('(p k n) -> p k n', p=128, k=KC)

    with ExitStack() as top:
        S = Sched(nc, top)
        bank2 = [top.enter_context(nc.psum_tensor('bankpair%d' % b, [128, 1024], F32)) for b in range(4)]
        banks = []
        for b in range(4):
            banks.append(bank2[b][:, 0:512])
            banks.append(bank2[b][:, 512:1024])
        PS = lambda b: ('ps', b)
        sb = lambda es, name, shape, dtype: es.enter_context(nc.sbuf_tensor('sb_' + name, shape, dtype))

        ident = sb(top, 'ident', [128, 128], BF16)
        gT = sb(top, 'gTs', [128, KC], F32)
        epsb = sb(top, 'epsb', [128, 1], F32)
        Wp = sb(top, 'Wp', [128, 8, 256], BF16)
        bandw = sb(top, 'bandw', [128, 4, 112], BF16)
        bandl = {64: sb(top, 'bandl64', [128, 4, 64], BF16), 32: sb(top, 'bandl32', [128, 4, 32], BF16)}
        invl = {64: sb(top, 'invl64', [128, 4, 64], F32), 32: sb(top, 'invl32', [128, 4, 32], F32)}
        psc = sb(top, 'psc', [128, KC], F32)

        def evac(eng, dst, src, reads, writes):
            if eng == 'vector':
                S.op('vector', lambda e: e.tensor_copy(dst(), src()), reads=reads, writes=writes)
            else:
                S.op('scalar', lambda e: e.copy(dst(), src()), reads=reads, writes=writes)

        with ExitStack() as es12:
            Wq = sb(es12, 'Wq', [128, KC, 1024], BF16)
            EB = sb(es12, 'EB', [128, NH, 15, 64], BF16)
            E_m2 = sb(es12, 'E_m2', [128, NH, 128], BF16)
            E5 = sb(es12, 'E5', [80, NH, 128], BF16)
            EM = sb(es12, 'EM', [16, NH, 128], BF16)

            with ExitStack() as es:
                ctmp = sb(es, 'ctmp', [128, 128], F32)
                S.op('sync', lambda e: e.dma_start(out=ctmp[:], in_=ident_d), writes=['ctmp'], dma='c0')
                S.op('vector', lambda e: e.tensor_copy(ident[:], ctmp[:]), reads=['ctmp'], writes=['ident'])
                S.op('sync', lambda e: e.dma_start(out=gT[:], in_=gT_d), writes=['gT'], dma='c1')
                S.op('vector', lambda e: e.memset(epsb[:], EPS), writes=['epsb'])

                Wk = sb(es, 'Wk', [128, KC, 1024], BF16)
                Wvu = sb(es, 'Wvu', [128, KC, 2048], BF16)
                def p1_weights(xdeps):
                    for cb in range(4):
                        S.op('gpsimd', lambda e, cb=cb: e.dma_start(out=Wk[:, :, cb * 256:(cb + 1) * 256],
                                                                   in_=w_in_v[:, :, 3072 + cb * 256:3072 + (cb + 1) * 256]),
                             reads=(xdeps if cb == 0 else []), writes=[('Wk', cb)], dma=('wk', cb))
                    for cb in range(4):
                        src0 = 4096 if cb < 2 else 0
                        S.op('gpsimd', lambda e, cb=cb, src0=src0: e.dma_start(
                            out=Wvu[:, :, cb * 512:(cb + 1) * 512],
                            in_=w_in_v[:, :, src0 + (cb % 2) * 512:src0 + (cb % 2 + 1) * 512]),
                             writes=[('Wvu', cb)], dma=('wvu', cb))
                    S.op('gpsimd', lambda e: e.dma_start(out=bandw[:], in_=bandw_d), writes=['band'], dma='c7')
                    S.op('gpsimd', lambda e: e.dma_start(out=bandl[64][:], in_=bandl64_d), writes=['band'], dma='c7')
                    S.op('gpsimd', lambda e: e.dma_start(out=bandl[32][:], in_=bandl32_d), writes=['band'], dma='c7')
                    S.op('gpsimd', lambda e: e.dma_start(out=Wp[:], in_=w_pool_v), writes=['Wp'], dma='wp')

                S.op('sync', lambda e: e.dma_start(out=psc[:], in_=psT_d), writes=['psc'], dma='c6')
                S.op('vector', lambda e: e.tensor_scalar(out=psc[:], in0=psc[:], scalar1=0.5, scalar2=None, op0=ALU.mult),
                     reads=['psc'], writes=['psc'])
                S.op('sync', lambda e: e.dma_start(out=invl[64][:], in_=invl64_d), writes=['band'], dma='c8')
                S.op('sync', lambda e: e.dma_start(out=invl[32][:], in_=invl32_d), writes=['band'], dma='c8')

                NX = 4
                X = [sb(es, 'X%d' % i, [128, D], F32) for i in range(NX)]
                XN = [sb(es, 'XN%d' % i, [128, D], BF16) for i in range(2)]
                junk = sb(es, 'junk', [128, D], BF16)
                ssq = sb(es, 'ssq', [128, 8], F32)
                sd = sb(es, 'sd', [128, 8], F32)
                rstd = sb(es, 'rstd', [128, 8], F32)
                HT = [sb(es, 'HT%d' % i, [128, KC, 512], BF16) for i in range(2)]
                KTs = [sb(es, 'KTs%d' % i, [128, KC, 512], BF16) for i in range(2)]
                Vs = [sb(es, 'Vs%d' % i, [128, NH, 65], BF16) for i in range(4)]
                Us = [sb(es, 'Us%d' % i, [128, D], BF16) for i in range(4)]
                for i in range(4):
                    S.op('vector', lambda e, i=i: e.memset(Vs[i][:, :, 64:65], 2.0), writes=[('Vs', i)])

                chunks = seq_chunks(0, TA) + seq_chunks(TA, TB) + [(MT, 1)]
                st = dict(xcnt=0, pbank=2, vcnt=0, ev=0, scnt=0)

                def nb1():
                    b = st['pbank']; st['pbank'] = 2 + (b - 1) % 6
                    return b

                def ev1():
                    st['ev'] += 1
                    return 'vector' if st['ev'] % 2 == 0 else 'scalar'

                tile_sc = {}

                tile_xs = {}

                def p1_load(ci, j):
                    t0, nt = chunks[ci]
                    t = t0 + j
                    xs = st['xcnt'] % NX; st['xcnt'] += 1
                    tile_xs[(ci, j)] = xs
                    S.op('sync', lambda e, xs=xs, t=t: e.dma_start(out=X[xs][:], in_=xall[t * 128:(t + 1) * 128, :]),
                         writes=[('X', xs)], dma=('X', xs))

                def p1_front(ci, j):
                    if (ci, j) not in tile_xs:
                        p1_load(ci, j)
                    xs = tile_xs[(ci, j)]
                    sc = st['scnt'] % 8; st['scnt'] += 1
                    tile_sc[(ci, j)] = sc
                    S.op('scalar', lambda e, xs=xs, sc=sc: e.activation(out=junk[:], in_=X[xs][:], func=AF.Square,
                                                                       accum_out=ssq[:, sc:sc + 1]),
                         reads=[('X', xs)], writes=['junk', ('ssq', sc)])
                    S.op('scalar', lambda e, sc=sc: e.activation(out=sd[:, sc:sc + 1], in_=ssq[:, sc:sc + 1], func=AF.Sqrt,
                                                                bias=epsb[:], scale=1.0 / D),
                         reads=[('ssq', sc), 'epsb'], writes=[('sd', sc)])
                    S.op('vector', lambda e, sc=sc: e.reciprocal(rstd[:, sc:sc + 1], sd[:, sc:sc + 1]),
                         reads=[('sd', sc)], writes=[('rstd', sc)])
                    xn = sc % 2
                    S.op('vector', lambda e, xs=xs, xn=xn, sc=sc: e.tensor_scalar(
                        out=XN[xn][:], in0=X[xs][:], scalar1=rstd[:, sc:sc + 1], scalar2=None, op0=ALU.mult),
                         reads=[('X', xs), ('rstd', sc)], writes=[('XN', xn)])

                def p1_T(ci, j):
                    sc = tile_sc[(ci, j)]
                    xn = sc % 2
                    tb = sc % 2
                    tpb = banks[tb].bitcast(BF16)
                    for kc in range(KC):
                        S.op('tensor', lambda e, tpb=tpb, xn=xn, kc=kc: e.transpose(
                            tpb[:, kc * 128:(kc + 1) * 128], XN[xn][:, kc * 128:(kc + 1) * 128], ident[:]),
                             reads=[('XN', xn), 'ident'], writes=[PS(tb)])

                def p1_HTevac(ci, j):
                    sc = tile_sc[(ci, j)]
                    hs = ci % 2
                    tb = sc % 2
                    tpb = banks[tb].bitcast(BF16)
                    S.op('vector', lambda e, tpb=tpb, hs=hs, j=j: e.tensor_tensor(
                        out=HT[hs][:, :, j * 128:(j + 1) * 128],
                        in0=tpb[:, 0:1024].rearrange('p (a b) -> p a b', a=KC),
                        in1=gT[:, :].unsqueeze(2).to_broadcast([128, KC, 128]), op=ALU.mult),
                         reads=[PS(tb), 'gT'], writes=[('HT', hs)])

                def p1_K(ci):
                    t0, nt = chunks[ci]
                    N = nt * 128
                    hs = ci % 2
                    for fc in range(KC):
                        pb = nb1()
                        for kc in range(KC):
                            S.op('tensor', lambda e, pb=pb, fc=fc, kc=kc, hs=hs, N=N: e.matmul(
                                banks[pb][:, 0:N], Wk[:, kc, fc * 128:(fc + 1) * 128], HT[hs][:, kc, 0:N],
                                start=(kc == 0), stop=(kc == KC - 1)),
                                 reads=[('HT', hs), ('Wk', fc // 2)], writes=[PS(pb)])
                        evac(ev1(), lambda fc=fc, hs=hs, N=N: KTs[hs][:, fc, 0:N], lambda pb=pb, N=N: banks[pb][:, 0:N],
                             [PS(pb)], [('KTs', hs)])
                    for j in range(nt):
                        t = t0 + j
                        S.op('gpsimd', lambda e, hs=hs, t=t, j=j: e.dma_start(out=kT_v(t), in_=KTs[hs][:, :, j * 128:(j + 1) * 128]),
                             reads=[('KTs', hs)], writes=[('kT_d', t)], dma=('sK', hs))

                def p1_VU(ci, j):
                    t0, nt = chunks[ci]
                    N = nt * 128
                    hs = ci % 2
                    t = t0 + j
                    vs = st['vcnt'] % 4; st['vcnt'] += 1
                    for cg in range(4):
                        pb = nb1()
                        for kc in range(KC):
                            S.op('tensor', lambda e, pb=pb, cg=cg, kc=kc, hs=hs, j=j: e.matmul(
                                banks[pb][:, :], HT[hs][:, kc, j * 128:(j + 1) * 128], Wvu[:, kc, cg * 512:(cg + 1) * 512],
                                start=(kc == 0), stop=(kc == KC - 1)),
                                 reads=[('HT', hs), ('Wvu', cg)], writes=[PS(pb)])
                        if cg < 2:
                            evac(ev1(), lambda vs=vs, cg=cg: Vs[vs][:, cg * 8:(cg + 1) * 8, 0:64],
                                 lambda pb=pb: banks[pb][:, :].rearrange('p (a b) -> p a b', a=8), [PS(pb)], [('Vs', vs)])
                        else:
                            evac(ev1(), lambda vs=vs, cg=cg: Us[vs][:, (cg - 2) * 512:(cg - 1) * 512],
                                 lambda pb=pb: banks[pb][:, :], [PS(pb)], [('Us', vs)])
                    S.op('gpsimd', lambda e, vs=vs, t=t: e.dma_start(out=v_d[t * 128:(t + 1) * 128, :],
                                                                   in_=Vs[vs][:].rearrange('p a b -> p (a b)')),
                         reads=[('Vs', vs)], writes=[('v_d', t)], dma=('sV', vs))
                    S.op('gpsimd', lambda e, vs=vs, t=t: e.dma_start(out=u_d[t * 128:(t + 1) * 128, :], in_=Us[vs][:]),
                         reads=[('Us', vs)], writes=[('u_d', t)], dma=('sU', vs))
                    if j == nt - 1:
                        S.op('gpsimd', lambda e, hs=hs, t0=t0, nt=nt, N=N: e.dma_start(out=hT_v(t0, nt), in_=HT[hs][:, :, 0:N]),
                             reads=[('HT', hs)], writes=[('hT_d', t0)], dma=('sH', hs))

                msk = sb(es, 'msk', [128, 64], F32)
                mbs = sb(es, 'mbs', [80, NH], F32)
                mbe = sb(es, 'mbe', [80, NH], F32)
                EBf = sb(es, 'EBf', [128, 4, 15, 64], F32)

                def p1_tables(hg):
                    if hg == 0:
                        S.op('sync', lambda e: e.dma_start(out=msk[0:64, :], in_=mask_d), writes=['msk'], dma='c2')
                        S.op('sync', lambda e: e.dma_start(out=msk[64:128, :], in_=mask_d), writes=['msk'], dma='c2')
                        S.op('sync', lambda e: e.dma_start(out=mbs[0:16, :], in_=mbT_d), writes=['mbs'], dma='c3')
                        S.op('sync', lambda e: e.dma_start(out=mbs[64:80, :], in_=mbT_d), writes=['mbs'], dma='c3')
                    S.op('sync', lambda e, hg=hg: e.dma_start(out=EBf[0:64], in_=b0_d[:, hg * 4:(hg + 1) * 4, 1:16, :]),
                         writes=['EBf'], dma='c4')
                    S.op('sync', lambda e, hg=hg: e.dma_start(out=EBf[64:128], in_=b0_d[:, hg * 4:(hg + 1) * 4, 0:15, :]),
                         writes=['EBf'], dma='c4')
                    S.op('scalar', lambda e: e.activation(out=EBf[:], in_=EBf[:], func=AF.Exp), reads=['EBf'], writes=['EBf'])
                    S.op('vector', lambda e, hg=hg: e.tensor_tensor(
                        out=EB[:, hg * 4:(hg + 1) * 4, :, :].rearrange('p a b c -> p (a b) c'),
                        in0=EBf[:].rearrange('p a b c -> p (a b) c'),
                        in1=msk[:, :].unsqueeze(1).to_broadcast([128, 60, 64]), op=ALU.mult),
                         reads=['EBf', 'msk'], writes=['EB'])
                    if hg == 3:
                        S.op('scalar', lambda e: e.activation(out=mbe[0:16, :], in_=mbs[0:16, :], func=AF.Exp), reads=['mbs'], writes=['mbe'])
                        S.op('scalar', lambda e: e.activation(out=mbe[64:80, :], in_=mbs[64:80, :], func=AF.Exp), reads=['mbs'], writes=['mbe'])
                        S.op('vector', lambda e: e.tensor_copy(E_m2[:].rearrange('p h (a b) -> p h a b', a=2), EB[:, :, 11:13, :]),
                             reads=['EB'], writes=['E_m2'])
                        S.op('vector', lambda e: e.memset(E_m2[0:64, :, 64:128], 0.0), writes=['E_m2'])
                        S.op('vector', lambda e: e.memset(E5[0:64, :, 0:64], 0.0), writes=['E5'])
                        S.op('vector', lambda e: e.tensor_copy(E5[0:64, :, 64:128], EB[0:64, :, 4, :]), reads=['EB'], writes=['E5'])
                        S.op('vector', lambda e: e.tensor_copy(E5[64:80, :, :], mbe[64:80, :].unsqueeze(2).to_broadcast([16, NH, 128])),
                             reads=['mbe'], writes=['E5'])
                        S.op('vector', lambda e: e.tensor_copy(EM[0:16, :, :], mbe[0:16, :].unsqueeze(2).to_broadcast([16, NH, 128])),
                             reads=['mbe'], writes=['EM'])
                        for cb in range(4):
                            S.op('gpsimd', lambda e, cb=cb: e.dma_start(out=Wq[:, :, cb * 256:(cb + 1) * 256],
                                                                       in_=w_in_v[:, :, 2048 + cb * 256:2048 + (cb + 1) * 256]),
                                 writes=['Wq'], dma='wq')

                for j in range(chunks[0][1]):
                    p1_load(0, j)
                p1_weights([('X', tile_xs[(0, j)]) for j in range(chunks[0][1])])
                for j in range(chunks[0][1]):
                    p1_front(0, j)
                    p1_T(0, j)
                    p1_HTevac(0, j)
                for ci in range(len(chunks)):
                    ntc = chunks[ci][1]
                    ntn = chunks[ci + 1][1] if ci + 1 < len(chunks) else 0
                    if ntn > 0:
                        p1_front(ci + 1, 0)
                    p1_K(ci)
                    for j in range(max(ntc, ntn)):
                        if j < ntn:
                            p1_T(ci + 1, j)
                        if j + 1 < ntn:
                            p1_front(ci + 1, j + 1)
                        if j < ntn:
                            p1_HTevac(ci + 1, j)
                        if j < ntc:
                            p1_VU(ci, j)
                    if 1 <= ci <= 4:
                        p1_tables(ci - 1)
                S.flush()

            with ExitStack() as es:
                KM = [sb(es, 'KM%d' % p, [128, KC, 16], BF16) for p in range(2)]
                VM0 = sb(es, 'VM0', [16, VW], BF16)
                HT2 = [sb(es, 'HT2_%d' % i, [128, KC, 512], BF16) for i in range(2)]
                _t0, _nt = seq_chunks(0, TA)[0]
                S.op('sync', lambda e: e.dma_start(out=HT2[0][:, :, 0:_nt * 128], in_=hT_v(_t0, _nt)),
                     reads=[('hT_d', _t0)], writes=[('HT2', 0)], dma=('HT2', 0))
                for p in range(2):
                    S.op('vector', lambda e, p=p: e.memset(KM[p][(1 - p) * 64:(2 - p) * 64, :, :], 0.0), writes=[('KMz', p)])
                    S.op('sync', lambda e, p=p: e.dma_start(out=KM[p][p * 64:(p + 1) * 64, :, :],
                                                          in_=kT_v(MT)[p * 64:(p + 1) * 64, :, 0:16]), writes=[('KM', p)], dma='c5')
                S.op('sync', lambda e: e.dma_start(out=VM0[:], in_=v_d[MT * 128:MT * 128 + 16, :]), writes=['VM0'], dma='c5')
                RK = 8
                RX = 4
                KR = [[sb(es, 'KR%d_%d' % (p, i), [128, KC, 128], BF16) for i in range(RK)] for p in range(2)]
                VR = [sb(es, 'VR%d' % i, [128, VW], BF16) for i in range(RK)]
                KX = [[sb(es, 'KX%d_%d' % (p, i), [128, KC, 80], BF16) for i in range(RX)] for p in range(2)]
                VX = [sb(es, 'VX%d' % i, [80, VW], BF16) for i in range(RX)]
                for i in range(RK):
                    for p in range(2):
                        S.op('vector' if p == 0 else 'gpsimd',
                             lambda e, i=i, p=p: e.memset(KR[p][i][(1 - p) * 64:(2 - p) * 64, :, :], 0.0), writes=[('KRz', p, i)])
                for i in range(RX):
                    for p in range(2):
                        S.op('vector' if p == 0 else 'gpsimd',
                             lambda e, i=i, p=p: e.memset(KX[p][i][(1 - p) * 64:(2 - p) * 64, :, :], 0.0), writes=[('KXz', p, i)])
                        S.op('sync', lambda e, i=i, p=p: e.dma_start(out=KX[p][i][p * 64:(p + 1) * 64, :, 64:80],
                                                                   in_=kT_v(MT)[p * 64:(p + 1) * 64, :, 0:16]),
                             writes=[('KXm', p, i)], dma='c5')
                    S.op('sync', lambda e, i=i: e.dma_start(out=VX[i][64:80, :], in_=v_d[MT * 128:MT * 128 + 16, :]),
                         writes=[('VXm', i)], dma='c5')
                QT = [sb(es, 'QT%d' % i, [128, KC, 512], BF16) for i in range(2)]
                PT = [sb(es, 'PT%d' % i, [128, 5, 2, 512], BF16) for i in range(2)]
                ATT = [sb(es, 'ATT%d' % i, [128, D], BF16) for i in range(2)]
                rec = sb(es, 'rec', [128, 8], F32)

                st2 = dict(qb=6, sbank=0, ev=0)
                seqs = [(0, TA), (TA, TB)]
                p2chunks = []
                for si, (s0, sn) in enumerate(seqs):
                    for (t0, nt) in seq_chunks(s0, sn):
                        p2chunks.append((si, t0, nt))
                loaded_k = [set(), set()]
                loaded_x = [set(), set()]

                def need_k(si, tl):
                    s0, sn = seqs[si]
                    if tl < 0 or tl >= sn or tl in loaded_k[si]:
                        return
                    loaded_k[si].add(tl)
                    t = s0 + tl
                    sl = t % RK
                    for p in range(2):
                        S.op('sync', lambda e, sl=sl, t=t, p=p: e.dma_start(out=KR[p][sl][p * 64:(p + 1) * 64, :, :],
                                                                        in_=kT_v(t)[p * 64:(p + 1) * 64]),
                             reads=[('kT_d', t)], writes=[('KR', p, sl)], dma=('KR', sl))
                    S.op('sync', lambda e, sl=sl, t=t: e.dma_start(out=VR[sl][:], in_=v_d[t * 128:(t + 1) * 128, :]),
                         reads=[('v_d', t)], writes=[('VR', sl)], dma=('VR', sl))

                def need_x(si, tl):
                    s0, sn = seqs[si]
                    if tl < 0 or tl >= sn or tl in loaded_x[si]:
                        return
                    loaded_x[si].add(tl)
                    t = s0 + tl
                    sl = t % RX
                    for p in range(2):
                        S.op('sync', lambda e, sl=sl, t=t, p=p: e.dma_start(out=KX[p][sl][p * 64:(p + 1) * 64, :, 0:64],
                                                                        in_=kT_v(t)[p * 64:(p + 1) * 64, :, 0:64]),
                             reads=[('kT_d', t)], writes=[('KX', p, sl)], dma=('KX', sl))
                    S.op('sync', lambda e, sl=sl, t=t: e.dma_start(out=VX[sl][0:64, :], in_=v_d[t * 128:t * 128 + 64, :]),
                         reads=[('v_d', t)], writes=[('VX', sl)], dma=('VX', sl))

                def tile_blocks(si, tl):
                    s0, sn = seqs[si]
                    if tl == 0:
                        kts, deltas, special = [0, 1, 2, 3], [0, 1, 2, 3], 'M'
                    elif tl == 1:
                        kts, deltas, special = [0, 1, 2, 3], [-1, 0, 1, 2], 'M'
                    elif tl == sn - 2:
                        kts, deltas, special = [sn - 4, sn - 3, sn - 2, sn - 1], [-2, -1, 0, 1], 'M'
                    elif tl == sn - 1:
                        kts, deltas, special = [sn - 4, sn - 3, sn - 2, sn - 1], [-3, -2, -1, 0], 'M'
                    else:
                        kts, deltas, special = [tl - 2, tl - 1, tl, tl + 1], [-2, -1, 0, 1], 'X'
                    return kts, deltas, special

                def ensure_tile(si, tl):
                    s0, sn = seqs[si]
                    if tl < 0 or tl >= sn:
                        return
                    kts, deltas, special = tile_blocks(si, tl)
                    for kt in kts:
                        need_k(si, kt)
                    if special == 'X':
                        need_x(si, tl + 2)

                def p2_load(ci):
                    si, t0, nt = p2chunks[ci]
                    qs = ci % 2
                    N = nt * 128
                    S.op('sync', lambda e, qs=qs, t0=t0, nt=nt, N=N: e.dma_start(out=HT2[qs][:, :, 0:N], in_=hT_v(t0, nt)),
                         reads=[('hT_d', t0)], writes=[('HT2', qs)], dma=('HT2', qs))

                qstate = {}

                def p2_qmm(ci, fc, kcs):
                    si, t0, nt = p2chunks[ci]
                    qs = ci % 2
                    N = nt * 128
                    if (ci, fc) not in qstate:
                        pb = st2['qb']; st2['qb'] = 6 + (pb - 5) % 2
                        qstate[(ci, fc)] = pb
                    pb = qstate[(ci, fc)]
                    for kc in kcs:
                        S.op('tensor', lambda e, pb=pb, fc=fc, kc=kc, qs=qs, N=N: e.matmul(
                            banks[pb][:, 0:N], Wq[:, kc, fc * 128:(fc + 1) * 128], HT2[qs][:, kc, 0:N],
                            start=(kc == 0), stop=(kc == KC - 1)),
                             reads=[('HT2', qs), 'Wq'], writes=[PS(pb)])
                    if kcs and kcs[-1] == KC - 1:
                        evac('vector', lambda fc=fc, qs=qs, N=N: QT[qs][:, fc, 0:N],
                             lambda pb=pb, N=N: banks[pb][:, 0:N], [PS(pb)], [('QT', qs)])

                def p2_qpiece(ci, fc):
                    p2_qmm(ci, fc, list(range(KC)))

                units = []
                for ci, (si, t0, nt) in enumerate(p2chunks):
                    for j in range(nt):
                        for G in range(2):
                            units.append((ci, j, G))

                def unit_blocks(u):
                    ci, j, G = u
                    si, t0, nt = p2chunks[ci]
                    s0, sn = seqs[si]
                    tl = t0 - s0 + j
                    kts, deltas, special = tile_blocks(si, tl)
                    blocks = []
                    for kt, dl in zip(kts, deltas):
                        sl = (s0 + kt) % RK
                        std_m2 = (special == 'X' and dl == -2)
                        blocks.append(dict(nk=128, K=[KR[0][sl], KR[1][sl]], V=VR[sl],
                                           kres=[[('KR', p, sl), ('KRz', p, sl)] for p in range(2)], vres=[('VR', sl)],
                                           E=('m2' if std_m2 else 'eb'), i0=7 - 2 * dl))
                    if special == 'X':
                        sl = (s0 + tl + 2) % RX
                        blocks.append(dict(nk=80, K=[KX[0][sl], KX[1][sl]], V=VX[sl],
                                           kres=[[('KX', p, sl), ('KXm', p, sl), ('KXz', p, sl)] for p in range(2)],
                                           vres=[('VX', sl), ('VXm', sl)], E='e5', i0=0))
                    else:
                        blocks.append(dict(nk=16, K=KM, V=VM0, kres=[[('KM', p), ('KMz', p)] for p in range(2)],
                                           vres=['VM0'], E='em', i0=0))
                    return blocks, tl, si

                def p2_qk(ui, b):
                    u = units[ui]
                    ci, j, G = u
                    qs = ci % 2
                    ps = ui % 2
                    blocks, tl, si = unit_blocks(u)
                    if G == 0 and b == 0:
                        ensure_tile(si, tl)
                        ensure_tile(si, tl + 1)
                    blk = blocks[b]
                    nk = blk['nk']
                    sbk = st2['sbank']; st2['sbank'] = (sbk + 2) % 4
                    bl = [sbk, sbk + 1]
                    for jj in range(4):
                        fc = 4 * G + jj
                        for par in range(2):
                            S.op('tensor', lambda e, blk=blk, nk=nk, par=par, fc=fc, jj=jj, bk=bl[par], qs=qs, j=j: e.matmul(
                                banks[bk][0:nk, jj * 128:(jj + 1) * 128],
                                blk['K'][par][:, fc, 0:nk],
                                QT[qs][:, fc, j * 128:(j + 1) * 128],
                                start=True, stop=True),
                                 reads=blk['kres'][par] + [('QT', qs)], writes=[PS(bl[par])])
                    S.op('scalar', lambda e, nk=nk, b=b, ps=ps, sbk=sbk: e.activation(
                        out=PT[ps][0:nk, b, :, :].rearrange('p a b -> p (a b)'),
                        in_=bank2[sbk // 2][0:nk, :], func=AF.Exp, scale=0.125),
                         reads=[PS(bl[0]), PS(bl[1])], writes=[('PT', ps, b, 0), ('PT', ps, b, 1)])
                    for par in range(2):
                        hsl = slice(par * 8 + 4 * G, par * 8 + 4 * G + 4)
                        if blk['E'] == 'eb':
                            i0 = blk['i0']
                            ein = lambda hsl=hsl, i0=i0: EB[:, hsl, i0:i0 + 2, :]
                            pin = lambda ps=ps, b=b, par=par: PT[ps][:, b, par, :].rearrange('p (a b c) -> p a b c', a=4, b=2)
                        elif blk['E'] == 'm2':
                            ein = lambda hsl=hsl: E_m2[:, hsl, :]
                            pin = lambda ps=ps, b=b, par=par: PT[ps][:, b, par, :].rearrange('p (a b) -> p a b', a=4)
                        elif blk['E'] == 'e5':
                            ein = lambda hsl=hsl: E5[0:80, hsl, :]
                            pin = lambda ps=ps, b=b, par=par: PT[ps][0:80, b, par, :].rearrange('p (a b) -> p a b', a=4)
                        else:
                            ein = lambda hsl=hsl: EM[0:16, hsl, :]
                            pin = lambda ps=ps, b=b, par=par: PT[ps][0:16, b, par, :].rearrange('p (a b) -> p a b', a=4)
                        S.op('vector', lambda e, ein=ein, pin=pin: e.tensor_tensor(out=pin(), in0=pin(), in1=ein(), op=ALU.mult),
                             reads=[('PT', ps, b, par)], writes=[('PT', ps, b, par)])

                def p2_pv(ui, par, jj):
                    u = units[ui]
                    ci, j, G = u
                    ps = ui % 2
                    blocks, tl, si = unit_blocks(u)
                    ab = 4 + par
                    fc = 4 * G + jj
                    h = 2 * fc + par
                    for b, blk in enumerate(blocks):
                        nk = blk['nk']
                        S.op('tensor', lambda e, blk=blk, nk=nk, b=b, par=par, jj=jj, h=h, ab=ab, ps=ps: e.matmul(
                            banks[ab][:, jj * 65:(jj + 1) * 65],
                            PT[ps][0:nk, b, par, jj * 128:(jj + 1) * 128],
                            blk['V'][0:nk, h * 65:(h + 1) * 65],
                            start=(b == 0), stop=(b == len(blocks) - 1)),
                             reads=blk['vres'] + [('PT', ps, b, par)], writes=[PS(ab)])

                def p2_norm(ui, par):
                    u = units[ui]
                    ci, j, G = u
                    si_, t0, nt = p2chunks[ci]
                    t = t0 + j
                    at = t % 2
                    ab = 4 + par
                    accv = lambda ab=ab: banks[ab][:, 0:260].rearrange('p (a b) -> p a b', a=4)
                    rs = slice(par * 4, par * 4 + 4)
                    S.op('vector', lambda e, accv=accv, rs=rs: e.reciprocal(rec[:, rs], accv()[:, :, 64]),
                         reads=[PS(ab)], writes=[('rec', par)])
                    outv = lambda at=at, G=G, par=par: ATT[at][:, :].rearrange('p (a b c) -> p a b c', a=8, b=2)[:, 4 * G:4 * G + 4, par, :]
                    S.op('vector', lambda e, accv=accv, outv=outv, rs=rs: e.tensor_tensor(
                        out=outv(), in0=accv()[:, :, 0:64],
                        in1=rec[:, rs].unsqueeze(2).to_broadcast([128, 4, 64]), op=ALU.mult),
                         reads=[PS(ab), ('rec', par)], writes=[('ATT', at)])
                    if G == 1 and par == 1:
                        S.op('gpsimd', lambda e, at=at, t=t: e.dma_start(out=at_d[t * 128:(t + 1) * 128, :], in_=ATT[at][:]),
                             reads=[('ATT', at)], writes=[('at_d', t)], dma=('sA', at))

                def p2_prev_slot(pu, b):
                    if pu < 0:
                        return
                    if b == 1:
                        p2_pv(pu, 0, 0); p2_pv(pu, 0, 1)
                    elif b == 2:
                        p2_pv(pu, 0, 2); p2_pv(pu, 0, 3); p2_norm(pu, 0)
                    elif b == 3:
                        p2_pv(pu, 1, 0); p2_pv(pu, 1, 1)
                    elif b == 4:
                        p2_pv(pu, 1, 2); p2_pv(pu, 1, 3); p2_norm(pu, 1)

                for fc in range(KC):
                    p2_qpiece(0, fc)
                if len(p2chunks) > 1:
                    p2_load(1)
                uic = {}
                for ui, u in enumerate(units):
                    ci, j, G = u
                    nt = p2chunks[ci][2]
                    k = uic.get(ci, 0); uic[ci] = k + 1
                    ppu = (KC + 2 * nt - 1) // (2 * nt)
                    QSPLIT = [[0, 1, 2], [3], [4, 5], [6], [7]]
                    for b in range(5):
                        p2_qk(ui, b)
                        if ci + 1 < len(p2chunks):
                            if ppu == 1:
                                if k < KC:
                                    p2_qmm(ci + 1, k, QSPLIT[b])
                            elif b == 0:
                                for fc in range(k * ppu, min(KC, (k + 1) * ppu)):
                                    p2_qpiece(ci + 1, fc)
                            if b == 0 and k == 2 * nt - 1 and ci + 2 < len(p2chunks):
                                p2_load(ci + 2)
                        if b > 0:
                            p2_prev_slot(ui - 1, b)
                for b in range(1, 5):
                    p2_prev_slot(len(units) - 1, b)
                S.flush()

        with ExitStack() as es:
            Wg = sb(es, 'Wg', [128, KC, 2048], BF16)
            Wo = sb(es, 'Wo', [128, 16, 1024], BF16)
            fgb = sb(es, 'fgb', [128, D], F32)
            S.op('sync', lambda e: e.dma_start(out=fgb[:], in_=fgb_d), writes=['fgb'], dma='c6b')

            HT3 = [sb(es, 'HT3_%d' % i, [128, KC, 512], BF16) for i in range(2)]
            UW = [sb(es, 'UW%d' % i, [128, 5, D], BF16) for i in range(2)]

            def windows(N):
                out = []
                o = 0
                while o < N:
                    out.append((o, min(112, N - o)))
                    o += 112
                return out
            AB = [sb(es, 'AB%d' % i, [128, 4, D], BF16) for i in range(2)]
            XR = [sb(es, 'XR%d' % i, [128, D], F32) for i in range(3)]
            PLT = sb(es, 'PLT', [128, KC, 512], BF16)
            MIX = sb(es, 'MIX', [128, 16, 512], BF16)
            TH = [sb(es, 'TH%d' % i, [128, 512], F32) for i in range(2)]
            SG = [sb(es, 'SG%d' % i, [128, 512], F32) for i in range(2)]
            Y = [sb(es, 'Y%d' % i, [128, D], F32) for i in range(2)]
            O = [sb(es, 'O%d' % i, [128, D], F32) for i in range(2)]
            junk3 = sb(es, 'junk3', [128, D], BF16)
            ssq3 = sb(es, 'ssq3', [128, 2], F32)
            sd3 = sb(es, 'sd3', [128, 2], F32)
            rs3 = sb(es, 'rs3', [128, 2], F32)

            st3 = dict(pbk=0, tcount=0, xrc=0, ycount=0)

            def nextbank():
                b = st3['pbk']; st3['pbk'] = (b + 1) % 8
                return b

            p3chunks = []
            for (s0, sn) in [(0, TA), (TA, TB)]:
                for (t0, nt) in seq_chunks(s0, sn):
                    p3chunks.append((s0, sn, t0, nt))

            def p3_load(ci):
                s0, sn, t0, nt = p3chunks[ci]
                N = nt * 128
                cs = ci % 2
                first = (t0 == s0)
                last = (t0 + nt == s0 + sn)
                S.op('sync', lambda e, cs=cs, t0=t0, nt=nt, N=N: e.dma_start(out=HT3[cs][:, :, 0:N], in_=hT_v(t0, nt)),
                     reads=[('hT_d', t0)], writes=[('HT3', cs)], dma=('HT3', cs))
                S0, S1 = s0 * 128, (s0 + sn) * 128
                for k, (o, n) in enumerate(windows(N)):
                    r0 = t0 * 128 + o - 8
                    if r0 < S0:
                        S.op('sync', lambda e, cs=cs, k=k: e.dma_start(out=UW[cs][0:8, k, :], in_=u_d[MT * 128 + 8:MT * 128 + 16, :]),
                             reads=[('u_d', MT)], writes=[('UWa', cs, k)], dma=('UW', cs))
                        S.op('sync', lambda e, cs=cs, k=k, S0=S0: e.dma_start(out=UW[cs][8:128, k, :], in_=u_d[S0:S0 + 120, :]),
                             reads=[('u_d', s0)], writes=[('UW', cs, k)], dma=('UW', cs))
                    else:
                        avail = 128
                        S.op('sync', lambda e, cs=cs, k=k, r0=r0, avail=avail: e.dma_start(
                            out=UW[cs][0:avail, k, :], in_=u_d[r0:r0 + avail, :]),
                             reads=[('u_d', tt) for tt in range(r0 // 128, (r0 + avail - 1) // 128 + 1)],
                             writes=[('UW', cs, k), ('UWa', cs, k)], dma=('UW', cs))
                S.op('sync', lambda e, cs=cs, t0=t0, nt=nt: e.dma_start(
                    out=AB[cs][:, 0:nt, :], in_=at_d[t0 * 128:(t0 + nt) * 128, :].rearrange('(a p) f -> p a f', p=128)),
                     reads=[('at_d', tt) for tt in range(t0, t0 + nt)], writes=[('AB', cs)], dma=('AB', cs))

            def p3_pooled(ci):
                s0, sn, t0, nt = p3chunks[ci]
                N = nt * 128
                cs = ci % 2
                last = (t0 + nt == s0 + sn)
                wins = windows(N)
                for cc in range(KC):
                    g = cc // 2
                    wdt = float(POOL_W[g])
                    pb = nextbank()
                    for k, (o, n) in enumerate(wins):
                        lastw = last and (k == len(wins) - 1)
                        bsrc = bandl[n] if lastw else bandw
                        S.op('tensor', lambda e, pb=pb, k=k, o=o, n=n, cs=cs, cc=cc, g=g, bsrc=bsrc: e.matmul(
                            banks[pb][:, o:o + n], UW[cs][:, k, cc * 128:(cc + 1) * 128], bsrc[:, g, 0:n],
                            start=True, stop=True), reads=[('UW', cs, k), ('UWa', cs, k), 'band'], writes=[PS(pb)])
                    o_l, n_l = wins[-1]
                    nint = o_l if last else N
                    if nint > 0:
                        S.op('vector', lambda e, pb=pb, cc=cc, nint=nint, wdt=wdt: e.tensor_scalar(
                            out=PLT[:, cc, 0:nint], in0=banks[pb][:, 0:nint], scalar1=1.0 / wdt, scalar2=None, op0=ALU.mult),
                             reads=[PS(pb)], writes=[('PLT', cc)])
                    if last:
                        S.op('vector', lambda e, pb=pb, cc=cc, g=g, o_l=o_l, n_l=n_l: e.tensor_tensor(
                            out=PLT[:, cc, o_l:o_l + n_l], in0=banks[pb][:, o_l:o_l + n_l],
                            in1=invl[n_l][:, g, :], op=ALU.mult), reads=[PS(pb), 'band'], writes=[('PLT', cc)])

            def p3_gates(ci):
                s0, sn, t0, nt = p3chunks[ci]
                N = nt * 128
                cs = ci % 2
                for dc in range(KC):
                    g = dc // 2
                    pg = nextbank()
                    for kc in range(KC):
                        S.op('tensor', lambda e, pg=pg, dc=dc, kc=kc, cs=cs, N=N: e.matmul(
                            banks[pg][:, 0:N], Wg[:, kc, dc * 128:(dc + 1) * 128], HT3[cs][:, kc, 0:N],
                            start=(kc == 0), stop=(kc == KC - 1)), reads=[('HT3', cs), ('Wg', dc // 2)], writes=[PS(pg)])
                    pm = nextbank()
                    for k2 in range(2):
                        S.op('tensor', lambda e, pm=pm, dc=dc, k2=k2, g=g, N=N: e.matmul(
                            banks[pm][:, 0:N], Wp[:, g * 2 + k2, (dc % 2) * 128:(dc % 2 + 1) * 128], PLT[:, g * 2 + k2, 0:N],
                            start=(k2 == 0), stop=(k2 == 1)), reads=[('PLT', g * 2 + k2), 'Wp'], writes=[PS(pm)])
                    ts = st3['tcount'] % 2; st3['tcount'] += 1
                    S.op('scalar', lambda e, ts=ts, pg=pg, N=N: e.activation(out=TH[ts][:, 0:N], in_=banks[pg][:, 0:N],
                                                                            func=AF.Tanh, scale=0.5),
                         reads=[PS(pg)], writes=[('TH', ts)])
                    S.op('vector', lambda e, ts=ts, pg=pg, N=N: e.scalar_tensor_tensor(
                        out=SG[ts][:, 0:N], in0=TH[ts][:, 0:N], scalar=1.0, in1=banks[pg][:, 0:N],
                        op0=ALU.add, op1=ALU.mult), reads=[('TH', ts), PS(pg)], writes=[('SG', ts)])
                    S.op('vector', lambda e, ts=ts, pm=pm, dc=dc, N=N: e.scalar_tensor_tensor(
                        out=MIX[:, dc, 0:N], in0=banks[pm][:, 0:N], scalar=psc[:, dc:dc + 1], in1=SG[ts][:, 0:N],
                        op0=ALU.mult, op1=ALU.mult), reads=[('SG', ts), PS(pm), 'psc'], writes=[('MIX', dc)])
                for fc in range(KC):
                    pg = nextbank()
                    for kc in range(KC):
                        S.op('tensor', lambda e, pg=pg, fc=fc, kc=kc, cs=cs, N=N: e.matmul(
                            banks[pg][:, 0:N], Wg[:, kc, 1024 + fc * 128:1024 + (fc + 1) * 128], HT3[cs][:, kc, 0:N],
                            start=(kc == 0), stop=(kc == KC - 1)), reads=[('HT3', cs), ('Wg', 4 + fc // 2)], writes=[PS(pg)])
                    pa = nextbank()
                    pab = banks[pa].bitcast(BF16)
                    for j in range(nt):
                        S.op('tensor', lambda e, pab=pab, j=j, fc=fc, cs=cs: e.transpose(
                            pab[:, j * 128:(j + 1) * 128], AB[cs][:, j, fc * 128:(fc + 1) * 128], ident[:]),
                             reads=[('AB', cs), 'ident'], writes=[PS(pa)])
                    ts = st3['tcount'] % 2; st3['tcount'] += 1
                    S.op('scalar', lambda e, ts=ts, pg=pg, N=N: e.activation(out=TH[ts][:, 0:N], in_=banks[pg][:, 0:N],
                                                                            func=AF.Tanh, scale=0.5),
                         reads=[PS(pg)], writes=[('TH', ts)])
                    S.op('vector', lambda e, ts=ts, pg=pg, N=N: e.scalar_tensor_tensor(
                        out=SG[ts][:, 0:N], in0=TH[ts][:, 0:N], scalar=1.0, in1=banks[pg][:, 0:N],
                        op0=ALU.add, op1=ALU.mult), reads=[('TH', ts), PS(pg)], writes=[('SG', ts)])
                    S.op('vector', lambda e, ts=ts, pab=pab, fc=fc, N=N: e.tensor_tensor(
                        out=MIX[:, 8 + fc, 0:N], in0=pab[:, 0:N], in1=SG[ts][:, 0:N], op=ALU.mult),
                         reads=[('SG', ts), PS(pa)], writes=[('MIX', 8 + fc)])
            def p3_out(ci):
                s0, sn, t0, nt = p3chunks[ci]
                for j in range(nt):
                    t = t0 + j
                    xr = st3['xrc'] % 3; st3['xrc'] += 1
                    S.op('sync', lambda e, xr=xr, t=t: e.dma_start(out=XR[xr][:], in_=xall[t * 128:(t + 1) * 128, :]),
                         writes=[('XR', xr)], dma=('XR', xr))
                    ys = st3['ycount'] % 2; st3['ycount'] += 1
                    for cg in range(2):
                        po = nextbank()
                        for kc in range(16):
                            S.op('tensor', lambda e, po=po, kc=kc, cg=cg, j=j: e.matmul(
                                banks[po][:, :], MIX[:, kc, j * 128:(j + 1) * 128], Wo[:, kc, cg * 512:(cg + 1) * 512],
                                start=(kc == 0), stop=(kc == 15)), reads=[('MIX', kc), ('Wo', kc // 4)], writes=[PS(po)])
                        S.op('vector', lambda e, po=po, cg=cg, ys=ys, xr=xr: e.tensor_tensor(
                            out=Y[ys][:, cg * 512:(cg + 1) * 512], in0=banks[po][:, :], in1=XR[xr][:, cg * 512:(cg + 1) * 512],
                            op=ALU.add), reads=[PS(po), ('XR', xr)], writes=[('Y', ys)])
                    S.op('scalar', lambda e, ys=ys: e.activation(out=junk3[:], in_=Y[ys][:], func=AF.Square,
                                                                accum_out=ssq3[:, ys:ys + 1]),
                         reads=[('Y', ys)], writes=['junk3', ('ssq3', ys)])
                    S.op('scalar', lambda e, ys=ys: e.activation(out=sd3[:, ys:ys + 1], in_=ssq3[:, ys:ys + 1], func=AF.Sqrt,
                                                                bias=epsb[:], scale=1.0 / D),
                         reads=[('ssq3', ys), 'epsb'], writes=[('sd3', ys)])
                    S.op('vector', lambda e, ys=ys: e.reciprocal(rs3[:, ys:ys + 1], sd3[:, ys:ys + 1]),
                         reads=[('sd3', ys)], writes=[('rs3', ys)])
                    S.op('vector', lambda e, ys=ys: e.scalar_tensor_tensor(
                        out=O[ys][:], in0=Y[ys][:], scalar=rs3[:, ys:ys + 1], in1=fgb[:], op0=ALU.mult, op1=ALU.mult),
                         reads=[('Y', ys), ('rs3', ys), 'fgb'], writes=[('O', ys)])
                    S.op('gpsimd', lambda e, ys=ys, t=t: e.dma_start(out=y_d[t * 128:(t + 1) * 128, :], in_=O[ys][:]),
                         reads=[('O', ys)], writes=[('y_d', t)], dma=('sO', ys))

            p3_load(0)
            for cb in range(8):
                src0 = 1024 if cb < 4 else 5120
                S.op('gpsimd', lambda e, cb=cb, src0=src0: e.dma_start(
                    out=Wg[:, :, cb * 256:(cb + 1) * 256],
                    in_=w_in_v[:, :, src0 + (cb % 4) * 256:src0 + (cb % 4 + 1) * 256]),
                     reads=([('HT3', 0), ('AB', 0)] + [('UW', 0, k) for k in range(5)] if cb == 0 else []), writes=[('Wg', cb)], dma=('wg', cb))
            for cb in range(4):
                S.op('gpsimd', lambda e, cb=cb: e.dma_start(out=Wo[:, cb * 4:(cb + 1) * 4, :], in_=w_out_v[:, cb * 4:(cb + 1) * 4, :]),
                     writes=[('Wo', cb)], dma=('wo', cb))
            p3_pooled(0)
            for ci in range(len(p3chunks)):
                if ci + 1 < len(p3chunks):
                    p3_load(ci + 1)
                p3_gates(ci)
                if ci + 1 < len(p3chunks):
                    p3_pooled(ci + 1)
                p3_out(ci)
            S.flush()
    return nc


def _constants():
    ident = np.eye(128, dtype=np.float32)
    consts = dict(ident=ident)
    tp = np.arange(128)[:, None] - 8
    bandw = np.zeros((128, 4, 112), np.float32)
    for g, w in enumerate(POOL_W):
        t = np.arange(112)[None, :]
        inwin = (tp >= t - w // 2) & (tp <= t + w // 2 - 1)
        bandw[:, g, :] = inwin.astype(np.float32) - w * (tp == t)
    consts['bandw'] = bandw
    for n in (64, 32):
        bl = np.zeros((128, 4, n), np.float32)
        il = np.zeros((128, 4, n), np.float32)
        for g, w in enumerate(POOL_W):
            t = np.arange(n)[None, :]
            inwin = (tp >= t - w // 2) & (tp <= t + w // 2 - 1) & (tp <= n - 1)
            cnt = np.minimum(t + w // 2 - 1, n - 1) - (t - w // 2) + 1
            bl[:, g, :] = inwin.astype(np.float32) - cnt.astype(np.float32) * (tp == t)
            il[:, g, :] = 1.0 / cnt.astype(np.float32)
        consts['bandl%d' % n] = bl
        consts['invl%d' % n] = il
    cols = np.arange(64)
    cstart = np.clip(cols - 8, 0, 48)
    kc = np.arange(64)
    consts['mask01'] = ((kc[:, None] >= cstart[None, :]) & (kc[:, None] < cstart[None, :] + 16)).astype(np.float32)
    return consts


_CACHE = {}


def kernel(x_prompt, x_sample, meta_tokens, norm_g, w_in, w_pool, pool_scale, rpb, meta_bias, w_out, final_g):
    f = lambda a: np.ascontiguousarray(np.asarray(a, dtype=np.float32))
    x_prompt, x_sample, meta_tokens = f(x_prompt), f(x_sample), f(meta_tokens)
    norm_g, w_in, w_pool, pool_scale = f(norm_g)[0], f(w_in)[0], f(w_pool)[0], f(pool_scale)[0]
    rpb, meta_bias, w_out, final_g = f(rpb)[0], f(meta_bias)[0], f(w_out)[0], f(final_g)
    if 'nc' not in _CACHE:
        _CACHE['nc'] = build_program()
        _CACHE['const'] = _constants()
    nc = _CACHE['nc']
    C = _CACHE['const']
    hperm = np.array([2 * (hp % 8) + hp // 8 for hp in range(NH)])
    kc = np.arange(64)[:, None]
    qc = np.arange(64)[None, :]
    cidx = np.clip(kc - qc + 15, 0, 30)
    ridx = np.clip(14 - (np.arange(16) - 1), 0, 14)
    b0 = rpb[hperm][:, ridx][:, :, cidx]
    b0 = np.ascontiguousarray(b0.transpose(2, 0, 1, 3))
    mbT = np.ascontiguousarray(meta_bias[hperm].T)
    gT = np.ascontiguousarray(norm_g.reshape(KC, 128).T)
    psT = np.ascontiguousarray(pool_scale.reshape(KC, 128).T)
    fgb = np.ascontiguousarray(np.broadcast_to(final_g[None, :], (128, D)))
    metax = np.zeros((128, D), np.float32)
    metax[:16] = meta_tokens
    in_maps = []
    for i in range(NCORES):
        half = i % 2
        tok0 = 0 if half == 0 else 30 * 128
        xb = x_sample[i // 2][tok0:tok0 + TB * 128]
        xall = np.concatenate([x_prompt[i], xb, metax], axis=0)
        m = dict(xall=xall, w_in=w_in, w_pool=w_pool, w_out=w_out, gT=gT, psT=psT, fgb=fgb, mbT=mbT, b0=b0)
        m.update(C)
        in_maps.append(m)
    res = run_bass_kernel_spmd(nc, in_maps, core_ids=list(range(NCORES)))
    y_prompt = np.empty((8, 2048, D), np.float32)
    y_sample = np.empty((4, 8192, D), np.float32)
    for i in range(NCORES):
        y = res.results[i]['y']
        y_prompt[i] = y[:TA * 128]
        yb = y[TA * 128:]
        if i % 2 == 0:
            y_sample[i // 2][:4096] = yb[:4096]
        else:
            y_sample[i // 2][4096:] = yb[2 * 128:]
    return (y_prompt, y_sample)
```

```python
import re
import numpy as np
from contextlib import ExitStack
import concourse.bass as bass
import concourse.mybir as mybir
from concourse.bass_utils import run_bass_kernel_spmd

F32 = mybir.dt.float32
BF16 = mybir.dt.bfloat16
AF = mybir.ActivationFunctionType
ALU = mybir.AluOpType

NCORES = 8
D = 1024
KC = 8
NH = 16
HD = 64
VW = NH * 65
TA = 16
TB = 34
NT = TA + TB + 1
MT = TA + TB
EPS = 1e-6
POOL_W = (2, 4, 8, 16)
ENGS = ['sync', 'gpsimd', 'scalar', 'vector', 'tensor']


class Op:
    __slots__ = ('eng', 'fn', 'deps', 'needed', 'val', 'is_dma', 'sem', 'idx', 'ent')


class Sched:
    def __init__(self, nc, es):
        self.nc = nc
        self.esem = {e: es.enter_context(nc.semaphore('s_' + e)) for e in ENGS}
        self.ecnt = {e: 0 for e in ENGS}
        self.dsem = {}
        self.es = es
        self.reset()

    def reset(self):
        self.q = {e: [] for e in ENGS}
        self.lastw = {}
        self.readers = {}
        self.n = 0

    def _tok(self, o):
        return o.sem if o.is_dma else o.eng

    def op(self, eng, fn, reads=(), writes=(), dma=None):
        o = Op()
        o.eng = eng; o.fn = fn; o.needed = False; o.val = None; o.idx = self.n
        self.n += 1
        o.is_dma = dma is not None
        o.sem = None
        if o.is_dma:
            if dma not in self.dsem:
                self.dsem[dma] = [self.es.enter_context(self.nc.semaphore('d_' + re.sub('[^0-9a-zA-Z]+', '_', str(dma)))), 0]
            ent = self.dsem[dma]
            o.sem = ent[0]; o.ent = ent
        deps = {}

        def add(d):
            if d is None:
                return
            if (not d.is_dma) and (not o.is_dma) and d.eng == 'tensor' and eng == 'tensor':
                return
            k = id(d.sem) if d.is_dma else d.eng
            cur = deps.get(k)
            if cur is None or cur.idx < d.idx:
                deps[k] = d
        for r in reads:
            add(self.lastw.get(r))
        for r in writes:
            add(self.lastw.get(r))
            for d in self.readers.get(r, {}).values():
                add(d)
        o.deps = [(d, (d.ent[1] if d.is_dma else None)) for d in deps.values()]
        if o.is_dma:
            o.ent[1] += 16
            o.val = o.ent[1]
        for d, _ in o.deps:
            d.needed = True
        for r in reads:
            self.readers.setdefault(r, {})[(id(o.sem) if o.is_dma else eng)] = o
        for r in writes:
            self.lastw[r] = o
            self.readers[r] = {}
        self.q[eng].append(o)
        return o

    def flush(self):
        nc = self.nc
        for e in ENGS:
            for o in self.q[e]:
                if (not o.is_dma) and o.needed:
                    self.ecnt[e] += 1
                    o.val = self.ecnt[e]
        final_dma = [(ent[0], ent[1]) for ent in self.dsem.values() if ent[1] > 0]
        with nc.Block(no_gpsimd_drain=True) as block:
            for e in ENGS:
                ops = self.q[e]

                def body(eng, ops=ops, e=e):
                    waited = {}
                    for o in ops:
                        for d, dv in o.deps:
                            sem = d.sem if d.is_dma else self.esem[d.eng]
                            val = dv if d.is_dma else d.val
                            if waited.get(id(sem), 0) >= val:
                                continue
                            eng.wait_ge(sem, val)
                            waited[id(sem)] = val
                        ins = o.fn(eng)
                        if o.is_dma:
                            ins.then_inc(o.sem, 16)
                        elif o.needed:
                            ins.then_inc(self.esem[e], 1)
                    if e == 'sync':
                        for sem, val in final_dma:
                            if waited.get(id(sem), 0) < val:
                                eng.wait_ge(sem, val)
                getattr(block, e)(body)
        self.reset()


def seq_chunks(t0, ntiles):
    out = []
    t = 0
    while t < ntiles:
        n = min(4, ntiles - t)
        out.append((t0 + t, n))
        t += n
    return out


def build_program():
    nc = bass.Bass('TRN2', target_bir_lowering=False)
    dt = lambda name, shape, dtype, kind: nc.dram_tensor(name, shape, dtype, kind=kind).ap()
    xall = dt('xall', [NT * 128, D], F32, 'ExternalInput')
    w_in = dt('w_in', [D, 6144], F32, 'ExternalInput')
    w_pool = dt('w_pool', [4, 256, 256], F32, 'ExternalInput')
    w_out = dt('w_out', [2048, D], F32, 'ExternalInput')
    gT_d = dt('gT', [128, KC], F32, 'ExternalInput')
    psT_d = dt('psT', [128, KC], F32, 'ExternalInput')
    fgb_d = dt('fgb', [128, D], F32, 'ExternalInput')
    mbT_d = dt('mbT', [16, NH], F32, 'ExternalInput')
    b0_d = dt('b0', [64, NH, 16, 64], F32, 'ExternalInput')
    mask_d = dt('mask01', [64, 64], F32, 'ExternalInput')
    ident_d = dt('ident', [128, 128], F32, 'ExternalInput')
    bandw_d = dt('bandw', [128, 4, 112], F32, 'ExternalInput')
    bandl64_d = dt('bandl64', [128, 4, 64], F32, 'ExternalInput')
    bandl32_d = dt('bandl32', [128, 4, 32], F32, 'ExternalInput')
    invl64_d = dt('invl64', [128, 4, 64], F32, 'ExternalInput')
    invl32_d = dt('invl32', [128, 4, 32], F32, 'ExternalInput')
    y_d = dt('y', [(TA + TB) * 128, D], F32, 'ExternalOutput')
    hT_d = dt('hT_s', [NT * 128 * D], BF16, 'Internal')
    kT_d = dt('kT_s', [NT * 128, D], BF16, 'Internal')
    v_d = dt('v_s', [NT * 128, VW], BF16, 'Internal')
    u_d = dt('u_s', [NT * 128, D], BF16, 'Internal')
    at_d = dt('at_s', [NT * 128, D], BF16, 'Internal')
    w_in_v = w_in.rearrange('(kc p) n -> p kc n', p=128)
    w_out_v = w_out.rearrange('(kc p) n -> p kc n', p=128)
    w_pool_v = w_pool.rearrange('g (k p) d -> p (g k) d', p=128)
    kT_v = lambda t: kT_d[t * 128:(t + 1) * 128, :].rearrange('p (a b) -> p a b', a=KC)
    hT_v = lambda t0, nt: hT_d[t0 * 128 * D:(t0 + nt) * 128 * D].rearrange('(p k n) -> p k n', p=128, k=KC)

    with ExitStack() as top:
        S = Sched(nc, top)
        bank2 = [top.enter_context(nc.psum_tensor('bankpair%d' % b, [128, 1024], F32)) for b in range(4)]
        banks = []
        for b in range(4):
            banks.append(bank2[b][:, 0:512])
            banks.append(bank2[b][:, 512:1024])
        PS = lambda b: ('ps', b)
        sb = lambda es, name, shape, dtype: es.enter_context(nc.sbuf_tensor('sb_' + name, shape, dtype))

        ident = sb(top, 'ident', [128, 128], BF16)
        gT = sb(top, 'gTs', [128, KC], F32)
        epsb = sb(top, 'epsb', [128, 1], F32)
        Wp = sb(top, 'Wp', [128, 8, 256], BF16)
        bandw = sb(top, 'bandw', [128, 4, 112], BF16)
        bandl = {64: sb(top, 'bandl64', [128, 4, 64], BF16), 32: sb(top, 'bandl32', [128, 4, 32], BF16)}
        invl = {64: sb(top, 'invl64', [128, 4, 64], F32), 32: sb(top, 'invl32', [128, 4, 32], F32)}
        psc = sb(top, 'psc', [128, KC], F32)

        def evac(eng, dst, src, reads, writes):
            if eng == 'vector':
                S.op('vector', lambda e: e.tensor_copy(dst(), src()), reads=reads, writes=writes)
            else:
                S.op('scalar', lambda e: e.copy(dst(), src()), reads=reads, writes=writes)

        with ExitStack() as es12:
            Wq = sb(es12, 'Wq', [128, KC, 1024], BF16)
            EB = sb(es12, 'EB', [128, NH, 15, 64], BF16)
            E_m2 = sb(es12, 'E_m2', [128, NH, 128], BF16)
            E5 = sb(es12, 'E5', [80, NH, 128], BF16)
            EM = sb(es12, 'EM', [16, NH, 128], BF16)

            with ExitStack() as es:
                ctmp = sb(es, 'ctmp', [128, 128], F32)
                S.op('sync', lambda e: e.dma_start(out=ctmp[:], in_=ident_d), writes=['ctmp'], dma='c0')
                S.op('vector', lambda e: e.tensor_copy(ident[:], ctmp[:]), reads=['ctmp'], writes=['ident'])
                S.op('sync', lambda e: e.dma_start(out=gT[:], in_=gT_d), writes=['gT'], dma='c1')
                S.op('vector', lambda e: e.memset(epsb[:], EPS), writes=['epsb'])

                Wk = sb(es, 'Wk', [128, KC, 1024], BF16)
                Wvu = sb(es, 'Wvu', [128, KC, 2048], BF16)
                def p1_weights(xdeps):
                    for cb in range(4):
                        S.op('gpsimd', lambda e, cb=cb: e.dma_start(out=Wk[:, :, cb * 256:(cb + 1) * 256],
                                                                   in_=w_in_v[:, :, 3072 + cb * 256:3072 + (cb + 1) * 256]),
                             reads=(xdeps if cb == 0 else []), writes=[('Wk', cb)], dma=('wk', cb))
                    for cb in range(4):
                        src0 = 4096 if cb < 2 else 0
                        S.op('gpsimd', lambda e, cb=cb, src0=src0: e.dma_start(
                            out=Wvu[:, :, cb * 512:(cb + 1) * 512],
                            in_=w_in_v[:, :, src0 + (cb % 2) * 512:src0 + (cb % 2 + 1) * 512]),
                             writes=[('Wvu', cb)], dma=('wvu', cb))
                    S.op('gpsimd', lambda e: e.dma_start(out=bandw[:], in_=bandw_d), writes=['band'], dma='c7')
                    S.op('gpsimd', lambda e: e.dma_start(out=bandl[64][:], in_=bandl64_d), writes=['band'], dma='c7')
                    S.op('gpsimd', lambda e: e.dma_start(out=bandl[32][:], in_=bandl32_d), writes=['band'], dma='c7')
                    S.op('gpsimd', lambda e: e.dma_start(out=Wp[:], in_=w_pool_v), writes=['Wp'], dma='wp')

                S.op('sync', lambda e: e.dma_start(out=psc[:], in_=psT_d), writes=['psc'], dma='c6')
                S.op('vector', lambda e: e.tensor_scalar(out=psc[:], in0=psc[:], scalar1=0.5, scalar2=None, op0=ALU.mult),
                     reads=['psc'], writes=['psc'])
                S.op('sync', lambda e: e.dma_start(out=invl[64][:], in_=invl64_d), writes=['band'], dma='c8')
                S.op('sync', lambda e: e.dma_start(out=invl[32][:], in_=invl32_d), writes=['band'], dma='c8')

                NX = 4
                X = [sb(es, 'X%d' % i, [128, D], F32) for i in range(NX)]
                XN = [sb(es, 'XN%d' % i, [128, D], BF16) for i in range(2)]
                junk = sb(es, 'junk', [128, D], BF16)
                ssq = sb(es, 'ssq', [128, 8], F32)
                sd = sb(es, 'sd', [128, 8], F32)
                rstd = sb(es, 'rstd', [128, 8], F32)
                HT = [sb(es, 'HT%d' % i, [128, KC, 512], BF16) for i in range(2)]
                KTs = [sb(es, 'KTs%d' % i, [128, KC, 512], BF16) for i in range(2)]
                Vs = [sb(es, 'Vs%d' % i, [128, NH, 65], BF16) for i in range(4)]
                Us = [sb(es, 'Us%d' % i, [128, D], BF16) for i in range(4)]
                for i in range(4):
                    S.op('vector', lambda e, i=i: e.memset(Vs[i][:, :, 64:65], 2.0), writes=[('Vs', i)])

                chunks = seq_chunks(0, TA) + seq_chunks(TA, TB) + [(MT, 1)]
                st = dict(xcnt=0, pbank=4, vcnt=0, ev=0, scnt=0)

                def nb1():
                    b = st['pbank']; st['pbank'] = 4 + (b - 3) % 4
                    return b

                def ev1():
                    st['ev'] += 1
                    return 'vector' if st['ev'] % 2 == 0 else 'scalar'

                tile_sc = {}

                tile_xs = {}

                def p1_load(ci, j):
                    t0, nt = chunks[ci]
                    t = t0 + j
                    xs = st['xcnt'] % NX; st['xcnt'] += 1
                    tile_xs[(ci, j)] = xs
                    S.op('sync', lambda e, xs=xs, t=t: e.dma_start(out=X[xs][:], in_=xall[t * 128:(t + 1) * 128, :]),
                         writes=[('X', xs)], dma=('X', xs))

                def p1_front(ci, j):
                    if (ci, j) not in tile_xs:
                        p1_load(ci, j)
                    xs = tile_xs[(ci, j)]
                    sc = st['scnt'] % 8; st['scnt'] += 1
                    tile_sc[(ci, j)] = sc
                    S.op('scalar', lambda e, xs=xs, sc=sc: e.activation(out=junk[:], in_=X[xs][:], func=AF.Square,
                                                                       accum_out=ssq[:, sc:sc + 1]),
                         reads=[('X', xs)], writes=['junk', ('ssq', sc)])
                    S.op('scalar', lambda e, sc=sc: e.activation(out=sd[:, sc:sc + 1], in_=ssq[:, sc:sc + 1], func=AF.Sqrt,
                                                                bias=epsb[:], scale=1.0 / D),
                         reads=[('ssq', sc), 'epsb'], writes=[('sd', sc)])
                    S.op('vector', lambda e, sc=sc: e.reciprocal(rstd[:, sc:sc + 1], sd[:, sc:sc + 1]),
                         reads=[('sd', sc)], writes=[('rstd', sc)])
                    xn = sc % 2
                    S.op('vector', lambda e, xs=xs, xn=xn, sc=sc: e.tensor_scalar(
                        out=XN[xn][:], in0=X[xs][:], scalar1=rstd[:, sc:sc + 1], scalar2=None, op0=ALU.mult),
                         reads=[('X', xs), ('rstd', sc)], writes=[('XN', xn)])

                def p1_T(ci, j):
                    sc = tile_sc[(ci, j)]
                    xn = sc % 2
                    tb = sc % 2
                    tpb = bank2[tb]
                    for kc in range(KC):
                        S.op('tensor', lambda e, tpb=tpb, xn=xn, kc=kc: e.matmul(
                            tpb[:, kc * 128:(kc + 1) * 128], XN[xn][:, kc * 128:(kc + 1) * 128], ident[:],
                            start=True, stop=True),
                             reads=[('XN', xn), 'ident'], writes=[PS(2 * tb), PS(2 * tb + 1)])

                def p1_HTevac(ci, j):
                    sc = tile_sc[(ci, j)]
                    hs = ci % 2
                    tb = sc % 2
                    tpb = bank2[tb]
                    S.op('vector', lambda e, tpb=tpb, hs=hs, j=j: e.tensor_tensor(
                        out=HT[hs][:, :, j * 128:(j + 1) * 128],
                        in0=tpb[:, 0:1024].rearrange('p (a b) -> p a b', a=KC),
                        in1=gT[:, :].unsqueeze(2).to_broadcast([128, KC, 128]), op=ALU.mult),
                         reads=[PS(2 * tb), PS(2 * tb + 1), 'gT'], writes=[('HT', hs)])

                def p1_K(ci):
                    t0, nt = chunks[ci]
                    N = nt * 128
                    hs = ci % 2
                    for fc in range(KC):
                        pb = nb1()
                        for kc in range(KC):
                            S.op('tensor', lambda e, pb=pb, fc=fc, kc=kc, hs=hs, N=N: e.matmul(
                                banks[pb][:, 0:N], Wk[:, kc, fc * 128:(fc + 1) * 128], HT[hs][:, kc, 0:N],
                                start=(kc == 0), stop=(kc == KC - 1)),
                                 reads=[('HT', hs), ('Wk', fc // 2)], writes=[PS(pb)])
                        evac(ev1(), lambda fc=fc, hs=hs, N=N: KTs[hs][:, fc, 0:N], lambda pb=pb, N=N: banks[pb][:, 0:N],
                             [PS(pb)], [('KTs', hs)])
                    for j in range(nt):
                        t = t0 + j
                        S.op('gpsimd', lambda e, hs=hs, t=t, j=j: e.dma_start(out=kT_v(t), in_=KTs[hs][:, :, j * 128:(j + 1) * 128]),
                             reads=[('KTs', hs)], writes=[('kT_d', t)], dma=('sK', hs))

                def p1_VU(ci, j):
                    t0, nt = chunks[ci]
                    N = nt * 128
                    hs = ci % 2
                    t = t0 + j
                    vs = st['vcnt'] % 4; st['vcnt'] += 1
                    for cg in range(4):
                        pb = nb1()
                        for kc in range(KC):
                            S.op('tensor', lambda e, pb=pb, cg=cg, kc=kc, hs=hs, j=j: e.matmul(
                                banks[pb][:, :], HT[hs][:, kc, j * 128:(j + 1) * 128], Wvu[:, kc, cg * 512:(cg + 1) * 512],
                                start=(kc == 0), stop=(kc == KC - 1)),
                                 reads=[('HT', hs), ('Wvu', cg)], writes=[PS(pb)])
                        if cg < 2:
                            evac(ev1(), lambda vs=vs, cg=cg: Vs[vs][:, cg * 8:(cg + 1) * 8, 0:64],
                                 lambda pb=pb: banks[pb][:, :].rearrange('p (a b) -> p a b', a=8), [PS(pb)], [('Vs', vs)])
                        else:
                            evac(ev1(), lambda vs=vs, cg=cg: Us[vs][:, (cg - 2) * 512:(cg - 1) * 512],
                                 lambda pb=pb: banks[pb][:, :], [PS(pb)], [('Us', vs)])
                    S.op('gpsimd', lambda e, vs=vs, t=t: e.dma_start(out=v_d[t * 128:(t + 1) * 128, :],
                                                                   in_=Vs[vs][:].rearrange('p a b -> p (a b)')),
                         reads=[('Vs', vs)], writes=[('v_d', t)], dma=('sV', vs))
                    S.op('gpsimd', lambda e, vs=vs, t=t: e.dma_start(out=u_d[t * 128:(t + 1) * 128, :], in_=Us[vs][:]),
                         reads=[('Us', vs)], writes=[('u_d', t)], dma=('sU', vs))
                    if j == nt - 1:
                        S.op('gpsimd', lambda e, hs=hs, t0=t0, nt=nt, N=N: e.dma_start(out=hT_v(t0, nt), in_=HT[hs][:, :, 0:N]),
                             reads=[('HT', hs)], writes=[('hT_d', t0)], dma=('sH', hs))

                msk = sb(es, 'msk', [128, 64], F32)
                mbs = sb(es, 'mbs', [80, NH], F32)
                mbe = sb(es, 'mbe', [80, NH], F32)
                EBf = sb(es, 'EBf', [128, 4, 15, 64], F32)

                def p1_tables(hg):
                    if hg == 0:
                        S.op('sync', lambda e: e.dma_start(out=msk[0:64, :], in_=mask_d), writes=['msk'], dma='c2')
                        S.op('sync', lambda e: e.dma_start(out=msk[64:128, :], in_=mask_d), writes=['msk'], dma='c2')
                        S.op('sync', lambda e: e.dma_start(out=mbs[0:16, :], in_=mbT_d), writes=['mbs'], dma='c3')
                        S.op('sync', lambda e: e.dma_start(out=mbs[64:80, :], in_=mbT_d), writes=['mbs'], dma='c3')
                    S.op('sync', lambda e, hg=hg: e.dma_start(out=EBf[0:64], in_=b0_d[:, hg * 4:(hg + 1) * 4, 1:16, :]),
                         writes=['EBf'], dma='c4')
                    S.op('sync', lambda e, hg=hg: e.dma_start(out=EBf[64:128], in_=b0_d[:, hg * 4:(hg + 1) * 4, 0:15, :]),
                         writes=['EBf'], dma='c4')
                    S.op('scalar', lambda e: e.activation(out=EBf[:], in_=EBf[:], func=AF.Exp), reads=['EBf'], writes=['EBf'])
                    S.op('vector', lambda e, hg=hg: e.tensor_tensor(
                        out=EB[:, hg * 4:(hg + 1) * 4, :, :].rearrange('p a b c -> p (a b) c'),
                        in0=EBf[:].rearrange('p a b c -> p (a b) c'),
                        in1=msk[:, :].unsqueeze(1).to_broadcast([128, 60, 64]), op=ALU.mult),
                         reads=['EBf', 'msk'], writes=['EB'])
                    if hg == 3:
                        S.op('scalar', lambda e: e.activation(out=mbe[0:16, :], in_=mbs[0:16, :], func=AF.Exp), reads=['mbs'], writes=['mbe'])
                        S.op('scalar', lambda e: e.activation(out=mbe[64:80, :], in_=mbs[64:80, :], func=AF.Exp), reads=['mbs'], writes=['mbe'])
                        S.op('vector', lambda e: e.tensor_copy(E_m2[:].rearrange('p h (a b) -> p h a b', a=2), EB[:, :, 11:13, :]),
                             reads=['EB'], writes=['E_m2'])
                        S.op('vector', lambda e: e.memset(E_m2[0:64, :, 64:128], 0.0), writes=['E_m2'])
                        S.op('vector', lambda e: e.memset(E5[0:64, :, 0:64], 0.0), writes=['E5'])
                        S.op('vector', lambda e: e.tensor_copy(E5[0:64, :, 64:128], EB[0:64, :, 4, :]), reads=['EB'], writes=['E5'])
                        S.op('vector', lambda e: e.tensor_copy(E5[64:80, :, :], mbe[64:80, :].unsqueeze(2).to_broadcast([16, NH, 128])),
                             reads=['mbe'], writes=['E5'])
                        S.op('vector', lambda e: e.tensor_copy(EM[0:16, :, :], mbe[0:16, :].unsqueeze(2).to_broadcast([16, NH, 128])),
                             reads=['mbe'], writes=['EM'])
                        for cb in range(4):
                            S.op('gpsimd', lambda e, cb=cb: e.dma_start(out=Wq[:, :, cb * 256:(cb + 1) * 256],
                                                                       in_=w_in_v[:, :, 2048 + cb * 256:2048 + (cb + 1) * 256]),
                                 writes=['Wq'], dma='wq')

                for j in range(chunks[0][1]):
                    p1_load(0, j)
                p1_weights([('X', tile_xs[(0, j)]) for j in range(chunks[0][1])])
                for j in range(chunks[0][1]):
                    p1_front(0, j)
                    p1_T(0, j)
                    p1_HTevac(0, j)
                for ci in range(len(chunks)):
                    ntc = chunks[ci][1]
                    ntn = chunks[ci + 1][1] if ci + 1 < len(chunks) else 0
                    if ntn > 0:
                        p1_front(ci + 1, 0)
                    p1_K(ci)
                    for j in range(max(ntc, ntn)):
                        if j < ntn:
                            p1_T(ci + 1, j)
                        if j + 1 < ntn:
                            p1_front(ci + 1, j + 1)
                        if j < ntn:
                            p1_HTevac(ci + 1, j)
                        if j < ntc:
                            p1_VU(ci, j)
                    if 1 <= ci <= 4:
                        p1_tables(ci - 1)
                S.flush()

            with ExitStack() as es:
                KM = sb(es, 'KM', [128, KC, 16], BF16)
                VM0 = sb(es, 'VM0', [16, VW], BF16)
                HT2 = [sb(es, 'HT2_%d' % i, [128, KC, 512], BF16) for i in range(2)]
                _t0, _nt = seq_chunks(0, TA)[0]
                S.op('sync', lambda e: e.dma_start(out=HT2[0][:, :, 0:_nt * 128], in_=hT_v(_t0, _nt)),
                     reads=[('hT_d', _t0)], writes=[('HT2', 0)], dma=('HT2', 0))
                S.op('sync', lambda e: e.dma_start(out=KM[:], in_=kT_v(MT)[:, :, 0:16]), writes=['KM'], dma='c5')
                S.op('sync', lambda e: e.dma_start(out=VM0[:], in_=v_d[MT * 128:MT * 128 + 16, :]), writes=['VM0'], dma='c5')
                RK = 8
                RX = 4
                KR = [sb(es, 'KR%d' % i, [128, KC, 128], BF16) for i in range(RK)]
                VR = [sb(es, 'VR%d' % i, [128, VW], BF16) for i in range(RK)]
                KX = [sb(es, 'KX%d' % i, [128, KC, 80], BF16) for i in range(RX)]
                VX = [sb(es, 'VX%d' % i, [80, VW], BF16) for i in range(RX)]
                for i in range(RX):
                    S.op('sync', lambda e, i=i: e.dma_start(out=KX[i][:, :, 64:80], in_=kT_v(MT)[:, :, 0:16]),
                         writes=[('KXm', i)], dma='c5')
                    S.op('sync', lambda e, i=i: e.dma_start(out=VX[i][64:80, :], in_=v_d[MT * 128:MT * 128 + 16, :]),
                         writes=[('VXm', i)], dma='c5')
                QT = [sb(es, 'QT%d' % i, [128, KC, 512], BF16) for i in range(2)]
                PT = [sb(es, 'PT%d' % i, [128, 5, 2, 512], BF16) for i in range(2)]
                ATT = [sb(es, 'ATT%d' % i, [128, D], BF16) for i in range(2)]
                rec = sb(es, 'rec', [128, 8], F32)

                st2 = dict(qb=6, sbank=0, ev=0)
                seqs = [(0, TA), (TA, TB)]
                p2chunks = []
                for si, (s0, sn) in enumerate(seqs):
                    for (t0, nt) in seq_chunks(s0, sn):
                        p2chunks.append((si, t0, nt))
                loaded_k = [set(), set()]
                loaded_x = [set(), set()]

                def need_k(si, tl):
                    s0, sn = seqs[si]
                    if tl < 0 or tl >= sn or tl in loaded_k[si]:
                        return
                    loaded_k[si].add(tl)
                    t = s0 + tl
                    sl = t % RK
                    S.op('sync', lambda e, sl=sl, t=t: e.dma_start(out=KR[sl][:], in_=kT_v(t)),
                         reads=[('kT_d', t)], writes=[('KR', sl)], dma=('KR', sl))
                    S.op('sync', lambda e, sl=sl, t=t: e.dma_start(out=VR[sl][:], in_=v_d[t * 128:(t + 1) * 128, :]),
                         reads=[('v_d', t)], writes=[('VR', sl)], dma=('VR', sl))

                def need_x(si, tl):
                    s0, sn = seqs[si]
                    if tl < 0 or tl >= sn or tl in loaded_x[si]:
                        return
                    loaded_x[si].add(tl)
                    t = s0 + tl
                    sl = t % RX
                    S.op('sync', lambda e, sl=sl, t=t: e.dma_start(out=KX[sl][:, :, 0:64], in_=kT_v(t)[:, :, 0:64]),
                         reads=[('kT_d', t)], writes=[('KX', sl)], dma=('KX', sl))
                    S.op('sync', lambda e, sl=sl, t=t: e.dma_start(out=VX[sl][0:64, :], in_=v_d[t * 128:t * 128 + 64, :]),
                         reads=[('v_d', t)], writes=[('VX', sl)], dma=('VX', sl))

                def tile_blocks(si, tl):
                    s0, sn = seqs[si]
                    if tl == 0:
                        kts, deltas, special = [0, 1, 2, 3], [0, 1, 2, 3], 'M'
                    elif tl == 1:
                        kts, deltas, special = [0, 1, 2, 3], [-1, 0, 1, 2], 'M'
                    elif tl == sn - 2:
                        kts, deltas, special = [sn - 4, sn - 3, sn - 2, sn - 1], [-2, -1, 0, 1], 'M'
                    elif tl == sn - 1:
                        kts, deltas, special = [sn - 4, sn - 3, sn - 2, sn - 1], [-3, -2, -1, 0], 'M'
                    else:
                        kts, deltas, special = [tl - 2, tl - 1, tl, tl + 1], [-2, -1, 0, 1], 'X'
                    return kts, deltas, special

                def ensure_tile(si, tl):
                    s0, sn = seqs[si]
                    if tl < 0 or tl >= sn:
                        return
                    kts, deltas, special = tile_blocks(si, tl)
                    for kt in kts:
                        need_k(si, kt)
                    if special == 'X':
                        need_x(si, tl + 2)

                def p2_load(ci):
                    si, t0, nt = p2chunks[ci]
                    qs = ci % 2
                    N = nt * 128
                    S.op('sync', lambda e, qs=qs, t0=t0, nt=nt, N=N: e.dma_start(out=HT2[qs][:, :, 0:N], in_=hT_v(t0, nt)),
                         reads=[('hT_d', t0)], writes=[('HT2', qs)], dma=('HT2', qs))

                def p2_qpiece(ci, fc):
                    si, t0, nt = p2chunks[ci]
                    qs = ci % 2
                    N = nt * 128
                    pb = st2['qb']; st2['qb'] = 6 + (pb - 5) % 2
                    for kc in range(KC):
                        S.op('tensor', lambda e, pb=pb, fc=fc, kc=kc, qs=qs, N=N: e.matmul(
                            banks[pb][:, 0:N], Wq[:, kc, fc * 128:(fc + 1) * 128], HT2[qs][:, kc, 0:N],
                            start=(kc == 0), stop=(kc == KC - 1)),
                             reads=[('HT2', qs), 'Wq'], writes=[PS(pb)])
                    evac('scalar' if fc % 2 == 0 else 'vector', lambda fc=fc, qs=qs, N=N: QT[qs][:, fc, 0:N],
                         lambda pb=pb, N=N: banks[pb][:, 0:N], [PS(pb)], [('QT', qs)])

                units = []
                for ci, (si, t0, nt) in enumerate(p2chunks):
                    for j in range(nt):
                        for G in range(2):
                            units.append((ci, j, G))

                def unit_blocks(u):
                    ci, j, G = u
                    si, t0, nt = p2chunks[ci]
                    s0, sn = seqs[si]
                    tl = t0 - s0 + j
                    kts, deltas, special = tile_blocks(si, tl)
                    blocks = []
                    for kt, dl in zip(kts, deltas):
                        sl = (s0 + kt) % RK
                        std_m2 = (special == 'X' and dl == -2)
                        blocks.append(dict(nk=128, K=KR[sl], V=VR[sl], kres=[('KR', sl)], vres=[('VR', sl)],
                                           E=('m2' if std_m2 else 'eb'), i0=7 - 2 * dl))
                    if special == 'X':
                        sl = (s0 + tl + 2) % RX
                        blocks.append(dict(nk=80, K=KX[sl], V=VX[sl], kres=[('KX', sl), ('KXm', sl)],
                                           vres=[('VX', sl), ('VXm', sl)], E='e5', i0=0))
                    else:
                        blocks.append(dict(nk=16, K=KM, V=VM0, kres=['KM'], vres=['VM0'], E='em', i0=0))
                    return blocks, tl, si

                def p2_qk(ui, b):
                    u = units[ui]
                    ci, j, G = u
                    qs = ci % 2
                    ps = ui % 2
                    blocks, tl, si = unit_blocks(u)
                    if G == 0 and b == 0:
                        ensure_tile(si, tl)
                        ensure_tile(si, tl + 1)
                    blk = blocks[b]
                    nk = blk['nk']
                    sbk = st2['sbank']; st2['sbank'] = (sbk + 2) % 4
                    bl = [sbk, sbk + 1]
                    for jj in range(4):
                        fc = 4 * G + jj
                        for par in range(2):
                            S.op('tensor', lambda e, blk=blk, nk=nk, par=par, fc=fc, jj=jj, bk=bl[par], qs=qs, j=j: e.matmul(
                                banks[bk][0:nk, jj * 128:(jj + 1) * 128],
                                blk['K'][par * 64:(par + 1) * 64, fc, 0:nk],
                                QT[qs][par * 64:(par + 1) * 64, fc, j * 128:(j + 1) * 128],
                                start=True, stop=True),
                                 reads=blk['kres'] + [('QT', qs)], writes=[PS(bl[par])])
                    for par in range(2):
                        S.op('scalar', lambda e, nk=nk, b=b, par=par, ps=ps, bk=bl[par]: e.activation(
                            out=PT[ps][0:nk, b, par, :], in_=banks[bk][0:nk, :], func=AF.Exp, scale=0.125),
                             reads=[PS(bl[par])], writes=[('PT', ps, b, par)])
                        hsl = slice(par * 8 + 4 * G, par * 8 + 4 * G + 4)
                        if blk['E'] == 'eb':
                            i0 = blk['i0']
                            ein = lambda hsl=hsl, i0=i0: EB[:, hsl, i0:i0 + 2, :]
                            pin = lambda ps=ps, b=b, par=par: PT[ps][:, b, par, :].rearrange('p (a b c) -> p a b c', a=4, b=2)
                        elif blk['E'] == 'm2':
                            ein = lambda hsl=hsl: E_m2[:, hsl, :]
                            pin = lambda ps=ps, b=b, par=par: PT[ps][:, b, par, :].rearrange('p (a b) -> p a b', a=4)
                        elif blk['E'] == 'e5':
                            ein = lambda hsl=hsl: E5[0:80, hsl, :]
                            pin = lambda ps=ps, b=b, par=par: PT[ps][0:80, b, par, :].rearrange('p (a b) -> p a b', a=4)
                        else:
                            ein = lambda hsl=hsl: EM[0:16, hsl, :]
                            pin = lambda ps=ps, b=b, par=par: PT[ps][0:16, b, par, :].rearrange('p (a b) -> p a b', a=4)
                        S.op('vector', lambda e, ein=ein, pin=pin: e.tensor_tensor(out=pin(), in0=pin(), in1=ein(), op=ALU.mult),
                             reads=[('PT', ps, b, par)], writes=[('PT', ps, b, par)])

                def p2_pv(ui, par, jj):
                    u = units[ui]
                    ci, j, G = u
                    ps = ui % 2
                    blocks, tl, si = unit_blocks(u)
                    ab = 4 + par
                    fc = 4 * G + jj
                    h = 2 * fc + par
                    for b, blk in enumerate(blocks):
                        nk = blk['nk']
                        S.op('tensor', lambda e, blk=blk, nk=nk, b=b, par=par, jj=jj, h=h, ab=ab, ps=ps: e.matmul(
                            banks[ab][:, jj * 65:(jj + 1) * 65],
                            PT[ps][0:nk, b, par, jj * 128:(jj + 1) * 128],
                            blk['V'][0:nk, h * 65:(h + 1) * 65],
                            start=(b == 0), stop=(b == len(blocks) - 1)),
                             reads=blk['vres'] + [('PT', ps, b, par)], writes=[PS(ab)])

                def p2_norm(ui, par):
                    u = units[ui]
                    ci, j, G = u
                    si_, t0, nt = p2chunks[ci]
                    t = t0 + j
                    at = t % 2
                    ab = 4 + par
                    accv = lambda ab=ab: banks[ab][:, 0:260].rearrange('p (a b) -> p a b', a=4)
                    rs = slice(par * 4, par * 4 + 4)
                    S.op('vector', lambda e, accv=accv, rs=rs: e.reciprocal(rec[:, rs], accv()[:, :, 64]),
                         reads=[PS(ab)], writes=[('rec', par)])
                    outv = lambda at=at, G=G, par=par: ATT[at][:, :].rearrange('p (a b c) -> p a b c', a=8, b=2)[:, 4 * G:4 * G + 4, par, :]
                    S.op('vector', lambda e, accv=accv, outv=outv, rs=rs: e.tensor_tensor(
                        out=outv(), in0=accv()[:, :, 0:64],
                        in1=rec[:, rs].unsqueeze(2).to_broadcast([128, 4, 64]), op=ALU.mult),
                         reads=[PS(ab), ('rec', par)], writes=[('ATT', at)])
                    if G == 1 and par == 1:
                        S.op('gpsimd', lambda e, at=at, t=t: e.dma_start(out=at_d[t * 128:(t + 1) * 128, :], in_=ATT[at][:]),
                             reads=[('ATT', at)], writes=[('at_d', t)], dma=('sA', at))

                def p2_prev_slot(pu, b):
                    if pu < 0:
                        return
                    if b == 1:
                        p2_pv(pu, 0, 0); p2_pv(pu, 0, 1)
                    elif b == 2:
                        p2_pv(pu, 0, 2); p2_pv(pu, 0, 3); p2_norm(pu, 0)
                    elif b == 3:
                        p2_pv(pu, 1, 0); p2_pv(pu, 1, 1)
                    elif b == 4:
                        p2_pv(pu, 1, 2); p2_pv(pu, 1, 3); p2_norm(pu, 1)

                for fc in range(KC):
                    p2_qpiece(0, fc)
                if len(p2chunks) > 1:
                    p2_load(1)
                uic = {}
                for ui, u in enumerate(units):
                    ci, j, G = u
                    nt = p2chunks[ci][2]
                    k = uic.get(ci, 0); uic[ci] = k + 1
                    ppu = (KC + 2 * nt - 1) // (2 * nt)
                    for b in range(5):
                        p2_qk(ui, b)
                        if b == 0:
                            if ci + 1 < len(p2chunks):
                                for fc in range(k * ppu, min(KC, (k + 1) * ppu)):
                                    p2_qpiece(ci + 1, fc)
                                if k == 2 * nt - 1 and ci + 2 < len(p2chunks):
                                    p2_load(ci + 2)
                        else:
                            p2_prev_slot(ui - 1, b)
                for b in range(1, 5):
                    p2_prev_slot(len(units) - 1, b)
                S.flush()

        with ExitStack() as es:
            Wg = sb(es, 'Wg', [128, KC, 2048], BF16)
            Wo = sb(es, 'Wo', [128, 16, 1024], BF16)
            fgb = sb(es, 'fgb', [128, D], F32)
            S.op('sync', lambda e: e.dma_start(out=fgb[:], in_=fgb_d), writes=['fgb'], dma='c6b')

            HT3 = [sb(es, 'HT3_%d' % i, [128, KC, 512], BF16) for i in range(2)]
            UW = [sb(es, 'UW%d' % i, [128, 5, D], BF16) for i in range(2)]

            def windows(N):
                out = []
                o = 0
                while o < N:
                    out.append((o, min(112, N - o)))
                    o += 112
                return out
            AB = [sb(es, 'AB%d' % i, [128, 4, D], BF16) for i in range(2)]
            XR = [sb(es, 'XR%d' % i, [128, D], F32) for i in range(3)]
            PLT = sb(es, 'PLT', [128, KC, 512], BF16)
            MIX = sb(es, 'MIX', [128, 16, 512], BF16)
            TH = [sb(es, 'TH%d' % i, [128, 512], F32) for i in range(2)]
            SG = [sb(es, 'SG%d' % i, [128, 512], F32) for i in range(2)]
            Y = [sb(es, 'Y%d' % i, [128, D], F32) for i in range(2)]
            O = [sb(es, 'O%d' % i, [128, D], F32) for i in range(2)]
            junk3 = sb(es, 'junk3', [128, D], BF16)
            ssq3 = sb(es, 'ssq3', [128, 2], F32)
            sd3 = sb(es, 'sd3', [128, 2], F32)
            rs3 = sb(es, 'rs3', [128, 2], F32)

            st3 = dict(pbk=0, tcount=0, xrc=0, ycount=0)

            def nextbank():
                b = st3['pbk']; st3['pbk'] = (b + 1) % 8
                return b

            p3chunks = []
            for (s0, sn) in [(0, TA), (TA, TB)]:
                for (t0, nt) in seq_chunks(s0, sn):
                    p3chunks.append((s0, sn, t0, nt))

            def p3_load(ci):
                s0, sn, t0, nt = p3chunks[ci]
                N = nt * 128
                cs = ci % 2
                first = (t0 == s0)
                last = (t0 + nt == s0 + sn)
                S.op('sync', lambda e, cs=cs, t0=t0, nt=nt, N=N: e.dma_start(out=HT3[cs][:, :, 0:N], in_=hT_v(t0, nt)),
                     reads=[('hT_d', t0)], writes=[('HT3', cs)], dma=('HT3', cs))
                S0, S1 = s0 * 128, (s0 + sn) * 128
                for k, (o, n) in enumerate(windows(N)):
                    r0 = t0 * 128 + o - 8
                    if r0 < S0:
                        S.op('sync', lambda e, cs=cs, k=k: e.dma_start(out=UW[cs][0:8, k, :], in_=u_d[MT * 128 + 8:MT * 128 + 16, :]),
                             reads=[('u_d', MT)], writes=[('UWa', cs, k)], dma=('UW', cs))
                        S.op('sync', lambda e, cs=cs, k=k, S0=S0: e.dma_start(out=UW[cs][8:128, k, :], in_=u_d[S0:S0 + 120, :]),
                             reads=[('u_d', s0)], writes=[('UW', cs, k)], dma=('UW', cs))
                    else:
                        avail = 128
                        S.op('sync', lambda e, cs=cs, k=k, r0=r0, avail=avail: e.dma_start(
                            out=UW[cs][0:avail, k, :], in_=u_d[r0:r0 + avail, :]),
                             reads=[('u_d', tt) for tt in range(r0 // 128, (r0 + avail - 1) // 128 + 1)],
                             writes=[('UW', cs, k), ('UWa', cs, k)], dma=('UW', cs))
                S.op('sync', lambda e, cs=cs, t0=t0, nt=nt: e.dma_start(
                    out=AB[cs][:, 0:nt, :], in_=at_d[t0 * 128:(t0 + nt) * 128, :].rearrange('(a p) f -> p a f', p=128)),
                     reads=[('at_d', tt) for tt in range(t0, t0 + nt)], writes=[('AB', cs)], dma=('AB', cs))

            def p3_pooled(ci):
                s0, sn, t0, nt = p3chunks[ci]
                N = nt * 128
                cs = ci % 2
                last = (t0 + nt == s0 + sn)
                wins = windows(N)
                for cc in range(KC):
                    g = cc // 2
                    wdt = float(POOL_W[g])
                    pb = nextbank()
                    for k, (o, n) in enumerate(wins):
                        lastw = last and (k == len(wins) - 1)
                        bsrc = bandl[n] if lastw else bandw
                        S.op('tensor', lambda e, pb=pb, k=k, o=o, n=n, cs=cs, cc=cc, g=g, bsrc=bsrc: e.matmul(
                            banks[pb][:, o:o + n], UW[cs][:, k, cc * 128:(cc + 1) * 128], bsrc[:, g, 0:n],
                            start=True, stop=True), reads=[('UW', cs, k), ('UWa', cs, k), 'band'], writes=[PS(pb)])
                    o_l, n_l = wins[-1]
                    nint = o_l if last else N
                    if nint > 0:
                        S.op('vector', lambda e, pb=pb, cc=cc, nint=nint, wdt=wdt: e.tensor_scalar(
                            out=PLT[:, cc, 0:nint], in0=banks[pb][:, 0:nint], scalar1=1.0 / wdt, scalar2=None, op0=ALU.mult),
                             reads=[PS(pb)], writes=[('PLT', cc)])
                    if last:
                        S.op('vector', lambda e, pb=pb, cc=cc, g=g, o_l=o_l, n_l=n_l: e.tensor_tensor(
                            out=PLT[:, cc, o_l:o_l + n_l], in0=banks[pb][:, o_l:o_l + n_l],
                            in1=invl[n_l][:, g, :], op=ALU.mult), reads=[PS(pb), 'band'], writes=[('PLT', cc)])

            def p3_gates(ci):
                s0, sn, t0, nt = p3chunks[ci]
                N = nt * 128
                cs = ci % 2
                for dc in range(KC):
                    g = dc // 2
                    pg = nextbank()
                    for kc in range(KC):
                        S.op('tensor', lambda e, pg=pg, dc=dc, kc=kc, cs=cs, N=N: e.matmul(
                            banks[pg][:, 0:N], Wg[:, kc, dc * 128:(dc + 1) * 128], HT3[cs][:, kc, 0:N],
                            start=(kc == 0), stop=(kc == KC - 1)), reads=[('HT3', cs), ('Wg', dc // 2)], writes=[PS(pg)])
                    pm = nextbank()
                    for k2 in range(2):
                        S.op('tensor', lambda e, pm=pm, dc=dc, k2=k2, g=g, N=N: e.matmul(
                            banks[pm][:, 0:N], Wp[:, g * 2 + k2, (dc % 2) * 128:(dc % 2 + 1) * 128], PLT[:, g * 2 + k2, 0:N],
                            start=(k2 == 0), stop=(k2 == 1)), reads=[('PLT', g * 2 + k2), 'Wp'], writes=[PS(pm)])
                    ts = st3['tcount'] % 2; st3['tcount'] += 1
                    S.op('scalar', lambda e, ts=ts, pg=pg, N=N: e.activation(out=TH[ts][:, 0:N], in_=banks[pg][:, 0:N],
                                                                            func=AF.Tanh, scale=0.5),
                         reads=[PS(pg)], writes=[('TH', ts)])
                    S.op('vector', lambda e, ts=ts, pg=pg, N=N: e.scalar_tensor_tensor(
                        out=SG[ts][:, 0:N], in0=TH[ts][:, 0:N], scalar=1.0, in1=banks[pg][:, 0:N],
                        op0=ALU.add, op1=ALU.mult), reads=[('TH', ts), PS(pg)], writes=[('SG', ts)])
                    S.op('vector', lambda e, ts=ts, pm=pm, dc=dc, N=N: e.scalar_tensor_tensor(
                        out=MIX[:, dc, 0:N], in0=banks[pm][:, 0:N], scalar=psc[:, dc:dc + 1], in1=SG[ts][:, 0:N],
                        op0=ALU.mult, op1=ALU.mult), reads=[('SG', ts), PS(pm), 'psc'], writes=[('MIX', dc)])
                for fc in range(KC):
                    pg = nextbank()
                    for kc in range(KC):
                        S.op('tensor', lambda e, pg=pg, fc=fc, kc=kc, cs=cs, N=N: e.matmul(
                            banks[pg][:, 0:N], Wg[:, kc, 1024 + fc * 128:1024 + (fc + 1) * 128], HT3[cs][:, kc, 0:N],
                            start=(kc == 0), stop=(kc == KC - 1)), reads=[('HT3', cs), ('Wg', 4 + fc // 2)], writes=[PS(pg)])
                    pa = nextbank()
                    pab = banks[pa]
                    for j in range(nt):
                        S.op('tensor', lambda e, pab=pab, j=j, fc=fc, cs=cs: e.matmul(
                            pab[:, j * 128:(j + 1) * 128], AB[cs][:, j, fc * 128:(fc + 1) * 128], ident[:],
                            start=True, stop=True),
                             reads=[('AB', cs), 'ident'], writes=[PS(pa)])
                    ts = st3['tcount'] % 2; st3['tcount'] += 1
                    S.op('scalar', lambda e, ts=ts, pg=pg, N=N: e.activation(out=TH[ts][:, 0:N], in_=banks[pg][:, 0:N],
                                                                            func=AF.Tanh, scale=0.5),
                         reads=[PS(pg)], writes=[('TH', ts)])
                    S.op('vector', lambda e, ts=ts, pg=pg, N=N: e.scalar_tensor_tensor(
                        out=SG[ts][:, 0:N], in0=TH[ts][:, 0:N], scalar=1.0, in1=banks[pg][:, 0:N],
                        op0=ALU.add, op1=ALU.mult), reads=[('TH', ts), PS(pg)], writes=[('SG', ts)])
                    S.op('vector', lambda e, ts=ts, pab=pab, fc=fc, N=N: e.tensor_tensor(
                        out=MIX[:, 8 + fc, 0:N], in0=pab[:, 0:N], in1=SG[ts][:, 0:N], op=ALU.mult),
                         reads=[('SG', ts), PS(pa)], writes=[('MIX', 8 + fc)])
            def p3_out(ci):
                s0, sn, t0, nt = p3chunks[ci]
                for j in range(nt):
                    t = t0 + j
                    xr = st3['xrc'] % 3; st3['xrc'] += 1
                    S.op('sync', lambda e, xr=xr, t=t: e.dma_start(out=XR[xr][:], in_=xall[t * 128:(t + 1) * 128, :]),
                         writes=[('XR', xr)], dma=('XR', xr))
                    ys = st3['ycount'] % 2; st3['ycount'] += 1
                    for cg in range(2):
                        po = nextbank()
                        for kc in range(16):
                            S.op('tensor', lambda e, po=po, kc=kc, cg=cg, j=j: e.matmul(
                                banks[po][:, :], MIX[:, kc, j * 128:(j + 1) * 128], Wo[:, kc, cg * 512:(cg + 1) * 512],
                                start=(kc == 0), stop=(kc == 15)), reads=[('MIX', kc), ('Wo', kc // 4)], writes=[PS(po)])
                        S.op('vector', lambda e, po=po, cg=cg, ys=ys, xr=xr: e.tensor_tensor(
                            out=Y[ys][:, cg * 512:(cg + 1) * 512], in0=banks[po][:, :], in1=XR[xr][:, cg * 512:(cg + 1) * 512],
                            op=ALU.add), reads=[PS(po), ('XR', xr)], writes=[('Y', ys)])
                    S.op('scalar', lambda e, ys=ys: e.activation(out=junk3[:], in_=Y[ys][:], func=AF.Square,
                                                                accum_out=ssq3[:, ys:ys + 1]),
                         reads=[('Y', ys)], writes=['junk3', ('ssq3', ys)])
                    S.op('scalar', lambda e, ys=ys: e.activation(out=sd3[:, ys:ys + 1], in_=ssq3[:, ys:ys + 1], func=AF.Sqrt,
                                                                bias=epsb[:], scale=1.0 / D),
                         reads=[('ssq3', ys), 'epsb'], writes=[('sd3', ys)])
                    S.op('vector', lambda e, ys=ys: e.reciprocal(rs3[:, ys:ys + 1], sd3[:, ys:ys + 1]),
                         reads=[('sd3', ys)], writes=[('rs3', ys)])
                    S.op('vector', lambda e, ys=ys: e.scalar_tensor_tensor(
                        out=O[ys][:], in0=Y[ys][:], scalar=rs3[:, ys:ys + 1], in1=fgb[:], op0=ALU.mult, op1=ALU.mult),
                         reads=[('Y', ys), ('rs3', ys), 'fgb'], writes=[('O', ys)])
                    S.op('gpsimd', lambda e, ys=ys, t=t: e.dma_start(out=y_d[t * 128:(t + 1) * 128, :], in_=O[ys][:]),
                         reads=[('O', ys)], writes=[('y_d', t)], dma=('sO', ys))

            p3_load(0)
            for cb in range(8):
                src0 = 1024 if cb < 4 else 5120
                S.op('gpsimd', lambda e, cb=cb, src0=src0: e.dma_start(
                    out=Wg[:, :, cb * 256:(cb + 1) * 256],
                    in_=w_in_v[:, :, src0 + (cb % 4) * 256:src0 + (cb % 4 + 1) * 256]),
                     reads=([('HT3', 0), ('AB', 0)] + [('UW', 0, k) for k in range(5)] if cb == 0 else []), writes=[('Wg', cb)], dma=('wg', cb))
            for cb in range(4):
                S.op('gpsimd', lambda e, cb=cb: e.dma_start(out=Wo[:, cb * 4:(cb + 1) * 4, :], in_=w_out_v[:, cb * 4:(cb + 1) * 4, :]),
                     writes=[('Wo', cb)], dma=('wo', cb))
            p3_pooled(0)
            for ci in range(len(p3chunks)):
                if ci + 1 < len(p3chunks):
                    p3_load(ci + 1)
                p3_gates(ci)
                if ci + 1 < len(p3chunks):
                    p3_pooled(ci + 1)
                p3_out(ci)
            S.flush()
    return nc


def _constants():
    ident = np.eye(128, dtype=np.float32)
    consts = dict(ident=ident)
    tp = np.arange(128)[:, None] - 8
    bandw = np.zeros((128, 4, 112), np.float32)
    for g, w in enumerate(POOL_W):
        t = np.arange(112)[None, :]
        inwin = (tp >= t - w // 2) & (tp <= t + w // 2 - 1)
        bandw[:, g, :] = inwin.astype(np.float32) - w * (tp == t)
    consts['bandw'] = bandw
    for n in (64, 32):
        bl = np.zeros((128, 4, n), np.float32)
        il = np.zeros((128, 4, n), np.float32)
        for g, w in enumerate(POOL_W):
            t = np.arange(n)[None, :]
            inwin = (tp >= t - w // 2) & (tp <= t + w // 2 - 1) & (tp <= n - 1)
            cnt = np.minimum(t + w // 2 - 1, n - 1) - (t - w // 2) + 1
            bl[:, g, :] = inwin.astype(np.float32) - cnt.astype(np.float32) * (tp == t)
            il[:, g, :] = 1.0 / cnt.astype(np.float32)
        consts['bandl%d' % n] = bl
        consts['invl%d' % n] = il
    cols = np.arange(64)
    cstart = np.clip(cols - 8, 0, 48)
    kc = np.arange(64)
    consts['mask01'] = ((kc[:, None] >= cstart[None, :]) & (kc[:, None] < cstart[None, :] + 16)).astype(np.float32)
    return consts


_CACHE = {}


def kernel(x_prompt, x_sample, meta_tokens, norm_g, w_in, w_pool, pool_scale, rpb, meta_bias, w_out, final_g):
    f = lambda a: np.ascontiguousarray(np.asarray(a, dtype=np.float32))
    x_prompt, x_sample, meta_tokens = f(x_prompt), f(x_sample), f(meta_tokens)
    norm_g, w_in, w_pool, pool_scale = f(norm_g)[0], f(w_in)[0], f(w_pool)[0], f(pool_scale)[0]
    rpb, meta_bias, w_out, final_g = f(rpb)[0], f(meta_bias)[0], f(w_out)[0], f(final_g)
    if 'nc' not in _CACHE:
        _CACHE['nc'] = build_program()
        _CACHE['const'] = _constants()
    nc = _CACHE['nc']
    C = _CACHE['const']
    hperm = np.array([2 * (hp % 8) + hp // 8 for hp in range(NH)])
    kc = np.arange(64)[:, None]
    qc = np.arange(64)[None, :]
    cidx = np.clip(kc - qc + 15, 0, 30)
    ridx = np.clip(14 - (np.arange(16) - 1), 0, 14)
    b0 = rpb[hperm][:, ridx][:, :, cidx]
    b0 = np.ascontiguousarray(b0.transpose(2, 0, 1, 3))
    mbT = np.ascontiguousarray(meta_bias[hperm].T)
    gT = np.ascontiguousarray(norm_g.reshape(KC, 128).T)
    psT = np.ascontiguousarray(pool_scale.reshape(KC, 128).T)
    fgb = np.ascontiguousarray(np.broadcast_to(final_g[None, :], (128, D)))
    metax = np.zeros((128, D), np.float32)
    metax[:16] = meta_tokens
    in_maps = []
    for i in range(NCORES):
        half = i % 2
        tok0 = 0 if half == 0 else 30 * 128
        xb = x_sample[i // 2][tok0:tok0 + TB * 128]
        xall = np.concatenate([x_prompt[i], xb, metax], axis=0)
        m = dict(xall=xall, w_in=w_in, w_pool=w_pool, w_out=w_out, gT=gT, psT=psT, fgb=fgb, mbT=mbT, b0=b0)
        m.update(C)
        in_maps.append(m)
    res = run_bass_kernel_spmd(nc, in_maps, core_ids=list(range(NCORES)))
    y_prompt = np.empty((8, 2048, D), np.float32)
    y_sample = np.empty((4, 8192, D), np.float32)
    for i in range(NCORES):
        y = res.results[i]['y']
        y_prompt[i] = y[:TA * 128]
        yb = y[TA * 128:]
        if i % 2 == 0:
            y_sample[i // 2][:4096] = yb[:4096]
        else:
            y_sample[i // 2][4096:] = yb[2 * 128:]
    return (y_prompt, y_sample)
```

```python
import re
import numpy as np
from contextlib import ExitStack
import concourse.bass as bass
import concourse.mybir as mybir
from concourse.bass_utils import run_bass_kernel_spmd

F32 = mybir.dt.float32
BF16 = mybir.dt.bfloat16
AF = mybir.ActivationFunctionType
ALU = mybir.AluOpType

NCORES = 8
D = 1024
KC = 8
NH = 16
HD = 64
VW = NH * 65
TA = 16
TB = 34
NT = TA + TB + 1
MT = TA + TB
EPS = 1e-6
POOL_W = (2, 4, 8, 16)
ENGS = ['sync', 'gpsimd', 'scalar', 'vector', 'tensor']


class Op:
    __slots__ = ('eng', 'fn', 'deps', 'needed', 'val', 'is_dma', 'sem', 'idx', 'ent')


class Sched:
    def __init__(self, nc, es):
        self.nc = nc
        self.esem = {e: es.enter_context(nc.semaphore('s_' + e)) for e in ENGS}
        self.ecnt = {e: 0 for e in ENGS}
        self.dsem = {}
        self.es = es
        self.reset()

    def reset(self):
        self.q = {e: [] for e in ENGS}
        self.lastw = {}
        self.readers = {}
        self.n = 0

    def _tok(self, o):
        return o.sem if o.is_dma else o.eng

    def op(self, eng, fn, reads=(), writes=(), dma=None):
        o = Op()
        o.eng = eng; o.fn = fn; o.needed = False; o.val = None; o.idx = self.n
        self.n += 1
        o.is_dma = dma is not None
        o.sem = None
        if o.is_dma:
            if dma not in self.dsem:
                self.dsem[dma] = [self.es.enter_context(self.nc.semaphore('d_' + re.sub('[^0-9a-zA-Z]+', '_', str(dma)))), 0]
            ent = self.dsem[dma]
            o.sem = ent[0]; o.ent = ent
        deps = {}

        def add(d):
            if d is None:
                return
            if (not d.is_dma) and (not o.is_dma) and d.eng == 'tensor' and eng == 'tensor':
                return
            k = id(d.sem) if d.is_dma else d.eng
            cur = deps.get(k)
            if cur is None or cur.idx < d.idx:
                deps[k] = d
        for r in reads:
            add(self.lastw.get(r))
        for r in writes:
            add(self.lastw.get(r))
            for d in self.readers.get(r, {}).values():
                add(d)
        o.deps = [(d, (d.ent[1] if d.is_dma else None)) for d in deps.values()]
        if o.is_dma:
            o.ent[1] += 16
            o.val = o.ent[1]
        for d, _ in o.deps:
            d.needed = True
        for r in reads:
            self.readers.setdefault(r, {})[(id(o.sem) if o.is_dma else eng)] = o
        for r in writes:
            self.lastw[r] = o
            self.readers[r] = {}
        self.q[eng].append(o)
        return o

    def flush(self):
        nc = self.nc
        for e in ENGS:
            for o in self.q[e]:
                if (not o.is_dma) and o.needed:
                    self.ecnt[e] += 1
                    o.val = self.ecnt[e]
        final_dma = [(ent[0], ent[1]) for ent in self.dsem.values() if ent[1] > 0]
        with nc.Block(no_gpsimd_drain=True) as block:
            for e in ENGS:
                ops = self.q[e]

                def body(eng, ops=ops, e=e):
                    waited = {}
                    for o in ops:
                        for d, dv in o.deps:
                            sem = d.sem if d.is_dma else self.esem[d.eng]
                            val = dv if d.is_dma else d.val
                            if waited.get(id(sem), 0) >= val:
                                continue
                            eng.wait_ge(sem, val)
                            waited[id(sem)] = val
                        ins = o.fn(eng)
                        if o.is_dma:
                            ins.then_inc(o.sem, 16)
                        elif o.needed:
                            ins.then_inc(self.esem[e], 1)
                    if e == 'sync':
                        for sem, val in final_dma:
                            if waited.get(id(sem), 0) < val:
                                eng.wait_ge(sem, val)
                getattr(block, e)(body)
        self.reset()


def seq_chunks(t0, ntiles):
    out = []
    t = 0
    while t < ntiles:
        n = min(4, ntiles - t)
        out.append((t0 + t, n))
        t += n
    return out


def build_program():
    nc = bass.Bass('TRN2', target_bir_lowering=False)
    dt = lambda name, shape, dtype, kind: nc.dram_tensor(name, shape, dtype, kind=kind).ap()
    xall = dt('xall', [NT * 128, D], F32, 'ExternalInput')
    w_in = dt('w_in', [D, 6144], F32, 'ExternalInput')
    w_pool = dt('w_pool', [4, 256, 256], F32, 'ExternalInput')
    w_out = dt('w_out', [2048, D], F32, 'ExternalInput')
    gT_d = dt('gT', [128, KC], F32, 'ExternalInput')
    psT_d = dt('psT', [128, KC], F32, 'ExternalInput')
    fgb_d = dt('fgb', [128, D], F32, 'ExternalInput')
    mbT_d = dt('mbT', [16, NH], F32, 'ExternalInput')
    b0_d = dt('b0', [64, NH, 16, 64], F32, 'ExternalInput')
    mask_d = dt('mask01', [64, 64], F32, 'ExternalInput')
    ident_d = dt('ident', [128, 128], F32, 'ExternalInput')
    bandw_d = dt('bandw', [128, 4, 112], F32, 'ExternalInput')
    bandl64_d = dt('bandl64', [128, 4, 64], F32, 'ExternalInput')
    bandl32_d = dt('bandl32', [128, 4, 32], F32, 'ExternalInput')
    invl64_d = dt('invl64', [128, 4, 64], F32, 'ExternalInput')
    invl32_d = dt('invl32', [128, 4, 32], F32, 'ExternalInput')
    y_d = dt('y', [(TA + TB) * 128, D], F32, 'ExternalOutput')
    hT_d = dt('hT_s', [NT * 128 * D], BF16, 'Internal')
    kT_d = dt('kT_s', [NT * 128, D], BF16, 'Internal')
    v_d = dt('v_s', [NT * 128, VW], BF16, 'Internal')
    u_d = dt('u_s', [NT * 128, D], BF16, 'Internal')
    wg_s = dt('wg_s', [128, KC, 2048], BF16, 'Internal')
    wo_s = dt('wo_s', [128, 16, D], BF16, 'Internal')
    at_d = dt('at_s', [NT * 128, D], BF16, 'Internal')
    w_in_v = w_in.rearrange('(kc p) n -> p kc n', p=128)
    w_out_v = w_out.rearrange('(kc p) n -> p kc n', p=128)
    w_pool_v = w_pool.rearrange('g (k p) d -> p (g k) d', p=128)
    kT_v = lambda t: kT_d[t * 128:(t + 1) * 128, :].rearrange('p (a b) -> p a b', a=KC)
    hT_v = lambda t0, nt: hT_d[t0 * 128 * D:(t0 + nt) * 128 * D].rearrange('(p k n) -> p k n', p=128, k=KC)

    with ExitStack() as top:
        S = Sched(nc, top)
        bank2 = [top.enter_context(nc.psum_tensor('bankpair%d' % b, [128, 1024], F32)) for b in range(4)]
        banks = []
        for b in range(4):
            banks.append(bank2[b][:, 0:512])
            banks.append(bank2[b][:, 512:1024])
        PS = lambda b: ('ps', b)
        sb = lambda es, name, shape, dtype: es.enter_context(nc.sbuf_tensor('sb_' + name, shape, dtype))

        ident = sb(top, 'ident', [128, 128], BF16)
        gT = sb(top, 'gTs', [128, KC], F32)
        epsb = sb(top, 'epsb', [128, 1], F32)
        Wp = sb(top, 'Wp', [128, 8, 256], BF16)
        bandw = sb(top, 'bandw', [128, 4, 112], BF16)
        bandl = {64: sb(top, 'bandl64', [128, 4, 64], BF16), 32: sb(top, 'bandl32', [128, 4, 32], BF16)}
        invl = {64: sb(top, 'invl64', [128, 4, 64], F32), 32: sb(top, 'invl32', [128, 4, 32], F32)}
        psc = sb(top, 'psc', [128, KC], F32)

        def evac(eng, dst, src, reads, writes):
            if eng == 'vector':
                S.op('vector', lambda e: e.tensor_copy(dst(), src()), reads=reads, writes=writes)
            else:
                S.op('scalar', lambda e: e.copy(dst(), src()), reads=reads, writes=writes)

        with ExitStack() as es12:
            Wq = sb(es12, 'Wq', [128, KC, 1024], BF16)
            EB = sb(es12, 'EB', [128, NH, 15, 64], BF16)
            E_m2 = sb(es12, 'E_m2', [128, NH, 128], BF16)
            E5 = sb(es12, 'E5', [80, NH, 128], BF16)
            EM = sb(es12, 'EM', [16, NH, 128], BF16)

            with ExitStack() as es:
                ctmp = sb(es, 'ctmp', [128, 128], F32)
                S.op('sync', lambda e: e.dma_start(out=ctmp[:], in_=ident_d), writes=['ctmp'], dma='c0')
                S.op('vector', lambda e: e.tensor_copy(ident[:], ctmp[:]), reads=['ctmp'], writes=['ident'])
                S.op('sync', lambda e: e.dma_start(out=gT[:], in_=gT_d), writes=['gT'], dma='c1')
                S.op('vector', lambda e: e.memset(epsb[:], EPS), writes=['epsb'])

                Wk = sb(es, 'Wk', [128, KC, 1024], BF16)
                Wvu = sb(es, 'Wvu', [128, KC, 2048], BF16)
                def p1_weights(xdeps):
                    for cb in range(4):
                        S.op('gpsimd', lambda e, cb=cb: e.dma_start(out=Wk[:, :, cb * 256:(cb + 1) * 256],
                                                                   in_=w_in_v[:, :, 3072 + cb * 256:3072 + (cb + 1) * 256]),
                             reads=(xdeps if cb == 0 else []), writes=[('Wk', cb)], dma=('wk', cb))
                    for cb in range(4):
                        src0 = 4096 if cb < 2 else 0
                        S.op('gpsimd', lambda e, cb=cb, src0=src0: e.dma_start(
                            out=Wvu[:, :, cb * 512:(cb + 1) * 512],
                            in_=w_in_v[:, :, src0 + (cb % 2) * 512:src0 + (cb % 2 + 1) * 512]),
                             writes=[('Wvu', cb)], dma=('wvu', cb))
                    S.op('gpsimd', lambda e: e.dma_start(out=bandw[:], in_=bandw_d), writes=['band'], dma='c7')
                    S.op('gpsimd', lambda e: e.dma_start(out=bandl[64][:], in_=bandl64_d), writes=['band'], dma='c7')
                    S.op('gpsimd', lambda e: e.dma_start(out=bandl[32][:], in_=bandl32_d), writes=['band'], dma='c7')
                    S.op('gpsimd', lambda e: e.dma_start(out=Wp[:], in_=w_pool_v), writes=['Wp'], dma='wp')

                S.op('sync', lambda e: e.dma_start(out=psc[:], in_=psT_d), writes=['psc'], dma='c6')
                S.op('vector', lambda e: e.tensor_scalar(out=psc[:], in0=psc[:], scalar1=0.5, scalar2=None, op0=ALU.mult),
                     reads=['psc'], writes=['psc'])
                S.op('sync', lambda e: e.dma_start(out=invl[64][:], in_=invl64_d), writes=['band'], dma='c8')
                S.op('sync', lambda e: e.dma_start(out=invl[32][:], in_=invl32_d), writes=['band'], dma='c8')

                NX = 4
                X = [sb(es, 'X%d' % i, [128, D], F32) for i in range(NX)]
                XN = [sb(es, 'XN%d' % i, [128, D], BF16) for i in range(2)]
                junk = sb(es, 'junk', [128, D], BF16)
                ssq = sb(es, 'ssq', [128, 8], F32)
                sd = sb(es, 'sd', [128, 8], F32)
                rstd = sb(es, 'rstd', [128, 8], F32)
                HT = [sb(es, 'HT%d' % i, [128, KC, 512], BF16) for i in range(2)]
                KTs = [sb(es, 'KTs%d' % i, [128, KC, 512], BF16) for i in range(2)]
                Vs = [sb(es, 'Vs%d' % i, [128, NH, 65], BF16) for i in range(4)]
                Us = [sb(es, 'Us%d' % i, [128, D], BF16) for i in range(4)]
                for i in range(4):
                    S.op('vector', lambda e, i=i: e.memset(Vs[i][:, :, 64:65], 2.0), writes=[('Vs', i)])

                chunks = seq_chunks(0, TA) + seq_chunks(TA, TB) + [(MT, 1)]
                st = dict(xcnt=0, pbank=2, vcnt=0, ev=0, scnt=0)

                def nb1():
                    b = st['pbank']; st['pbank'] = 2 + (b - 1) % 6
                    return b

                def ev1():
                    st['ev'] += 1
                    return 'vector' if st['ev'] % 2 == 0 else 'scalar'

                tile_sc = {}

                tile_xs = {}

                def p1_load(ci, j):
                    t0, nt = chunks[ci]
                    t = t0 + j
                    xs = st['xcnt'] % NX; st['xcnt'] += 1
                    tile_xs[(ci, j)] = xs
                    S.op('sync', lambda e, xs=xs, t=t: e.dma_start(out=X[xs][:], in_=xall[t * 128:(t + 1) * 128, :]),
                         writes=[('X', xs)], dma=('X', xs))

                def p1_front(ci, j):
                    if (ci, j) not in tile_xs:
                        p1_load(ci, j)
                    xs = tile_xs[(ci, j)]
                    sc = st['scnt'] % 8; st['scnt'] += 1
                    tile_sc[(ci, j)] = sc
                    S.op('scalar', lambda e, xs=xs, sc=sc: e.activation(out=junk[:], in_=X[xs][:], func=AF.Square,
                                                                       accum_out=ssq[:, sc:sc + 1]),
                         reads=[('X', xs)], writes=['junk', ('ssq', sc)])
                    S.op('scalar', lambda e, sc=sc: e.activation(out=sd[:, sc:sc + 1], in_=ssq[:, sc:sc + 1], func=AF.Sqrt,
                                                                bias=epsb[:], scale=1.0 / D),
                         reads=[('ssq', sc), 'epsb'], writes=[('sd', sc)])
                    S.op('vector', lambda e, sc=sc: e.reciprocal(rstd[:, sc:sc + 1], sd[:, sc:sc + 1]),
                         reads=[('sd', sc)], writes=[('rstd', sc)])
                    xn = sc % 2
                    S.op('vector', lambda e, xs=xs, xn=xn, sc=sc: e.tensor_scalar(
                        out=XN[xn][:], in0=X[xs][:], scalar1=rstd[:, sc:sc + 1], scalar2=None, op0=ALU.mult),
                         reads=[('X', xs), ('rstd', sc)], writes=[('XN', xn)])

                def p1_T(ci, j):
                    sc = tile_sc[(ci, j)]
                    xn = sc % 2
                    tb = 0
                    tpb = bank2[tb]
                    for kc in range(KC):
                        S.op('tensor', lambda e, tpb=tpb, xn=xn, kc=kc: e.matmul(
                            tpb[:, kc * 128:(kc + 1) * 128], XN[xn][:, kc * 128:(kc + 1) * 128], ident[:],
                            start=True, stop=True),
                             reads=[('XN', xn), 'ident'], writes=[PS(2 * tb), PS(2 * tb + 1)])

                def p1_HTevac(ci, j):
                    sc = tile_sc[(ci, j)]
                    hs = ci % 2
                    tb = 0
                    tpb = bank2[tb]
                    S.op('vector', lambda e, tpb=tpb, hs=hs, j=j: e.tensor_tensor(
                        out=HT[hs][:, :, j * 128:(j + 1) * 128],
                        in0=tpb[:, 0:1024].rearrange('p (a b) -> p a b', a=KC),
                        in1=gT[:, :].unsqueeze(2).to_broadcast([128, KC, 128]), op=ALU.mult),
                         reads=[PS(2 * tb), PS(2 * tb + 1), 'gT'], writes=[('HT', hs)])

                def p1_K(ci):
                    t0, nt = chunks[ci]
                    N = nt * 128
                    hs = ci % 2
                    for fc in range(KC):
                        pb = nb1()
                        for kc in range(KC):
                            S.op('tensor', lambda e, pb=pb, fc=fc, kc=kc, hs=hs, N=N: e.matmul(
                                banks[pb][:, 0:N], Wk[:, kc, fc * 128:(fc + 1) * 128], HT[hs][:, kc, 0:N],
                                start=(kc == 0), stop=(kc == KC - 1)),
                                 reads=[('HT', hs), ('Wk', fc // 2)], writes=[PS(pb)])
                        evac(ev1(), lambda fc=fc, hs=hs, N=N: KTs[hs][:, fc, 0:N], lambda pb=pb, N=N: banks[pb][:, 0:N],
                             [PS(pb)], [('KTs', hs)])
                    for j in range(nt):
                        t = t0 + j
                        S.op('gpsimd', lambda e, hs=hs, t=t, j=j: e.dma_start(out=kT_v(t), in_=KTs[hs][:, :, j * 128:(j + 1) * 128]),
                             reads=[('KTs', hs)], writes=[('kT_d', t)], dma=('sK', hs))

                def p1_VU(ci, j):
                    t0, nt = chunks[ci]
                    N = nt * 128
                    hs = ci % 2
                    t = t0 + j
                    vs = st['vcnt'] % 4; st['vcnt'] += 1
                    for cg in range(4):
                        pb = nb1()
                        for kc in range(KC):
                            S.op('tensor', lambda e, pb=pb, cg=cg, kc=kc, hs=hs, j=j: e.matmul(
                                banks[pb][:, :], HT[hs][:, kc, j * 128:(j + 1) * 128], Wvu[:, kc, cg * 512:(cg + 1) * 512],
                                start=(kc == 0), stop=(kc == KC - 1)),
                                 reads=[('HT', hs), ('Wvu', cg)], writes=[PS(pb)])
                        if cg < 2:
                            evac(ev1(), lambda vs=vs, cg=cg: Vs[vs][:, cg * 8:(cg + 1) * 8, 0:64],
                                 lambda pb=pb: banks[pb][:, :].rearrange('p (a b) -> p a b', a=8), [PS(pb)], [('Vs', vs)])
                        else:
                            evac(ev1(), lambda vs=vs, cg=cg: Us[vs][:, (cg - 2) * 512:(cg - 1) * 512],
                                 lambda pb=pb: banks[pb][:, :], [PS(pb)], [('Us', vs)])
                    S.op('gpsimd', lambda e, vs=vs, t=t: e.dma_start(out=v_d[t * 128:(t + 1) * 128, :],
                                                                   in_=Vs[vs][:].rearrange('p a b -> p (a b)')),
                         reads=[('Vs', vs)], writes=[('v_d', t)], dma=('sV', vs))
                    S.op('gpsimd', lambda e, vs=vs, t=t: e.dma_start(out=u_d[t * 128:(t + 1) * 128, :], in_=Us[vs][:]),
                         reads=[('Us', vs)], writes=[('u_d', t)], dma=('sU', vs))
                    if j == nt - 1:
                        S.op('gpsimd', lambda e, hs=hs, t0=t0, nt=nt, N=N: e.dma_start(out=hT_v(t0, nt), in_=HT[hs][:, :, 0:N]),
                             reads=[('HT', hs)], writes=[('hT_d', t0)], dma=('sH', hs))

                msk = sb(es, 'msk', [128, 64], F32)
                mbs = sb(es, 'mbs', [80, NH], F32)
                mbe = sb(es, 'mbe', [80, NH], F32)
                EBf = sb(es, 'EBf', [128, 4, 15, 64], F32)

                def p1_tables(hg):
                    if hg == 0:
                        S.op('sync', lambda e: e.dma_start(out=msk[0:64, :], in_=mask_d), writes=['msk'], dma='c2')
                        S.op('sync', lambda e: e.dma_start(out=msk[64:128, :], in_=mask_d), writes=['msk'], dma='c2')
                        S.op('sync', lambda e: e.dma_start(out=mbs[0:16, :], in_=mbT_d), writes=['mbs'], dma='c3')
                        S.op('sync', lambda e: e.dma_start(out=mbs[64:80, :], in_=mbT_d), writes=['mbs'], dma='c3')
                    S.op('sync', lambda e, hg=hg: e.dma_start(out=EBf[0:64], in_=b0_d[:, hg * 4:(hg + 1) * 4, 1:16, :]),
                         writes=['EBf'], dma='c4')
                    S.op('sync', lambda e, hg=hg: e.dma_start(out=EBf[64:128], in_=b0_d[:, hg * 4:(hg + 1) * 4, 0:15, :]),
                         writes=['EBf'], dma='c4')
                    S.op('scalar', lambda e: e.activation(out=EBf[:], in_=EBf[:], func=AF.Exp), reads=['EBf'], writes=['EBf'])
                    S.op('vector', lambda e, hg=hg: e.tensor_tensor(
                        out=EB[:, hg * 4:(hg + 1) * 4, :, :].rearrange('p a b c -> p (a b) c'),
                        in0=EBf[:].rearrange('p a b c -> p (a b) c'),
                        in1=msk[:, :].unsqueeze(1).to_broadcast([128, 60, 64]), op=ALU.mult),
                         reads=['EBf', 'msk'], writes=['EB'])
                    if hg == 3:
                        S.op('scalar', lambda e: e.activation(out=mbe[0:16, :], in_=mbs[0:16, :], func=AF.Exp), reads=['mbs'], writes=['mbe'])
                        S.op('scalar', lambda e: e.activation(out=mbe[64:80, :], in_=mbs[64:80, :], func=AF.Exp), reads=['mbs'], writes=['mbe'])
                        S.op('vector', lambda e: e.tensor_copy(E_m2[:].rearrange('p h (a b) -> p h a b', a=2), EB[:, :, 11:13, :]),
                             reads=['EB'], writes=['E_m2'])
                        S.op('vector', lambda e: e.memset(E_m2[0:64, :, 64:128], 0.0), writes=['E_m2'])
                        S.op('vector', lambda e: e.memset(E5[0:64, :, 0:64], 0.0), writes=['E5'])
                        S.op('vector', lambda e: e.tensor_copy(E5[0:64, :, 64:128], EB[0:64, :, 4, :]), reads=['EB'], writes=['E5'])
                        S.op('vector', lambda e: e.tensor_copy(E5[64:80, :, :], mbe[64:80, :].unsqueeze(2).to_broadcast([16, NH, 128])),
                             reads=['mbe'], writes=['E5'])
                        S.op('vector', lambda e: e.tensor_copy(EM[0:16, :, :], mbe[0:16, :].unsqueeze(2).to_broadcast([16, NH, 128])),
                             reads=['mbe'], writes=['EM'])
                        for cb in range(4):
                            S.op('gpsimd', lambda e, cb=cb: e.dma_start(out=Wq[:, :, cb * 256:(cb + 1) * 256],
                                                                       in_=w_in_v[:, :, 2048 + cb * 256:2048 + (cb + 1) * 256]),
                                 writes=['Wq'], dma='wq')

                for j in range(chunks[0][1]):
                    p1_load(0, j)
                p1_weights([('X', tile_xs[(0, j)]) for j in range(chunks[0][1])])
                for j in range(chunks[0][1]):
                    p1_front(0, j)
                    p1_T(0, j)
                    p1_HTevac(0, j)
                for ci in range(len(chunks)):
                    ntc = chunks[ci][1]
                    ntn = chunks[ci + 1][1] if ci + 1 < len(chunks) else 0
                    if ntn > 0:
                        p1_front(ci + 1, 0)
                    p1_K(ci)
                    for j in range(max(ntc, ntn)):
                        if j < ntn:
                            p1_T(ci + 1, j)
                        if j + 1 < ntn:
                            p1_front(ci + 1, j + 1)
                        if j < ntn:
                            p1_HTevac(ci + 1, j)
                        if j < ntc:
                            p1_VU(ci, j)
                    if 1 <= ci <= 4:
                        p1_tables(ci - 1)
                S.flush()

            with ExitStack() as es:
                KM = sb(es, 'KM', [128, KC, 16], BF16)
                VM0 = sb(es, 'VM0', [16, VW], BF16)
                HT2 = [sb(es, 'HT2_%d' % i, [128, KC, 512], BF16) for i in range(2)]
                _t0, _nt = seq_chunks(0, TA)[0]
                S.op('sync', lambda e: e.dma_start(out=HT2[0][:, :, 0:_nt * 128], in_=hT_v(_t0, _nt)),
                     reads=[('hT_d', _t0)], writes=[('HT2', 0)], dma=('HT2', 0))
                S.op('sync', lambda e: e.dma_start(out=KM[:], in_=kT_v(MT)[:, :, 0:16]), writes=['KM'], dma='c5')
                S.op('sync', lambda e: e.dma_start(out=VM0[:], in_=v_d[MT * 128:MT * 128 + 16, :]), writes=['VM0'], dma='c5')
                RK = 8
                RX = 4
                KR = [sb(es, 'KR%d' % i, [128, KC, 128], BF16) for i in range(RK)]
                VR = [sb(es, 'VR%d' % i, [128, VW], BF16) for i in range(RK)]
                KX = [sb(es, 'KX%d' % i, [128, KC, 80], BF16) for i in range(RX)]
                VX = [sb(es, 'VX%d' % i, [80, VW], BF16) for i in range(RX)]
                for i in range(RX):
                    S.op('sync', lambda e, i=i: e.dma_start(out=KX[i][:, :, 64:80], in_=kT_v(MT)[:, :, 0:16]),
                         writes=[('KXm', i)], dma='c5')
                    S.op('sync', lambda e, i=i: e.dma_start(out=VX[i][64:80, :], in_=v_d[MT * 128:MT * 128 + 16, :]),
                         writes=[('VXm', i)], dma='c5')
                QT = [sb(es, 'QT%d' % i, [128, KC, 512], BF16) for i in range(2)]
                def p2_convert(cb):
                    if cb < 8:
                        src0 = 1024 if cb < 4 else 5120
                        S.op('gpsimd', lambda e, cb=cb, src0=src0: e.dma_start(
                            out=wg_s[:, :, cb * 256:(cb + 1) * 256],
                            in_=w_in_v[:, :, src0 + (cb % 4) * 256:src0 + (cb % 4 + 1) * 256]),
                             writes=[('wg_s', cb)], dma=('cvg', cb % 2))
                    else:
                        c2 = cb - 8
                        S.op('gpsimd', lambda e, c2=c2: e.dma_start(out=wo_s[:, c2 * 4:(c2 + 1) * 4, :], in_=w_out_v[:, c2 * 4:(c2 + 1) * 4, :]),
                             writes=[('wo_s', c2)], dma=('cvo', c2 % 2))
                PT = [sb(es, 'PT%d' % i, [128, 5, 2, 512], BF16) for i in range(2)]
                ATT = [sb(es, 'ATT%d' % i, [128, D], BF16) for i in range(2)]
                rec = sb(es, 'rec', [128, 8], F32)

                st2 = dict(qb=6, sbank=0, ev=0)
                seqs = [(0, TA), (TA, TB)]
                p2chunks = []
                for si, (s0, sn) in enumerate(seqs):
                    for (t0, nt) in seq_chunks(s0, sn):
                        p2chunks.append((si, t0, nt))
                loaded_k = [set(), set()]
                loaded_x = [set(), set()]

                def need_k(si, tl):
                    s0, sn = seqs[si]
                    if tl < 0 or tl >= sn or tl in loaded_k[si]:
                        return
                    loaded_k[si].add(tl)
                    t = s0 + tl
                    sl = t % RK
                    S.op('sync', lambda e, sl=sl, t=t: e.dma_start(out=KR[sl][:], in_=kT_v(t)),
                         reads=[('kT_d', t)], writes=[('KR', sl)], dma=('KR', sl))
                    S.op('sync', lambda e, sl=sl, t=t: e.dma_start(out=VR[sl][:], in_=v_d[t * 128:(t + 1) * 128, :]),
                         reads=[('v_d', t)], writes=[('VR', sl)], dma=('VR', sl))

                def need_x(si, tl):
                    s0, sn = seqs[si]
                    if tl < 0 or tl >= sn or tl in loaded_x[si]:
                        return
                    loaded_x[si].add(tl)
                    t = s0 + tl
                    sl = t % RX
                    S.op('sync', lambda e, sl=sl, t=t: e.dma_start(out=KX[sl][:, :, 0:64], in_=kT_v(t)[:, :, 0:64]),
                         reads=[('kT_d', t)], writes=[('KX', sl)], dma=('KX', sl))
                    S.op('sync', lambda e, sl=sl, t=t: e.dma_start(out=VX[sl][0:64, :], in_=v_d[t * 128:t * 128 + 64, :]),
                         reads=[('v_d', t)], writes=[('VX', sl)], dma=('VX', sl))

                def tile_blocks(si, tl):
                    s0, sn = seqs[si]
                    if tl == 0:
                        kts, deltas, special = [0, 1, 2, 3], [0, 1, 2, 3], 'M'
                    elif tl == 1:
                        kts, deltas, special = [0, 1, 2, 3], [-1, 0, 1, 2], 'M'
                    elif tl == sn - 2:
                        kts, deltas, special = [sn - 4, sn - 3, sn - 2, sn - 1], [-2, -1, 0, 1], 'M'
                    elif tl == sn - 1:
                        kts, deltas, special = [sn - 4, sn - 3, sn - 2, sn - 1], [-3, -2, -1, 0], 'M'
                    else:
                        kts, deltas, special = [tl - 2, tl - 1, tl, tl + 1], [-2, -1, 0, 1], 'X'
                    return kts, deltas, special

                def ensure_tile(si, tl):
                    s0, sn = seqs[si]
                    if tl < 0 or tl >= sn:
                        return
                    kts, deltas, special = tile_blocks(si, tl)
                    for kt in kts:
                        need_k(si, kt)
                    if special == 'X':
                        need_x(si, tl + 2)

                def p2_load(ci):
                    si, t0, nt = p2chunks[ci]
                    qs = ci % 2
                    N = nt * 128
                    S.op('sync', lambda e, qs=qs, t0=t0, nt=nt, N=N: e.dma_start(out=HT2[qs][:, :, 0:N], in_=hT_v(t0, nt)),
                         reads=[('hT_d', t0)], writes=[('HT2', qs)], dma=('HT2', qs))

                def p2_qpiece(ci, fc):
                    si, t0, nt = p2chunks[ci]
                    qs = ci % 2
                    N = nt * 128
                    pb = st2['qb']; st2['qb'] = 6 + (pb - 5) % 2
                    for kc in range(KC):
                        S.op('tensor', lambda e, pb=pb, fc=fc, kc=kc, qs=qs, N=N: e.matmul(
                            banks[pb][:, 0:N], Wq[:, kc, fc * 128:(fc + 1) * 128], HT2[qs][:, kc, 0:N],
                            start=(kc == 0), stop=(kc == KC - 1)),
                             reads=[('HT2', qs), 'Wq'], writes=[PS(pb)])
                    evac('scalar' if fc % 2 == 0 else 'vector', lambda fc=fc, qs=qs, N=N: QT[qs][:, fc, 0:N],
                         lambda pb=pb, N=N: banks[pb][:, 0:N], [PS(pb)], [('QT', qs)])

                units = []
                for ci, (si, t0, nt) in enumerate(p2chunks):
                    for j in range(nt):
                        for G in range(2):
                            units.append((ci, j, G))

                def unit_blocks(u):
                    ci, j, G = u
                    si, t0, nt = p2chunks[ci]
                    s0, sn = seqs[si]
                    tl = t0 - s0 + j
                    kts, deltas, special = tile_blocks(si, tl)
                    blocks = []
                    for kt, dl in zip(kts, deltas):
                        sl = (s0 + kt) % RK
                        std_m2 = (special == 'X' and dl == -2)
                        blocks.append(dict(nk=128, K=KR[sl], V=VR[sl], kres=[('KR', sl)], vres=[('VR', sl)],
                                           E=('m2' if std_m2 else 'eb'), i0=7 - 2 * dl))
                    if special == 'X':
                        sl = (s0 + tl + 2) % RX
                        blocks.append(dict(nk=80, K=KX[sl], V=VX[sl], kres=[('KX', sl), ('KXm', sl)],
                                           vres=[('VX', sl), ('VXm', sl)], E='e5', i0=0))
                    else:
                        blocks.append(dict(nk=16, K=KM, V=VM0, kres=['KM'], vres=['VM0'], E='em', i0=0))
                    return blocks, tl, si

                def p2_qk(ui, b):
                    u = units[ui]
                    ci, j, G = u
                    qs = ci % 2
                    ps = ui % 2
                    blocks, tl, si = unit_blocks(u)
                    if G == 0 and b == 0:
                        ensure_tile(si, tl)
                        ensure_tile(si, tl + 1)
                    blk = blocks[b]
                    nk = blk['nk']
                    sbk = st2['sbank']; st2['sbank'] = (sbk + 2) % 4
                    bl = [sbk, sbk + 1]
                    for jj in range(4):
                        fc = 4 * G + jj
                        for par in range(2):
                            S.op('tensor', lambda e, blk=blk, nk=nk, par=par, fc=fc, jj=jj, bk=bl[par], qs=qs, j=j: e.matmul(
                                banks[bk][0:nk, jj * 128:(jj + 1) * 128],
                                blk['K'][par * 64:(par + 1) * 64, fc, 0:nk],
                                QT[qs][par * 64:(par + 1) * 64, fc, j * 128:(j + 1) * 128],
                                start=True, stop=True),
                                 reads=blk['kres'] + [('QT', qs)], writes=[PS(bl[par])])
                    for par in range(2):
                        S.op('scalar', lambda e, nk=nk, b=b, par=par, ps=ps, bk=bl[par]: e.activation(
                            out=PT[ps][0:nk, b, par, :], in_=banks[bk][0:nk, :], func=AF.Exp, scale=0.125),
                             reads=[PS(bl[par])], writes=[('PT', ps, b, par)])
                        hsl = slice(par * 8 + 4 * G, par * 8 + 4 * G + 4)
                        if blk['E'] == 'eb':
                            i0 = blk['i0']
                            ein = lambda hsl=hsl, i0=i0: EB[:, hsl, i0:i0 + 2, :]
                            pin = lambda ps=ps, b=b, par=par: PT[ps][:, b, par, :].rearrange('p (a b c) -> p a b c', a=4, b=2)
                        elif blk['E'] == 'm2':
                            ein = lambda hsl=hsl: E_m2[:, hsl, :]
                            pin = lambda ps=ps, b=b, par=par: PT[ps][:, b, par, :].rearrange('p (a b) -> p a b', a=4)
                        elif blk['E'] == 'e5':
                            ein = lambda hsl=hsl: E5[0:80, hsl, :]
                            pin = lambda ps=ps, b=b, par=par: PT[ps][0:80, b, par, :].rearrange('p (a b) -> p a b', a=4)
                        else:
                            ein = lambda hsl=hsl: EM[0:16, hsl, :]
                            pin = lambda ps=ps, b=b, par=par: PT[ps][0:16, b, par, :].rearrange('p (a b) -> p a b', a=4)
                        S.op('vector', lambda e, ein=ein, pin=pin: e.tensor_tensor(out=pin(), in0=pin(), in1=ein(), op=ALU.mult),
                             reads=[('PT', ps, b, par)], writes=[('PT', ps, b, par)])

                def p2_pv(ui, par, jj):
                    u = units[ui]
                    ci, j, G = u
                    ps = ui % 2
                    blocks, tl, si = unit_blocks(u)
                    ab = 4 + par
                    fc = 4 * G + jj
                    h = 2 * fc + par
                    for b, blk in enumerate(blocks):
                        nk = blk['nk']
                        S.op('tensor', lambda e, blk=blk, nk=nk, b=b, par=par, jj=jj, h=h, ab=ab, ps=ps: e.matmul(
                            banks[ab][:, jj * 65:(jj + 1) * 65],
                            PT[ps][0:nk, b, par, jj * 128:(jj + 1) * 128],
                            blk['V'][0:nk, h * 65:(h + 1) * 65],
                            start=(b == 0), stop=(b == len(blocks) - 1)),
                             reads=blk['vres'] + [('PT', ps, b, par)], writes=[PS(ab)])

                def p2_norm(ui, par):
                    u = units[ui]
                    ci, j, G = u
                    si_, t0, nt = p2chunks[ci]
                    t = t0 + j
                    at = t % 2
                    ab = 4 + par
                    accv = lambda ab=ab: banks[ab][:, 0:260].rearrange('p (a b) -> p a b', a=4)
                    rs = slice(par * 4, par * 4 + 4)
                    S.op('vector', lambda e, accv=accv, rs=rs: e.reciprocal(rec[:, rs], accv()[:, :, 64]),
                         reads=[PS(ab)], writes=[('rec', par)])
                    outv = lambda at=at, G=G, par=par: ATT[at][:, :].rearrange('p (a b c) -> p a b c', a=8, b=2)[:, 4 * G:4 * G + 4, par, :]
                    S.op('vector', lambda e, accv=accv, outv=outv, rs=rs: e.tensor_tensor(
                        out=outv(), in0=accv()[:, :, 0:64],
                        in1=rec[:, rs].unsqueeze(2).to_broadcast([128, 4, 64]), op=ALU.mult),
                         reads=[PS(ab), ('rec', par)], writes=[('ATT', at)])
                    if G == 1 and par == 1:
                        S.op('gpsimd', lambda e, at=at, t=t: e.dma_start(out=at_d[t * 128:(t + 1) * 128, :], in_=ATT[at][:]),
                             reads=[('ATT', at)], writes=[('at_d', t)], dma=('sA', at))

                def p2_prev_slot(pu, b):
                    if pu < 0:
                        return
                    if b == 1:
                        p2_pv(pu, 0, 0); p2_pv(pu, 0, 1)
                    elif b == 2:
                        p2_pv(pu, 0, 2); p2_pv(pu, 0, 3); p2_norm(pu, 0)
                    elif b == 3:
                        p2_pv(pu, 1, 0); p2_pv(pu, 1, 1)
                    elif b == 4:
                        p2_pv(pu, 1, 2); p2_pv(pu, 1, 3); p2_norm(pu, 1)

                for fc in range(KC):
                    p2_qpiece(0, fc)
                if len(p2chunks) > 1:
                    p2_load(1)
                uic = {}
                for ui, u in enumerate(units):
                    ci, j, G = u
                    nt = p2chunks[ci][2]
                    k = uic.get(ci, 0); uic[ci] = k + 1
                    ppu = (KC + 2 * nt - 1) // (2 * nt)
                    if 4 <= ui < 16:
                        p2_convert(ui - 4)
                    for b in range(5):
                        p2_qk(ui, b)
                        if b == 0:
                            if ci + 1 < len(p2chunks):
                                for fc in range(k * ppu, min(KC, (k + 1) * ppu)):
                                    p2_qpiece(ci + 1, fc)
                                if k == 2 * nt - 1 and ci + 2 < len(p2chunks):
                                    p2_load(ci + 2)
                        else:
                            p2_prev_slot(ui - 1, b)
                for b in range(1, 5):
                    p2_prev_slot(len(units) - 1, b)
                S.flush()

        with ExitStack() as es:
            Wg = sb(es, 'Wg', [128, KC, 2048], BF16)
            Wo = sb(es, 'Wo', [128, 16, 1024], BF16)
            fgb = sb(es, 'fgb', [128, D], F32)
            S.op('sync', lambda e: e.dma_start(out=fgb[:], in_=fgb_d), writes=['fgb'], dma='c6b')

            HT3 = [sb(es, 'HT3_%d' % i, [128, KC, 512], BF16) for i in range(2)]
            UW = [sb(es, 'UW%d' % i, [128, 5, D], BF16) for i in range(2)]

            def windows(N):
                out = []
                o = 0
                while o < N:
                    out.append((o, min(112, N - o)))
                    o += 112
                return out
            AB = [sb(es, 'AB%d' % i, [128, 4, D], BF16) for i in range(2)]
            XR = [sb(es, 'XR%d' % i, [128, D], F32) for i in range(3)]
            PLT = sb(es, 'PLT', [128, KC, 512], BF16)
            MIX = sb(es, 'MIX', [128, 16, 512], BF16)
            TH = [sb(es, 'TH%d' % i, [128, 512], F32) for i in range(2)]
            SG = [sb(es, 'SG%d' % i, [128, 512], F32) for i in range(2)]
            Y = [sb(es, 'Y%d' % i, [128, D], F32) for i in range(2)]
            O = [sb(es, 'O%d' % i, [128, D], F32) for i in range(2)]
            junk3 = sb(es, 'junk3', [128, D], BF16)
            ssq3 = sb(es, 'ssq3', [128, 2], F32)
            sd3 = sb(es, 'sd3', [128, 2], F32)
            rs3 = sb(es, 'rs3', [128, 2], F32)

            st3 = dict(pbk=0, tcount=0, xrc=0, ycount=0)

            def nextbank():
                b = st3['pbk']; st3['pbk'] = (b + 1) % 8
                return b

            p3chunks = []
            for (s0, sn) in [(0, TA), (TA, TB)]:
                for (t0, nt) in seq_chunks(s0, sn):
                    p3chunks.append((s0, sn, t0, nt))

            def p3_load(ci):
                s0, sn, t0, nt = p3chunks[ci]
                N = nt * 128
                cs = ci % 2
                first = (t0 == s0)
                last = (t0 + nt == s0 + sn)
                S.op('sync', lambda e, cs=cs, t0=t0, nt=nt, N=N: e.dma_start(out=HT3[cs][:, :, 0:N], in_=hT_v(t0, nt)),
                     reads=[('hT_d', t0)], writes=[('HT3', cs)], dma=('HT3', cs))
                S0, S1 = s0 * 128, (s0 + sn) * 128
                for k, (o, n) in enumerate(windows(N)):
                    r0 = t0 * 128 + o - 8
                    if r0 < S0:
                        S.op('sync', lambda e, cs=cs, k=k: e.dma_start(out=UW[cs][0:8, k, :], in_=u_d[MT * 128 + 8:MT * 128 + 16, :]),
                             reads=[('u_d', MT)], writes=[('UWa', cs, k)], dma=('UW', cs))
                        S.op('sync', lambda e, cs=cs, k=k, S0=S0: e.dma_start(out=UW[cs][8:128, k, :], in_=u_d[S0:S0 + 120, :]),
                             reads=[('u_d', s0)], writes=[('UW', cs, k)], dma=('UW', cs))
                    else:
                        avail = 128
                        S.op('sync', lambda e, cs=cs, k=k, r0=r0, avail=avail: e.dma_start(
                            out=UW[cs][0:avail, k, :], in_=u_d[r0:r0 + avail, :]),
                             reads=[('u_d', tt) for tt in range(r0 // 128, (r0 + avail - 1) // 128 + 1)],
                             writes=[('UW', cs, k), ('UWa', cs, k)], dma=('UW', cs))
                S.op('sync', lambda e, cs=cs, t0=t0, nt=nt: e.dma_start(
                    out=AB[cs][:, 0:nt, :], in_=at_d[t0 * 128:(t0 + nt) * 128, :].rearrange('(a p) f -> p a f', p=128)),
                     reads=[('at_d', tt) for tt in range(t0, t0 + nt)], writes=[('AB', cs)], dma=('AB', cs))

            def p3_pooled(ci):
                s0, sn, t0, nt = p3chunks[ci]
                N = nt * 128
                cs = ci % 2
                last = (t0 + nt == s0 + sn)
                wins = windows(N)
                for cc in range(KC):
                    g = cc // 2
                    wdt = float(POOL_W[g])
                    pb = nextbank()
                    for k, (o, n) in enumerate(wins):
                        lastw = last and (k == len(wins) - 1)
                        bsrc = bandl[n] if lastw else bandw
                        S.op('tensor', lambda e, pb=pb, k=k, o=o, n=n, cs=cs, cc=cc, g=g, bsrc=bsrc: e.matmul(
                            banks[pb][:, o:o + n], UW[cs][:, k, cc * 128:(cc + 1) * 128], bsrc[:, g, 0:n],
                            start=True, stop=True), reads=[('UW', cs, k), ('UWa', cs, k), 'band'], writes=[PS(pb)])
                    o_l, n_l = wins[-1]
                    nint = o_l if last else N
                    if nint > 0:
                        S.op('vector', lambda e, pb=pb, cc=cc, nint=nint, wdt=wdt: e.tensor_scalar(
                            out=PLT[:, cc, 0:nint], in0=banks[pb][:, 0:nint], scalar1=1.0 / wdt, scalar2=None, op0=ALU.mult),
                             reads=[PS(pb)], writes=[('PLT', cc)])
                    if last:
                        S.op('vector', lambda e, pb=pb, cc=cc, g=g, o_l=o_l, n_l=n_l: e.tensor_tensor(
                            out=PLT[:, cc, o_l:o_l + n_l], in0=banks[pb][:, o_l:o_l + n_l],
                            in1=invl[n_l][:, g, :], op=ALU.mult), reads=[PS(pb), 'band'], writes=[('PLT', cc)])

            def p3_gates(ci):
                s0, sn, t0, nt = p3chunks[ci]
                N = nt * 128
                cs = ci % 2
                for dc in range(KC):
                    g = dc // 2
                    pg = nextbank()
                    for kc in range(KC):
                        S.op('tensor', lambda e, pg=pg, dc=dc, kc=kc, cs=cs, N=N: e.matmul(
                            banks[pg][:, 0:N], Wg[:, kc, dc * 128:(dc + 1) * 128], HT3[cs][:, kc, 0:N],
                            start=(kc == 0), stop=(kc == KC - 1)), reads=[('HT3', cs), ('Wg', dc // 2)], writes=[PS(pg)])
                    pm = nextbank()
                    for k2 in range(2):
                        S.op('tensor', lambda e, pm=pm, dc=dc, k2=k2, g=g, N=N: e.matmul(
                            banks[pm][:, 0:N], Wp[:, g * 2 + k2, (dc % 2) * 128:(dc % 2 + 1) * 128], PLT[:, g * 2 + k2, 0:N],
                            start=(k2 == 0), stop=(k2 == 1)), reads=[('PLT', g * 2 + k2), 'Wp'], writes=[PS(pm)])
                    ts = st3['tcount'] % 2; st3['tcount'] += 1
                    S.op('scalar', lambda e, ts=ts, pg=pg, N=N: e.activation(out=TH[ts][:, 0:N], in_=banks[pg][:, 0:N],
                                                                            func=AF.Tanh, scale=0.5),
                         reads=[PS(pg)], writes=[('TH', ts)])
                    S.op('vector', lambda e, ts=ts, pg=pg, N=N: e.scalar_tensor_tensor(
                        out=SG[ts][:, 0:N], in0=TH[ts][:, 0:N], scalar=1.0, in1=banks[pg][:, 0:N],
                        op0=ALU.add, op1=ALU.mult), reads=[('TH', ts), PS(pg)], writes=[('SG', ts)])
                    S.op('vector', lambda e, ts=ts, pm=pm, dc=dc, N=N: e.scalar_tensor_tensor(
                        out=MIX[:, dc, 0:N], in0=banks[pm][:, 0:N], scalar=psc[:, dc:dc + 1], in1=SG[ts][:, 0:N],
                        op0=ALU.mult, op1=ALU.mult), reads=[('SG', ts), PS(pm), 'psc'], writes=[('MIX', dc)])
                for fc in range(KC):
                    pg = nextbank()
                    for kc in range(KC):
                        S.op('tensor', lambda e, pg=pg, fc=fc, kc=kc, cs=cs, N=N: e.matmul(
                            banks[pg][:, 0:N], Wg[:, kc, 1024 + fc * 128:1024 + (fc + 1) * 128], HT3[cs][:, kc, 0:N],
                            start=(kc == 0), stop=(kc == KC - 1)), reads=[('HT3', cs), ('Wg', 4 + fc // 2)], writes=[PS(pg)])
                    pa = nextbank()
                    pab = banks[pa]
                    for j in range(nt):
                        S.op('tensor', lambda e, pab=pab, j=j, fc=fc, cs=cs: e.matmul(
                            pab[:, j * 128:(j + 1) * 128], AB[cs][:, j, fc * 128:(fc + 1) * 128], ident[:],
                            start=True, stop=True),
                             reads=[('AB', cs), 'ident'], writes=[PS(pa)])
                    ts = st3['tcount'] % 2; st3['tcount'] += 1
                    S.op('scalar', lambda e, ts=ts, pg=pg, N=N: e.activation(out=TH[ts][:, 0:N], in_=banks[pg][:, 0:N],
                                                                            func=AF.Tanh, scale=0.5),
                         reads=[PS(pg)], writes=[('TH', ts)])
                    S.op('vector', lambda e, ts=ts, pg=pg, N=N: e.scalar_tensor_tensor(
                        out=SG[ts][:, 0:N], in0=TH[ts][:, 0:N], scalar=1.0, in1=banks[pg][:, 0:N],
                        op0=ALU.add, op1=ALU.mult), reads=[('TH', ts), PS(pg)], writes=[('SG', ts)])
                    S.op('vector', lambda e, ts=ts, pab=pab, fc=fc, N=N: e.tensor_tensor(
                        out=MIX[:, 8 + fc, 0:N], in0=pab[:, 0:N], in1=SG[ts][:, 0:N], op=ALU.mult),
                         reads=[('SG', ts), PS(pa)], writes=[('MIX', 8 + fc)])
            def p3_out(ci):
                s0, sn, t0, nt = p3chunks[ci]
                for j in range(nt):
                    t = t0 + j
                    xr = st3['xrc'] % 3; st3['xrc'] += 1
                    S.op('sync', lambda e, xr=xr, t=t: e.dma_start(out=XR[xr][:], in_=xall[t * 128:(t + 1) * 128, :]),
                         writes=[('XR', xr)], dma=('XR', xr))
                    ys = st3['ycount'] % 2; st3['ycount'] += 1
                    for cg in range(2):
                        po = nextbank()
                        for kc in range(16):
                            S.op('tensor', lambda e, po=po, kc=kc, cg=cg, j=j: e.matmul(
                                banks[po][:, :], MIX[:, kc, j * 128:(j + 1) * 128], Wo[:, kc, cg * 512:(cg + 1) * 512],
                                start=(kc == 0), stop=(kc == 15)), reads=[('MIX', kc), ('Wo', kc // 4)], writes=[PS(po)])
                        S.op('vector', lambda e, po=po, cg=cg, ys=ys, xr=xr: e.tensor_tensor(
                            out=Y[ys][:, cg * 512:(cg + 1) * 512], in0=banks[po][:, :], in1=XR[xr][:, cg * 512:(cg + 1) * 512],
                            op=ALU.add), reads=[PS(po), ('XR', xr)], writes=[('Y', ys)])
                    S.op('scalar', lambda e, ys=ys: e.activation(out=junk3[:], in_=Y[ys][:], func=AF.Square,
                                                                accum_out=ssq3[:, ys:ys + 1]),
                         reads=[('Y', ys)], writes=['junk3', ('ssq3', ys)])
                    S.op('scalar', lambda e, ys=ys: e.activation(out=sd3[:, ys:ys + 1], in_=ssq3[:, ys:ys + 1], func=AF.Sqrt,
                                                                bias=epsb[:], scale=1.0 / D),
                         reads=[('ssq3', ys), 'epsb'], writes=[('sd3', ys)])
                    S.op('vector', lambda e, ys=ys: e.reciprocal(rs3[:, ys:ys + 1], sd3[:, ys:ys + 1]),
                         reads=[('sd3', ys)], writes=[('rs3', ys)])
                    S.op('vector', lambda e, ys=ys: e.scalar_tensor_tensor(
                        out=O[ys][:], in0=Y[ys][:], scalar=rs3[:, ys:ys + 1], in1=fgb[:], op0=ALU.mult, op1=ALU.mult),
                         reads=[('Y', ys), ('rs3', ys), 'fgb'], writes=[('O', ys)])
                    S.op('gpsimd', lambda e, ys=ys, t=t: e.dma_start(out=y_d[t * 128:(t + 1) * 128, :], in_=O[ys][:]),
                         reads=[('O', ys)], writes=[('y_d', t)], dma=('sO', ys))

            p3_load(0)
            for cb in range(8):
                S.op('sync', lambda e, cb=cb: e.dma_start(out=Wg[:, :, cb * 256:(cb + 1) * 256], in_=wg_s[:, :, cb * 256:(cb + 1) * 256]),
                     writes=[('Wg', cb)], dma=('wg', cb))
            for cb in range(4):
                S.op('sync', lambda e, cb=cb: e.dma_start(out=Wo[:, cb * 4:(cb + 1) * 4, :], in_=wo_s[:, cb * 4:(cb + 1) * 4, :]),
                     writes=[('Wo', cb)], dma=('wo', cb))
            p3_pooled(0)
            for ci in range(len(p3chunks)):
                if ci + 1 < len(p3chunks):
                    p3_load(ci + 1)
                p3_gates(ci)
                if ci + 1 < len(p3chunks):
                    p3_pooled(ci + 1)
                p3_out(ci)
            S.flush()
    return nc


def _constants():
    ident = np.eye(128, dtype=np.float32)
    consts = dict(ident=ident)
    tp = np.arange(128)[:, None] - 8
    bandw = np.zeros((128, 4, 112), np.float32)
    for g, w in enumerate(POOL_W):
        t = np.arange(112)[None, :]
        inwin = (tp >= t - w // 2) & (tp <= t + w // 2 - 1)
        bandw[:, g, :] = inwin.astype(np.float32) - w * (tp == t)
    consts['bandw'] = bandw
    for n in (64, 32):
        bl = np.zeros((128, 4, n), np.float32)
        il = np.zeros((128, 4, n), np.float32)
        for g, w in enumerate(POOL_W):
            t = np.arange(n)[None, :]
            inwin = (tp >= t - w // 2) & (tp <= t + w // 2 - 1) & (tp <= n - 1)
            cnt = np.minimum(t + w // 2 - 1, n - 1) - (t - w // 2) + 1
            bl[:, g, :] = inwin.astype(np.float32) - cnt.astype(np.float32) * (tp == t)
            il[:, g, :] = 1.0 / cnt.astype(np.float32)
        consts['bandl%d' % n] = bl
        consts['invl%d' % n] = il
    cols = np.arange(64)
    cstart = np.clip(cols - 8, 0, 48)
    kc = np.arange(64)
    consts['mask01'] = ((kc[:, None] >= cstart[None, :]) & (kc[:, None] < cstart[None, :] + 16)).astype(np.float32)
    return consts


_CACHE = {}


def kernel(x_prompt, x_sample, meta_tokens, norm_g, w_in, w_pool, pool_scale, rpb, meta_bias, w_out, final_g):
    f = lambda a: np.ascontiguousarray(np.asarray(a, dtype=np.float32))
    x_prompt, x_sample, meta_tokens = f(x_prompt), f(x_sample), f(meta_tokens)
    norm_g, w_in, w_pool, pool_scale = f(norm_g)[0], f(w_in)[0], f(w_pool)[0], f(pool_scale)[0]
    rpb, meta_bias, w_out, final_g = f(rpb)[0], f(meta_bias)[0], f(w_out)[0], f(final_g)
    if 'nc' not in _CACHE:
        _CACHE['nc'] = build_program()
        _CACHE['const'] = _constants()
    nc = _CACHE['nc']
    C = _CACHE['const']
    hperm = np.array([2 * (hp % 8) + hp // 8 for hp in range(NH)])
    kc = np.arange(64)[:, None]
    qc = np.arange(64)[None, :]
    cidx = np.clip(kc - qc + 15, 0, 30)
    ridx = np.clip(14 - (np.arange(16) - 1), 0, 14)
    b0 = rpb[hperm][:, ridx][:, :, cidx]
    b0 = np.ascontiguousarray(b0.transpose(2, 0, 1, 3))
    mbT = np.ascontiguousarray(meta_bias[hperm].T)
    gT = np.ascontiguousarray(norm_g.reshape(KC, 128).T)
    psT = np.ascontiguousarray(pool_scale.reshape(KC, 128).T)
    fgb = np.ascontiguousarray(np.broadcast_to(final_g[None, :], (128, D)))
    metax = np.zeros((128, D), np.float32)
    metax[:16] = meta_tokens
    in_maps = []
    for i in range(NCORES):
        half = i % 2
        tok0 = 0 if half == 0 else 30 * 128
        xb = x_sample[i // 2][tok0:tok0 + TB * 128]
        xall = np.concatenate([x_prompt[i], xb, metax], axis=0)
        m = dict(xall=xall, w_in=w_in, w_pool=w_pool, w_out=w_out, gT=gT, psT=psT, fgb=fgb, mbT=mbT, b0=b0)
        m.update(C)
        in_maps.append(m)
    res = run_bass_kernel_spmd(nc, in_maps, core_ids=list(range(NCORES)))
    y_prompt = np.empty((8, 2048, D), np.float32)
    y_sample = np.empty((4, 8192, D), np.float32)
    for i in range(NCORES):
        y = res.results[i]['y']
        y_prompt[i] = y[:TA * 128]
        yb = y[TA * 128:]
        if i % 2 == 0:
            y_sample[i // 2][:4096] = yb[:4096]
        else:
            y_sample[i // 2][4096:] = yb[2 * 128:]
    return (y_prompt, y_sample)
```

```python
import re
import numpy as np
from contextlib import ExitStack
import concourse.bass as bass
import concourse.mybir as mybir
from concourse.bass_utils import run_bass_kernel_spmd

F32 = mybir.dt.float32
BF16 = mybir.dt.bfloat16
AF = mybir.ActivationFunctionType
ALU = mybir.AluOpType

NCORES = 8
D = 1024
KC = 8
NH = 16
HD = 64
VW = NH * 65
TA = 16
TB = 34
NT = TA + TB + 1
MT = TA + TB
EPS = 1e-6
POOL_W = (2, 4, 8, 16)
ENGS = ['sync', 'gpsimd', 'scalar', 'vector', 'tensor']


class Op:
    __slots__ = ('eng', 'fn', 'deps', 'needed', 'val', 'is_dma', 'sem', 'idx', 'ent')


class Sched:
    def __init__(self, nc, es):
        self.nc = nc
        self.esem = {e: es.enter_context(nc.semaphore('s_' + e)) for e in ENGS}
        self.ecnt = {e: 0 for e in ENGS}
        self.dsem = {}
        self.es = es
        self.reset()

    def reset(self):
        self.q = {e: [] for e in ENGS}
        self.lastw = {}
        self.readers = {}
        self.n = 0

    def _tok(self, o):
        return o.sem if o.is_dma else o.eng

    def op(self, eng, fn, reads=(), writes=(), dma=None):
        o = Op()
        o.eng = eng; o.fn = fn; o.needed = False; o.val = None; o.idx = self.n
        self.n += 1
        o.is_dma = dma is not None
        o.sem = None
        if o.is_dma:
            if dma not in self.dsem:
                self.dsem[dma] = [self.es.enter_context(self.nc.semaphore('d_' + re.sub('[^0-9a-zA-Z]+', '_', str(dma)))), 0]
            ent = self.dsem[dma]
            o.sem = ent[0]; o.ent = ent
        deps = {}

        def add(d):
            if d is None:
                return
            if (not d.is_dma) and (not o.is_dma) and d.eng == 'tensor' and eng == 'tensor':
                return
            k = id(d.sem) if d.is_dma else d.eng
            cur = deps.get(k)
            if cur is None or cur.idx < d.idx:
                deps[k] = d
        for r in reads:
            add(self.lastw.get(r))
        for r in writes:
            add(self.lastw.get(r))
            for d in self.readers.get(r, {}).values():
                add(d)
        o.deps = [(d, (d.ent[1] if d.is_dma else None)) for d in deps.values()]
        if o.is_dma:
            o.ent[1] += 16
            o.val = o.ent[1]
        for d, _ in o.deps:
            d.needed = True
        for r in reads:
            self.readers.setdefault(r, {})[(id(o.sem) if o.is_dma else eng)] = o
        for r in writes:
            self.lastw[r] = o
            self.readers[r] = {}
        self.q[eng].append(o)
        return o

    def flush(self):
        nc = self.nc
        for e in ENGS:
            for o in self.q[e]:
                if (not o.is_dma) and o.needed:
                    self.ecnt[e] += 1
                    o.val = self.ecnt[e]
        final_dma = [(ent[0], ent[1]) for ent in self.dsem.values() if ent[1] > 0]
        with nc.Block(no_gpsimd_drain=True) as block:
            for e in ENGS:
                ops = self.q[e]

                def body(eng, ops=ops, e=e):
                    waited = {}
                    for o in ops:
                        for d, dv in o.deps:
                            sem = d.sem if d.is_dma else self.esem[d.eng]
                            val = dv if d.is_dma else d.val
                            if waited.get(id(sem), 0) >= val:
                                continue
                            eng.wait_ge(sem, val)
                            waited[id(sem)] = val
                        ins = o.fn(eng)
                        if o.is_dma:
                            ins.then_inc(o.sem, 16)
                        elif o.needed:
                            ins.then_inc(self.esem[e], 1)
                    if e == 'sync':
                        for sem, val in final_dma:
                            if waited.get(id(sem), 0) < val:
                                eng.wait_ge(sem, val)
                getattr(block, e)(body)
        self.reset()


def seq_chunks(t0, ntiles):
    out = []
    t = 0
    while t < ntiles:
        n = min(4, ntiles - t)
        out.append((t0 + t, n))
        t += n
    return out


def build_program():
    nc = bass.Bass('TRN2', target_bir_lowering=False)
    dt = lambda name, shape, dtype, kind: nc.dram_tensor(name, shape, dtype, kind=kind).ap()
    xall = dt('xall', [NT * 128, D], F32, 'ExternalInput')
    w_in = dt('w_in', [D, 6144], F32, 'ExternalInput')
    w_pool = dt('w_pool', [4, 256, 256], F32, 'ExternalInput')
    w_out = dt('w_out', [2048, D], F32, 'ExternalInput')
    gT_d = dt('gT', [128, KC], F32, 'ExternalInput')
    psT_d = dt('psT', [128, KC], F32, 'ExternalInput')
    fgb_d = dt('fgb', [128, D], F32, 'ExternalInput')
    mbT_d = dt('mbT', [16, NH], F32, 'ExternalInput')
    b0_d = dt('b0', [64, NH, 16, 64], F32, 'ExternalInput')
    mask_d = dt('mask01', [64, 64], F32, 'ExternalInput')
    ident_d = dt('ident', [128, 128], F32, 'ExternalInput')
    bandw_d = dt('bandw', [128, 4, 112], F32, 'ExternalInput')
    bandl64_d = dt('bandl64', [128, 4, 64], F32, 'ExternalInput')
    bandl32_d = dt('bandl32', [128, 4, 32], F32, 'ExternalInput')
    invl64_d = dt('invl64', [128, 4, 64], F32, 'ExternalInput')
    invl32_d = dt('invl32', [128, 4, 32], F32, 'ExternalInput')
    y_d = dt('y', [(TA + TB) * 128, D], F32, 'ExternalOutput')
    hT_d = dt('hT_s', [NT * 128 * D], BF16, 'Internal')
    kT_d = dt('kT_s', [NT * 128, D], BF16, 'Internal')
    v_d = dt('v_s', [NT * 128, VW], BF16, 'Internal')
    u_d = dt('u_s', [NT * 128, D], BF16, 'Internal')
    wg_s = dt('wg_s', [128, KC, 2048], BF16, 'Internal')
    wo_s = dt('wo_s', [128, 16, D], BF16, 'Internal')
    at_d = dt('at_s', [NT * 128, D], BF16, 'Internal')
    w_in_v = w_in.rearrange('(kc p) n -> p kc n', p=128)
    w_out_v = w_out.rearrange('(kc p) n -> p kc n', p=128)
    w_pool_v = w_pool.rearrange('g (k p) d -> p (g k) d', p=128)
    kT_v = lambda t: kT_d[t * 128:(t + 1) * 128, :].rearrange('p (a b) -> p a b', a=KC)
    hT_v = lambda t0, nt: hT_d[t0 * 128 * D:(t0 + nt) * 128 * D].rearrange('(p k n) -> p k n', p=128, k=KC)

    with ExitStack() as top:
        S = Sched(nc, top)
        bank2 = [top.enter_context(nc.psum_tensor('bankpair%d' % b, [128, 1024], F32)) for b in range(4)]
        banks = []
        for b in range(4):
            banks.append(bank2[b][:, 0:512])
            banks.append(bank2[b][:, 512:1024])
        PS = lambda b: ('ps', b)
        sb = lambda es, name, shape, dtype: es.enter_context(nc.sbuf_tensor('sb_' + name, shape, dtype))

        ident = sb(top, 'ident', [128, 128], BF16)
        gT = sb(top, 'gTs', [128, KC], F32)
        epsb = sb(top, 'epsb', [128, 1], F32)
        Wp = sb(top, 'Wp', [128, 8, 256], BF16)
        bandw = sb(top, 'bandw', [128, 4, 112], BF16)
        bandl = {64: sb(top, 'bandl64', [128, 4, 64], BF16), 32: sb(top, 'bandl32', [128, 4, 32], BF16)}
        invl = {64: sb(top, 'invl64', [128, 4, 64], F32), 32: sb(top, 'invl32', [128, 4, 32], F32)}
        psc = sb(top, 'psc', [128, KC], F32)

        def evac(eng, dst, src, reads, writes):
            if eng == 'vector':
                S.op('vector', lambda e: e.tensor_copy(dst(), src()), reads=reads, writes=writes)
            else:
                S.op('scalar', lambda e: e.copy(dst(), src()), reads=reads, writes=writes)

        with ExitStack() as es12:
            Wq = sb(es12, 'Wq', [128, KC, 1024], BF16)
            EB = sb(es12, 'EB', [128, NH, 15, 64], BF16)
            E_m2 = sb(es12, 'E_m2', [128, NH, 128], BF16)
            E5 = sb(es12, 'E5', [80, NH, 128], BF16)
            EM = sb(es12, 'EM', [16, NH, 128], BF16)

            with ExitStack() as es:
                ctmp = sb(es, 'ctmp', [128, 128], F32)
                S.op('sync', lambda e: e.dma_start(out=ctmp[:], in_=ident_d), writes=['ctmp'], dma='c0')
                S.op('vector', lambda e: e.tensor_copy(ident[:], ctmp[:]), reads=['ctmp'], writes=['ident'])
                S.op('sync', lambda e: e.dma_start(out=gT[:], in_=gT_d), writes=['gT'], dma='c1')
                S.op('vector', lambda e: e.memset(epsb[:], EPS), writes=['epsb'])

                Wk = sb(es, 'Wk', [128, KC, 1024], BF16)
                Wvu = sb(es, 'Wvu', [128, KC, 2048], BF16)
                def p1_weights(xdeps):
                    for cb in range(4):
                        S.op('gpsimd', lambda e, cb=cb: e.dma_start(out=Wk[:, :, cb * 256:(cb + 1) * 256],
                                                                   in_=w_in_v[:, :, 3072 + cb * 256:3072 + (cb + 1) * 256]),
                             reads=(xdeps if cb == 0 else []), writes=[('Wk', cb)], dma=('wk', cb))
                    for cb in range(4):
                        src0 = 4096 if cb < 2 else 0
                        S.op('gpsimd', lambda e, cb=cb, src0=src0: e.dma_start(
                            out=Wvu[:, :, cb * 512:(cb + 1) * 512],
                            in_=w_in_v[:, :, src0 + (cb % 2) * 512:src0 + (cb % 2 + 1) * 512]),
                             writes=[('Wvu', cb)], dma=('wvu', cb))
                    S.op('gpsimd', lambda e: e.dma_start(out=bandw[:], in_=bandw_d), writes=['band'], dma='c7')
                    S.op('gpsimd', lambda e: e.dma_start(out=bandl[64][:], in_=bandl64_d), writes=['band'], dma='c7')
                    S.op('gpsimd', lambda e: e.dma_start(out=bandl[32][:], in_=bandl32_d), writes=['band'], dma='c7')
                    S.op('gpsimd', lambda e: e.dma_start(out=Wp[:], in_=w_pool_v), writes=['Wp'], dma='wp')

                S.op('sync', lambda e: e.dma_start(out=psc[:], in_=psT_d), writes=['psc'], dma='c6')
                S.op('vector', lambda e: e.tensor_scalar(out=psc[:], in0=psc[:], scalar1=0.5, scalar2=None, op0=ALU.mult),
                     reads=['psc'], writes=['psc'])
                S.op('sync', lambda e: e.dma_start(out=invl[64][:], in_=invl64_d), writes=['band'], dma='c8')
                S.op('sync', lambda e: e.dma_start(out=invl[32][:], in_=invl32_d), writes=['band'], dma='c8')

                NX = 4
                X = [sb(es, 'X%d' % i, [128, D], F32) for i in range(NX)]
                XN = [sb(es, 'XN%d' % i, [128, D], BF16) for i in range(2)]
                junk = sb(es, 'junk', [128, D], BF16)
                ssq = sb(es, 'ssq', [128, 8], F32)
                sd = sb(es, 'sd', [128, 8], F32)
                rstd = sb(es, 'rstd', [128, 8], F32)
                HT = [sb(es, 'HT%d' % i, [128, KC, 512], BF16) for i in range(2)]
                KTs = [sb(es, 'KTs%d' % i, [128, KC, 512], BF16) for i in range(2)]
                Vs = [sb(es, 'Vs%d' % i, [128, NH, 65], BF16) for i in range(4)]
                Us = [sb(es, 'Us%d' % i, [128, D], BF16) for i in range(4)]
                for i in range(4):
                    S.op('vector', lambda e, i=i: e.memset(Vs[i][:, :, 64:65], 2.0), writes=[('Vs', i)])

                chunks = seq_chunks(0, TA) + seq_chunks(TA, TB) + [(MT, 1)]
                st = dict(xcnt=0, pbank=2, vcnt=0, ev=0, scnt=0)

                def nb1():
                    b = st['pbank']; st['pbank'] = 2 + (b - 1) % 6
                    return b

                def ev1():
                    st['ev'] += 1
                    return 'vector' if st['ev'] % 2 == 0 else 'scalar'

                tile_sc = {}

                tile_xs = {}

                def p1_load(ci, j):
                    t0, nt = chunks[ci]
                    t = t0 + j
                    xs = st['xcnt'] % NX; st['xcnt'] += 1
                    tile_xs[(ci, j)] = xs
                    S.op('sync', lambda e, xs=xs, t=t: e.dma_start(out=X[xs][:], in_=xall[t * 128:(t + 1) * 128, :]),
                         writes=[('X', xs)], dma=('X', xs))

                def p1_front(ci, j):
                    if (ci, j) not in tile_xs:
                        p1_load(ci, j)
                    xs = tile_xs[(ci, j)]
                    sc = st['scnt'] % 8; st['scnt'] += 1
                    tile_sc[(ci, j)] = sc
                    S.op('scalar', lambda e, xs=xs, sc=sc: e.activation(out=junk[:], in_=X[xs][:], func=AF.Square,
                                                                       accum_out=ssq[:, sc:sc + 1]),
                         reads=[('X', xs)], writes=['junk', ('ssq', sc)])
                    S.op('scalar', lambda e, sc=sc: e.activation(out=sd[:, sc:sc + 1], in_=ssq[:, sc:sc + 1], func=AF.Sqrt,
                                                                bias=epsb[:], scale=1.0 / D),
                         reads=[('ssq', sc), 'epsb'], writes=[('sd', sc)])
                    S.op('vector', lambda e, sc=sc: e.reciprocal(rstd[:, sc:sc + 1], sd[:, sc:sc + 1]),
                         reads=[('sd', sc)], writes=[('rstd', sc)])
                    xn = sc % 2
                    S.op('vector', lambda e, xs=xs, xn=xn, sc=sc: e.tensor_scalar(
                        out=XN[xn][:], in0=X[xs][:], scalar1=rstd[:, sc:sc + 1], scalar2=None, op0=ALU.mult),
                         reads=[('X', xs), ('rstd', sc)], writes=[('XN', xn)])

                def p1_T(ci, j):
                    sc = tile_sc[(ci, j)]
                    xn = sc % 2
                    tb = 0
                    tpb = bank2[tb]
                    for kc in range(KC):
                        S.op('tensor', lambda e, tpb=tpb, xn=xn, kc=kc: e.matmul(
                            tpb[:, kc * 128:(kc + 1) * 128], XN[xn][:, kc * 128:(kc + 1) * 128], ident[:],
                            start=True, stop=True),
                             reads=[('XN', xn), 'ident'], writes=[PS(2 * tb), PS(2 * tb + 1)])

                def p1_HTevac(ci, j):
                    sc = tile_sc[(ci, j)]
                    hs = ci % 2
                    tb = 0
                    tpb = bank2[tb]
                    S.op('vector', lambda e, tpb=tpb, hs=hs, j=j: e.tensor_tensor(
                        out=HT[hs][:, :, j * 128:(j + 1) * 128],
                        in0=tpb[:, 0:1024].rearrange('p (a b) -> p a b', a=KC),
                        in1=gT[:, :].unsqueeze(2).to_broadcast([128, KC, 128]), op=ALU.mult),
                         reads=[PS(2 * tb), PS(2 * tb + 1), 'gT'], writes=[('HT', hs)])

                def p1_K(ci):
                    t0, nt = chunks[ci]
                    N = nt * 128
                    hs = ci % 2
                    for fc in range(KC):
                        pb = nb1()
                        for kc in range(KC):
                            S.op('tensor', lambda e, pb=pb, fc=fc, kc=kc, hs=hs, N=N: e.matmul(
                                banks[pb][:, 0:N], Wk[:, kc, fc * 128:(fc + 1) * 128], HT[hs][:, kc, 0:N],
                                start=(kc == 0), stop=(kc == KC - 1)),
                                 reads=[('HT', hs), ('Wk', fc // 2)], writes=[PS(pb)])
                        evac(ev1(), lambda fc=fc, hs=hs, N=N: KTs[hs][:, fc, 0:N], lambda pb=pb, N=N: banks[pb][:, 0:N],
                             [PS(pb)], [('KTs', hs)])
                    for j in range(nt):
                        t = t0 + j
                        S.op('gpsimd', lambda e, hs=hs, t=t, j=j: e.dma_start(out=kT_v(t), in_=KTs[hs][:, :, j * 128:(j + 1) * 128]),
                             reads=[('KTs', hs)], writes=[('kT_d', t)], dma=('sK', hs))

                def p1_VU(ci, j):
                    t0, nt = chunks[ci]
                    N = nt * 128
                    hs = ci % 2
                    t = t0 + j
                    vs = st['vcnt'] % 4; st['vcnt'] += 1
                    for cg in range(4):
                        pb = nb1()
                        for kc in range(KC):
                            S.op('tensor', lambda e, pb=pb, cg=cg, kc=kc, hs=hs, j=j: e.matmul(
                                banks[pb][:, :], HT[hs][:, kc, j * 128:(j + 1) * 128], Wvu[:, kc, cg * 512:(cg + 1) * 512],
                                start=(kc == 0), stop=(kc == KC - 1)),
                                 reads=[('HT', hs), ('Wvu', cg)], writes=[PS(pb)])
                        if cg < 2:
                            evac(ev1(), lambda vs=vs, cg=cg: Vs[vs][:, cg * 8:(cg + 1) * 8, 0:64],
                                 lambda pb=pb: banks[pb][:, :].rearrange('p (a b) -> p a b', a=8), [PS(pb)], [('Vs', vs)])
                        else:
                            evac(ev1(), lambda vs=vs, cg=cg: Us[vs][:, (cg - 2) * 512:(cg - 1) * 512],
                                 lambda pb=pb: banks[pb][:, :], [PS(pb)], [('Us', vs)])
                    S.op('gpsimd', lambda e, vs=vs, t=t: e.dma_start(out=v_d[t * 128:(t + 1) * 128, :],
                                                                   in_=Vs[vs][:].rearrange('p a b -> p (a b)')),
                         reads=[('Vs', vs)], writes=[('v_d', t)], dma=('sV', vs))
                    S.op('gpsimd', lambda e, vs=vs, t=t: e.dma_start(out=u_d[t * 128:(t + 1) * 128, :], in_=Us[vs][:]),
                         reads=[('Us', vs)], writes=[('u_d', t)], dma=('sU', vs))
                    if j == nt - 1:
                        S.op('gpsimd', lambda e, hs=hs, t0=t0, nt=nt, N=N: e.dma_start(out=hT_v(t0, nt), in_=HT[hs][:, :, 0:N]),
                             reads=[('HT', hs)], writes=[('hT_d', t0)], dma=('sH', hs))

                msk = sb(es, 'msk', [128, 64], F32)
                mbs = sb(es, 'mbs', [80, NH], F32)
                mbe = sb(es, 'mbe', [80, NH], F32)
                EBf = sb(es, 'EBf', [128, 4, 15, 64], F32)

                def p1_tables(hg):
                    if hg == 0:
                        S.op('sync', lambda e: e.dma_start(out=msk[0:64, :], in_=mask_d), writes=['msk'], dma='c2')
                        S.op('sync', lambda e: e.dma_start(out=msk[64:128, :], in_=mask_d), writes=['msk'], dma='c2')
                        S.op('sync', lambda e: e.dma_start(out=mbs[0:16, :], in_=mbT_d), writes=['mbs'], dma='c3')
                        S.op('sync', lambda e: e.dma_start(out=mbs[64:80, :], in_=mbT_d), writes=['mbs'], dma='c3')
                    S.op('sync', lambda e, hg=hg: e.dma_start(out=EBf[0:64], in_=b0_d[:, hg * 4:(hg + 1) * 4, 1:16, :]),
                         writes=['EBf'], dma='c4')
                    S.op('sync', lambda e, hg=hg: e.dma_start(out=EBf[64:128], in_=b0_d[:, hg * 4:(hg + 1) * 4, 0:15, :]),
                         writes=['EBf'], dma='c4')
                    S.op('scalar', lambda e: e.activation(out=EBf[:], in_=EBf[:], func=AF.Exp), reads=['EBf'], writes=['EBf'])
                    S.op('vector', lambda e, hg=hg: e.tensor_tensor(
                        out=EB[:, hg * 4:(hg + 1) * 4, :, :].rearrange('p a b c -> p (a b) c'),
                        in0=EBf[:].rearrange('p a b c -> p (a b) c'),
                        in1=msk[:, :].unsqueeze(1).to_broadcast([128, 60, 64]), op=ALU.mult),
                         reads=['EBf', 'msk'], writes=['EB'])
                    if hg == 3:
                        S.op('scalar', lambda e: e.activation(out=mbe[0:16, :], in_=mbs[0:16, :], func=AF.Exp), reads=['mbs'], writes=['mbe'])
                        S.op('scalar', lambda e: e.activation(out=mbe[64:80, :], in_=mbs[64:80, :], func=AF.Exp), reads=['mbs'], writes=['mbe'])
                        S.op('vector', lambda e: e.tensor_copy(E_m2[:].rearrange('p h (a b) -> p h a b', a=2), EB[:, :, 11:13, :]),
                             reads=['EB'], writes=['E_m2'])
                        S.op('vector', lambda e: e.memset(E_m2[0:64, :, 64:128], 0.0), writes=['E_m2'])
                        S.op('vector', lambda e: e.memset(E5[0:64, :, 0:64], 0.0), writes=['E5'])
                        S.op('vector', lambda e: e.tensor_copy(E5[0:64, :, 64:128], EB[0:64, :, 4, :]), reads=['EB'], writes=['E5'])
                        S.op('vector', lambda e: e.tensor_copy(E5[64:80, :, :], mbe[64:80, :].unsqueeze(2).to_broadcast([16, NH, 128])),
                             reads=['mbe'], writes=['E5'])
                        S.op('vector', lambda e: e.tensor_copy(EM[0:16, :, :], mbe[0:16, :].unsqueeze(2).to_broadcast([16, NH, 128])),
                             reads=['mbe'], writes=['EM'])
                        for cb in range(4):
                            S.op('gpsimd', lambda e, cb=cb: e.dma_start(out=Wq[:, :, cb * 256:(cb + 1) * 256],
                                                                       in_=w_in_v[:, :, 2048 + cb * 256:2048 + (cb + 1) * 256]),
                                 writes=['Wq'], dma='wq')

                for j in range(chunks[0][1]):
                    p1_load(0, j)
                p1_weights([('X', tile_xs[(0, j)]) for j in range(chunks[0][1])])
                for j in range(chunks[0][1]):
                    p1_front(0, j)
                    p1_T(0, j)
                    p1_HTevac(0, j)
                for ci in range(len(chunks)):
                    ntc = chunks[ci][1]
                    ntn = chunks[ci + 1][1] if ci + 1 < len(chunks) else 0
                    if ntn > 0:
                        p1_front(ci + 1, 0)
                    p1_K(ci)
                    for j in range(max(ntc, ntn)):
                        if j < ntn:
                            p1_T(ci + 1, j)
                        if j + 1 < ntn:
                            p1_front(ci + 1, j + 1)
                        if j < ntn:
                            p1_HTevac(ci + 1, j)
                        if j < ntc:
                            p1_VU(ci, j)
                    if 1 <= ci <= 4:
                        p1_tables(ci - 1)
                S.flush()

            with ExitStack() as es:
                KM = sb(es, 'KM', [128, KC, 16], BF16)
                VM0 = sb(es, 'VM0', [16, VW], BF16)
                HT2 = [sb(es, 'HT2_%d' % i, [128, KC, 512], BF16) for i in range(2)]
                _t0, _nt = seq_chunks(0, TA)[0]
                S.op('sync', lambda e: e.dma_start(out=HT2[0][:, :, 0:_nt * 128], in_=hT_v(_t0, _nt)),
                     reads=[('hT_d', _t0)], writes=[('HT2', 0)], dma=('HT2', 0))
                S.op('sync', lambda e: e.dma_start(out=KM[:], in_=kT_v(MT)[:, :, 0:16]), writes=['KM'], dma='c5')
                S.op('sync', lambda e: e.dma_start(out=VM0[:], in_=v_d[MT * 128:MT * 128 + 16, :]), writes=['VM0'], dma='c5')
                RK = 8
                RX = 4
                KR = [sb(es, 'KR%d' % i, [128, KC, 128], BF16) for i in range(RK)]
                VR = [sb(es, 'VR%d' % i, [128, VW], BF16) for i in range(RK)]
                KX = [sb(es, 'KX%d' % i, [128, KC, 80], BF16) for i in range(RX)]
                VX = [sb(es, 'VX%d' % i, [80, VW], BF16) for i in range(RX)]
                for i in range(RX):
                    S.op('sync', lambda e, i=i: e.dma_start(out=KX[i][:, :, 64:80], in_=kT_v(MT)[:, :, 0:16]),
                         writes=[('KXm', i)], dma='c5')
                    S.op('sync', lambda e, i=i: e.dma_start(out=VX[i][64:80, :], in_=v_d[MT * 128:MT * 128 + 16, :]),
                         writes=[('VXm', i)], dma='c5')
                QT = [sb(es, 'QT%d' % i, [128, KC, 512], BF16) for i in range(2)]
                def p2_convert(cb):
                    if cb < 8:
                        src0 = 1024 if cb < 4 else 5120
                        S.op('gpsimd', lambda e, cb=cb, src0=src0: e.dma_start(
                            out=wg_s[:, :, cb * 256:(cb + 1) * 256],
                            in_=w_in_v[:, :, src0 + (cb % 4) * 256:src0 + (cb % 4 + 1) * 256]),
                             writes=[('wg_s', cb)], dma=('cvg', cb % 2))
                    else:
                        c2 = cb - 8
                        S.op('gpsimd', lambda e, c2=c2: e.dma_start(out=wo_s[:, c2 * 4:(c2 + 1) * 4, :], in_=w_out_v[:, c2 * 4:(c2 + 1) * 4, :]),
                             writes=[('wo_s', c2)], dma=('cvo', c2 % 2))
                PT = [sb(es, 'PT%d' % i, [128, 5, 2, 512], BF16) for i in range(2)]
                ATT = [sb(es, 'ATT%d' % i, [128, D], BF16) for i in range(2)]
                rec = sb(es, 'rec', [128, 8], F32)

                st2 = dict(qb=6, sbank=0, ev=0)
                seqs = [(0, TA), (TA, TB)]
                p2chunks = []
                for si, (s0, sn) in enumerate(seqs):
                    for (t0, nt) in seq_chunks(s0, sn):
                        p2chunks.append((si, t0, nt))
                loaded_k = [set(), set()]
                loaded_x = [set(), set()]

                def need_k(si, tl):
                    s0, sn = seqs[si]
                    if tl < 0 or tl >= sn or tl in loaded_k[si]:
                        return
                    loaded_k[si].add(tl)
                    t = s0 + tl
                    sl = t % RK
                    S.op('sync', lambda e, sl=sl, t=t: e.dma_start(out=KR[sl][:], in_=kT_v(t)),
                         reads=[('kT_d', t)], writes=[('KR', sl)], dma=('KR', sl))
                    S.op('sync', lambda e, sl=sl, t=t: e.dma_start(out=VR[sl][:], in_=v_d[t * 128:(t + 1) * 128, :]),
                         reads=[('v_d', t)], writes=[('VR', sl)], dma=('VR', sl))

                def need_x(si, tl):
                    s0, sn = seqs[si]
                    if tl < 0 or tl >= sn or tl in loaded_x[si]:
                        return
                    loaded_x[si].add(tl)
                    t = s0 + tl
                    sl = t % RX
                    S.op('sync', lambda e, sl=sl, t=t: e.dma_start(out=KX[sl][:, :, 0:64], in_=kT_v(t)[:, :, 0:64]),
                         reads=[('kT_d', t)], writes=[('KX', sl)], dma=('KX', sl))
                    S.op('sync', lambda e, sl=sl, t=t: e.dma_start(out=VX[sl][0:64, :], in_=v_d[t * 128:t * 128 + 64, :]),
                         reads=[('v_d', t)], writes=[('VX', sl)], dma=('VX', sl))

                def tile_blocks(si, tl):
                    s0, sn = seqs[si]
                    if tl == 0:
                        kts, deltas, special = [0, 1, 2, 3], [0, 1, 2, 3], 'M'
                    elif tl == 1:
                        kts, deltas, special = [0, 1, 2, 3], [-1, 0, 1, 2], 'M'
                    elif tl == sn - 2:
                        kts, deltas, special = [sn - 4, sn - 3, sn - 2, sn - 1], [-2, -1, 0, 1], 'M'
                    elif tl == sn - 1:
                        kts, deltas, special = [sn - 4, sn - 3, sn - 2, sn - 1], [-3, -2, -1, 0], 'M'
                    else:
                        kts, deltas, special = [tl - 2, tl - 1, tl, tl + 1], [-2, -1, 0, 1], 'X'
                    return kts, deltas, special

                def ensure_tile(si, tl):
                    s0, sn = seqs[si]
                    if tl < 0 or tl >= sn:
                        return
                    kts, deltas, special = tile_blocks(si, tl)
                    for kt in kts:
                        need_k(si, kt)
                    if special == 'X':
                        need_x(si, tl + 2)

                def p2_load(ci):
                    si, t0, nt = p2chunks[ci]
                    qs = ci % 2
                    N = nt * 128
                    S.op('sync', lambda e, qs=qs, t0=t0, nt=nt, N=N: e.dma_start(out=HT2[qs][:, :, 0:N], in_=hT_v(t0, nt)),
                         reads=[('hT_d', t0)], writes=[('HT2', qs)], dma=('HT2', qs))

                qstate = {}

                def p2_qmm(ci, fc, kcs):
                    si, t0, nt = p2chunks[ci]
                    qs = ci % 2
                    N = nt * 128
                    if (ci, fc) not in qstate:
                        pb = st2['qb']; st2['qb'] = 6 + (pb - 5) % 2
                        qstate[(ci, fc)] = pb
                    pb = qstate[(ci, fc)]
                    for kc in kcs:
                        S.op('tensor', lambda e, pb=pb, fc=fc, kc=kc, qs=qs, N=N: e.matmul(
                            banks[pb][:, 0:N], Wq[:, kc, fc * 128:(fc + 1) * 128], HT2[qs][:, kc, 0:N],
                            start=(kc == 0), stop=(kc == KC - 1)),
                             reads=[('HT2', qs), 'Wq'], writes=[PS(pb)])
                    if kcs and kcs[-1] == KC - 1:
                        evac('scalar' if fc % 2 == 0 else 'vector', lambda fc=fc, qs=qs, N=N: QT[qs][:, fc, 0:N],
                             lambda pb=pb, N=N: banks[pb][:, 0:N], [PS(pb)], [('QT', qs)])

                def p2_qpiece(ci, fc):
                    p2_qmm(ci, fc, list(range(KC)))

                units = []
                for ci, (si, t0, nt) in enumerate(p2chunks):
                    for j in range(nt):
                        for G in range(2):
                            units.append((ci, j, G))

                def unit_blocks(u):
                    ci, j, G = u
                    si, t0, nt = p2chunks[ci]
                    s0, sn = seqs[si]
                    tl = t0 - s0 + j
                    kts, deltas, special = tile_blocks(si, tl)
                    blocks = []
                    for kt, dl in zip(kts, deltas):
                        sl = (s0 + kt) % RK
                        std_m2 = (special == 'X' and dl == -2)
                        blocks.append(dict(nk=128, K=KR[sl], V=VR[sl], kres=[('KR', sl)], vres=[('VR', sl)],
                                           E=('m2' if std_m2 else 'eb'), i0=7 - 2 * dl))
                    if special == 'X':
                        sl = (s0 + tl + 2) % RX
                        blocks.append(dict(nk=80, K=KX[sl], V=VX[sl], kres=[('KX', sl), ('KXm', sl)],
                                           vres=[('VX', sl), ('VXm', sl)], E='e5', i0=0))
                    else:
                        blocks.append(dict(nk=16, K=KM, V=VM0, kres=['KM'], vres=['VM0'], E='em', i0=0))
                    return blocks, tl, si

                def p2_qk(ui, b):
                    u = units[ui]
                    ci, j, G = u
                    qs = ci % 2
                    ps = ui % 2
                    blocks, tl, si = unit_blocks(u)
                    if G == 0 and b == 0:
                        ensure_tile(si, tl)
                        ensure_tile(si, tl + 1)
                    blk = blocks[b]
                    nk = blk['nk']
                    sbk = st2['sbank']; st2['sbank'] = (sbk + 2) % 4
                    bl = [sbk, sbk + 1]
                    for jj in range(4):
                        fc = 4 * G + jj
                        for par in range(2):
                            S.op('tensor', lambda e, blk=blk, nk=nk, par=par, fc=fc, jj=jj, bk=bl[par], qs=qs, j=j: e.matmul(
                                banks[bk][0:nk, jj * 128:(jj + 1) * 128],
                                blk['K'][par * 64:(par + 1) * 64, fc, 0:nk],
                                QT[qs][par * 64:(par + 1) * 64, fc, j * 128:(j + 1) * 128],
                                start=True, stop=True),
                                 reads=blk['kres'] + [('QT', qs)], writes=[PS(bl[par])])
                    for par in range(2):
                        S.op('scalar', lambda e, nk=nk, b=b, par=par, ps=ps, bk=bl[par]: e.activation(
                            out=PT[ps][0:nk, b, par, :], in_=banks[bk][0:nk, :], func=AF.Exp, scale=0.125),
                             reads=[PS(bl[par])], writes=[('PT', ps, b, par)])
                        hsl = slice(par * 8 + 4 * G, par * 8 + 4 * G + 4)
                        if blk['E'] == 'eb':
                            i0 = blk['i0']
                            ein = lambda hsl=hsl, i0=i0: EB[:, hsl, i0:i0 + 2, :]
                            pin = lambda ps=ps, b=b, par=par: PT[ps][:, b, par, :].rearrange('p (a b c) -> p a b c', a=4, b=2)
                        elif blk['E'] == 'm2':
                            ein = lambda hsl=hsl: E_m2[:, hsl, :]
                            pin = lambda ps=ps, b=b, par=par: PT[ps][:, b, par, :].rearrange('p (a b) -> p a b', a=4)
                        elif blk['E'] == 'e5':
                            ein = lambda hsl=hsl: E5[0:80, hsl, :]
                            pin = lambda ps=ps, b=b, par=par: PT[ps][0:80, b, par, :].rearrange('p (a b) -> p a b', a=4)
                        else:
                            ein = lambda hsl=hsl: EM[0:16, hsl, :]
                            pin = lambda ps=ps, b=b, par=par: PT[ps][0:16, b, par, :].rearrange('p (a b) -> p a b', a=4)
                        S.op('vector', lambda e, ein=ein, pin=pin: e.tensor_tensor(out=pin(), in0=pin(), in1=ein(), op=ALU.mult),
                             reads=[('PT', ps, b, par)], writes=[('PT', ps, b, par)])

                def p2_pv(ui, par, jj):
                    u = units[ui]
                    ci, j, G = u
                    ps = ui % 2
                    blocks, tl, si = unit_blocks(u)
                    ab = 4 + par
                    fc = 4 * G + jj
                    h = 2 * fc + par
                    for b, blk in enumerate(blocks):
                        nk = blk['nk']
                        S.op('tensor', lambda e, blk=blk, nk=nk, b=b, par=par, jj=jj, h=h, ab=ab, ps=ps: e.matmul(
                            banks[ab][:, jj * 65:(jj + 1) * 65],
                            PT[ps][0:nk, b, par, jj * 128:(jj + 1) * 128],
                            blk['V'][0:nk, h * 65:(h + 1) * 65],
                            start=(b == 0), stop=(b == len(blocks) - 1)),
                             reads=blk['vres'] + [('PT', ps, b, par)], writes=[PS(ab)])

                def p2_norm(ui, par):
                    u = units[ui]
                    ci, j, G = u
                    si_, t0, nt = p2chunks[ci]
                    t = t0 + j
                    at = t % 2
                    ab = 4 + par
                    accv = lambda ab=ab: banks[ab][:, 0:260].rearrange('p (a b) -> p a b', a=4)
                    rs = slice(par * 4, par * 4 + 4)
                    S.op('vector', lambda e, accv=accv, rs=rs: e.reciprocal(rec[:, rs], accv()[:, :, 64]),
                         reads=[PS(ab)], writes=[('rec', par)])
                    outv = lambda at=at, G=G, par=par: ATT[at][:, :].rearrange('p (a b c) -> p a b c', a=8, b=2)[:, 4 * G:4 * G + 4, par, :]
                    S.op('vector', lambda e, accv=accv, outv=outv, rs=rs: e.tensor_tensor(
                        out=outv(), in0=accv()[:, :, 0:64],
                        in1=rec[:, rs].unsqueeze(2).to_broadcast([128, 4, 64]), op=ALU.mult),
                         reads=[PS(ab), ('rec', par)], writes=[('ATT', at)])
                    if G == 1 and par == 1:
                        S.op('gpsimd', lambda e, at=at, t=t: e.dma_start(out=at_d[t * 128:(t + 1) * 128, :], in_=ATT[at][:]),
                             reads=[('ATT', at)], writes=[('at_d', t)], dma=('sA', at))

                def p2_prev_slot(pu, b):
                    if pu < 0:
                        return
                    if b == 1:
                        p2_pv(pu, 0, 0); p2_pv(pu, 0, 1)
                    elif b == 2:
                        p2_pv(pu, 0, 2); p2_pv(pu, 0, 3); p2_norm(pu, 0)
                    elif b == 3:
                        p2_pv(pu, 1, 0); p2_pv(pu, 1, 1)
                    elif b == 4:
                        p2_pv(pu, 1, 2); p2_pv(pu, 1, 3); p2_norm(pu, 1)

                for fc in range(KC):
                    p2_qpiece(0, fc)
                if len(p2chunks) > 1:
                    p2_load(1)
                uic = {}
                for ui, u in enumerate(units):
                    ci, j, G = u
                    nt = p2chunks[ci][2]
                    k = uic.get(ci, 0); uic[ci] = k + 1
                    ppu = (KC + 2 * nt - 1) // (2 * nt)
                    if 4 <= ui < 16:
                        p2_convert(ui - 4)
                    QSPLIT = [[0, 1, 2], [3], [4, 5], [6], [7]]
                    for b in range(5):
                        p2_qk(ui, b)
                        if ci + 1 < len(p2chunks):
                            if ppu == 1:
                                if k < KC:
                                    p2_qmm(ci + 1, k, QSPLIT[b])
                            elif b == 0:
                                for fc in range(k * ppu, min(KC, (k + 1) * ppu)):
                                    p2_qpiece(ci + 1, fc)
                            if b == 0 and k == 2 * nt - 1 and ci + 2 < len(p2chunks):
                                p2_load(ci + 2)
                        if b > 0:
                            p2_prev_slot(ui - 1, b)
                for b in range(1, 5):
                    p2_prev_slot(len(units) - 1, b)
                S.flush()

        with ExitStack() as es:
            Wg = sb(es, 'Wg', [128, KC, 2048], BF16)
            Wo = sb(es, 'Wo', [128, 16, 1024], BF16)
            fgb = sb(es, 'fgb', [128, D], F32)
            S.op('sync', lambda e: e.dma_start(out=fgb[:], in_=fgb_d), writes=['fgb'], dma='c6b')

            HT3 = [sb(es, 'HT3_%d' % i, [128, KC, 512], BF16) for i in range(2)]
            UW = [sb(es, 'UW%d' % i, [128, 5, D], BF16) for i in range(2)]

            def windows(N):
                out = []
                o = 0
                while o < N:
                    out.append((o, min(112, N - o)))
                    o += 112
                return out
            AB = [sb(es, 'AB%d' % i, [128, 4, D], BF16) for i in range(2)]
            XR = [sb(es, 'XR%d' % i, [128, D], F32) for i in range(3)]
            PLT = sb(es, 'PLT', [128, KC, 512], BF16)
            MIX = sb(es, 'MIX', [128, 16, 512], BF16)
            TH = [sb(es, 'TH%d' % i, [128, 512], F32) for i in range(2)]
            SG = [sb(es, 'SG%d' % i, [128, 512], F32) for i in range(2)]
            Y = [sb(es, 'Y%d' % i, [128, D], F32) for i in range(2)]
            O = [sb(es, 'O%d' % i, [128, D], F32) for i in range(2)]
            junk3 = sb(es, 'junk3', [128, D], BF16)
            ssq3 = sb(es, 'ssq3', [128, 2], F32)
            sd3 = sb(es, 'sd3', [128, 2], F32)
            rs3 = sb(es, 'rs3', [128, 2], F32)

            st3 = dict(pbk=0, tcount=0, xrc=0, ycount=0)

            def nextbank():
                b = st3['pbk']; st3['pbk'] = (b + 1) % 8
                return b

            p3chunks = []
            for (s0, sn) in [(0, TA), (TA, TB)]:
                for (t0, nt) in seq_chunks(s0, sn):
                    p3chunks.append((s0, sn, t0, nt))

            def p3_load(ci):
                s0, sn, t0, nt = p3chunks[ci]
                N = nt * 128
                cs = ci % 2
                first = (t0 == s0)
                last = (t0 + nt == s0 + sn)
                S.op('sync', lambda e, cs=cs, t0=t0, nt=nt, N=N: e.dma_start(out=HT3[cs][:, :, 0:N], in_=hT_v(t0, nt)),
                     reads=[('hT_d', t0)], writes=[('HT3', cs)], dma=('HT3', cs))
                S0, S1 = s0 * 128, (s0 + sn) * 128
                for k, (o, n) in enumerate(windows(N)):
                    r0 = t0 * 128 + o - 8
                    if r0 < S0:
                        S.op('sync', lambda e, cs=cs, k=k: e.dma_start(out=UW[cs][0:8, k, :], in_=u_d[MT * 128 + 8:MT * 128 + 16, :]),
                             reads=[('u_d', MT)], writes=[('UWa', cs, k)], dma=('UW', cs))
                        S.op('sync', lambda e, cs=cs, k=k, S0=S0: e.dma_start(out=UW[cs][8:128, k, :], in_=u_d[S0:S0 + 120, :]),
                             reads=[('u_d', s0)], writes=[('UW', cs, k)], dma=('UW', cs))
                    else:
                        avail = 128
                        S.op('sync', lambda e, cs=cs, k=k, r0=r0, avail=avail: e.dma_start(
                            out=UW[cs][0:avail, k, :], in_=u_d[r0:r0 + avail, :]),
                             reads=[('u_d', tt) for tt in range(r0 // 128, (r0 + avail - 1) // 128 + 1)],
                             writes=[('UW', cs, k), ('UWa', cs, k)], dma=('UW', cs))
                S.op('sync', lambda e, cs=cs, t0=t0, nt=nt: e.dma_start(
                    out=AB[cs][:, 0:nt, :], in_=at_d[t0 * 128:(t0 + nt) * 128, :].rearrange('(a p) f -> p a f', p=128)),
                     reads=[('at_d', tt) for tt in range(t0, t0 + nt)], writes=[('AB', cs)], dma=('AB', cs))

            def p3_pooled(ci):
                s0, sn, t0, nt = p3chunks[ci]
                N = nt * 128
                cs = ci % 2
                last = (t0 + nt == s0 + sn)
                wins = windows(N)
                for cc in range(KC):
                    g = cc // 2
                    wdt = float(POOL_W[g])
                    pb = nextbank()
                    for k, (o, n) in enumerate(wins):
                        lastw = last and (k == len(wins) - 1)
                        bsrc = bandl[n] if lastw else bandw
                        S.op('tensor', lambda e, pb=pb, k=k, o=o, n=n, cs=cs, cc=cc, g=g, bsrc=bsrc: e.matmul(
                            banks[pb][:, o:o + n], UW[cs][:, k, cc * 128:(cc + 1) * 128], bsrc[:, g, 0:n],
                            start=True, stop=True), reads=[('UW', cs, k), ('UWa', cs, k), 'band'], writes=[PS(pb)])
                    o_l, n_l = wins[-1]
                    nint = o_l if last else N
                    if nint > 0:
                        S.op('vector', lambda e, pb=pb, cc=cc, nint=nint, wdt=wdt: e.tensor_scalar(
                            out=PLT[:, cc, 0:nint], in0=banks[pb][:, 0:nint], scalar1=1.0 / wdt, scalar2=None, op0=ALU.mult),
                             reads=[PS(pb)], writes=[('PLT', cc)])
                    if last:
                        S.op('vector', lambda e, pb=pb, cc=cc, g=g, o_l=o_l, n_l=n_l: e.tensor_tensor(
                            out=PLT[:, cc, o_l:o_l + n_l], in0=banks[pb][:, o_l:o_l + n_l],
                            in1=invl[n_l][:, g, :], op=ALU.mult), reads=[PS(pb), 'band'], writes=[('PLT', cc)])

            def p3_gates(ci):
                s0, sn, t0, nt = p3chunks[ci]
                N = nt * 128
                cs = ci % 2
                for dc in range(KC):
                    g = dc // 2
                    pg = nextbank()
                    for kc in range(KC):
                        S.op('tensor', lambda e, pg=pg, dc=dc, kc=kc, cs=cs, N=N: e.matmul(
                            banks[pg][:, 0:N], Wg[:, kc, dc * 128:(dc + 1) * 128], HT3[cs][:, kc, 0:N],
                            start=(kc == 0), stop=(kc == KC - 1)), reads=[('HT3', cs), ('Wg', dc // 2)], writes=[PS(pg)])
                    pm = nextbank()
                    for k2 in range(2):
                        S.op('tensor', lambda e, pm=pm, dc=dc, k2=k2, g=g, N=N: e.matmul(
                            banks[pm][:, 0:N], Wp[:, g * 2 + k2, (dc % 2) * 128:(dc % 2 + 1) * 128], PLT[:, g * 2 + k2, 0:N],
                            start=(k2 == 0), stop=(k2 == 1)), reads=[('PLT', g * 2 + k2), 'Wp'], writes=[PS(pm)])
                    ts = st3['tcount'] % 2; st3['tcount'] += 1
                    S.op('scalar', lambda e, ts=ts, pg=pg, N=N: e.activation(out=TH[ts][:, 0:N], in_=banks[pg][:, 0:N],
                                                                            func=AF.Tanh, scale=0.5),
                         reads=[PS(pg)], writes=[('TH', ts)])
                    S.op('vector', lambda e, ts=ts, pg=pg, N=N: e.scalar_tensor_tensor(
                        out=SG[ts][:, 0:N], in0=TH[ts][:, 0:N], scalar=1.0, in1=banks[pg][:, 0:N],
                        op0=ALU.add, op1=ALU.mult), reads=[('TH', ts), PS(pg)], writes=[('SG', ts)])
                    S.op('vector', lambda e, ts=ts, pm=pm, dc=dc, N=N: e.scalar_tensor_tensor(
                        out=MIX[:, dc, 0:N], in0=banks[pm][:, 0:N], scalar=psc[:, dc:dc + 1], in1=SG[ts][:, 0:N],
                        op0=ALU.mult, op1=ALU.mult), reads=[('SG', ts), PS(pm), 'psc'], writes=[('MIX', dc)])
                for fc in range(KC):
                    pg = nextbank()
                    for kc in range(KC):
                        S.op('tensor', lambda e, pg=pg, fc=fc, kc=kc, cs=cs, N=N: e.matmul(
                            banks[pg][:, 0:N], Wg[:, kc, 1024 + fc * 128:1024 + (fc + 1) * 128], HT3[cs][:, kc, 0:N],
                            start=(kc == 0), stop=(kc == KC - 1)), reads=[('HT3', cs), ('Wg', 4 + fc // 2)], writes=[PS(pg)])
                    pa = nextbank()
                    pab = banks[pa]
                    for j in range(nt):
                        S.op('tensor', lambda e, pab=pab, j=j, fc=fc, cs=cs: e.matmul(
                            pab[:, j * 128:(j + 1) * 128], AB[cs][:, j, fc * 128:(fc + 1) * 128], ident[:],
                            start=True, stop=True),
                             reads=[('AB', cs), 'ident'], writes=[PS(pa)])
                    ts = st3['tcount'] % 2; st3['tcount'] += 1
                    S.op('scalar', lambda e, ts=ts, pg=pg, N=N: e.activation(out=TH[ts][:, 0:N], in_=banks[pg][:, 0:N],
                                                                            func=AF.Tanh, scale=0.5),
                         reads=[PS(pg)], writes=[('TH', ts)])
                    S.op('vector', lambda e, ts=ts, pg=pg, N=N: e.scalar_tensor_tensor(
                        out=SG[ts][:, 0:N], in0=TH[ts][:, 0:N], scalar=1.0, in1=banks[pg][:, 0:N],
                        op0=ALU.add, op1=ALU.mult), reads=[('TH', ts), PS(pg)], writes=[('SG', ts)])
                    S.op('vector', lambda e, ts=ts, pab=pab, fc=fc, N=N: e.tensor_tensor(
                        out=MIX[:, 8 + fc, 0:N], in0=pab[:, 0:N], in1=SG[ts][:, 0:N], op=ALU.mult),
                         reads=[('SG', ts), PS(pa)], writes=[('MIX', 8 + fc)])
            def p3_out(ci):
                s0, sn, t0, nt = p3chunks[ci]
                for j in range(nt):
                    t = t0 + j
                    xr = st3['xrc'] % 3; st3['xrc'] += 1
                    S.op('sync', lambda e, xr=xr, t=t: e.dma_start(out=XR[xr][:], in_=xall[t * 128:(t + 1) * 128, :]),
                         writes=[('XR', xr)], dma=('XR', xr))
                    ys = st3['ycount'] % 2; st3['ycount'] += 1
                    for cg in range(2):
                        po = nextbank()
                        for kc in range(16):
                            S.op('tensor', lambda e, po=po, kc=kc, cg=cg, j=j: e.matmul(
                                banks[po][:, :], MIX[:, kc, j * 128:(j + 1) * 128], Wo[:, kc, cg * 512:(cg + 1) * 512],
                                start=(kc == 0), stop=(kc == 15)), reads=[('MIX', kc), ('Wo', kc // 4)], writes=[PS(po)])
                        S.op('vector', lambda e, po=po, cg=cg, ys=ys, xr=xr: e.tensor_tensor(
                            out=Y[ys][:, cg * 512:(cg + 1) * 512], in0=banks[po][:, :], in1=XR[xr][:, cg * 512:(cg + 1) * 512],
                            op=ALU.add), reads=[PS(po), ('XR', xr)], writes=[('Y', ys)])
                    S.op('scalar', lambda e, ys=ys: e.activation(out=junk3[:], in_=Y[ys][:], func=AF.Square,
                                                                accum_out=ssq3[:, ys:ys + 1]),
                         reads=[('Y', ys)], writes=['junk3', ('ssq3', ys)])
                    S.op('scalar', lambda e, ys=ys: e.activation(out=sd3[:, ys:ys + 1], in_=ssq3[:, ys:ys + 1], func=AF.Sqrt,
                                                                bias=epsb[:], scale=1.0 / D),
                         reads=[('ssq3', ys), 'epsb'], writes=[('sd3', ys)])
                    S.op('vector', lambda e, ys=ys: e.reciprocal(rs3[:, ys:ys + 1], sd3[:, ys:ys + 1]),
                         reads=[('sd3', ys)], writes=[('rs3', ys)])
                    S.op('vector', lambda e, ys=ys: e.scalar_tensor_tensor(
                        out=O[ys][:], in0=Y[ys][:], scalar=rs3[:, ys:ys + 1], in1=fgb[:], op0=ALU.mult, op1=ALU.mult),
                         reads=[('Y', ys), ('rs3', ys), 'fgb'], writes=[('O', ys)])
                    S.op('gpsimd', lambda e, ys=ys, t=t: e.dma_start(out=y_d[t * 128:(t + 1) * 128, :], in_=O[ys][:]),
                         reads=[('O', ys)], writes=[('y_d', t)], dma=('sO', ys))

            p3_load(0)
            for cb in range(8):
                S.op('sync', lambda e, cb=cb: e.dma_start(out=Wg[:, :, cb * 256:(cb + 1) * 256], in_=wg_s[:, :, cb * 256:(cb + 1) * 256]),
                     writes=[('Wg', cb)], dma=('wg', cb))
            for cb in range(4):
                S.op('sync', lambda e, cb=cb: e.dma_start(out=Wo[:, cb * 4:(cb + 1) * 4, :], in_=wo_s[:, cb * 4:(cb + 1) * 4, :]),
                     writes=[('Wo', cb)], dma=('wo', cb))
            p3_pooled(0)
            for ci in range(len(p3chunks)):
                if ci + 1 < len(p3chunks):
                    p3_load(ci + 1)
                p3_gates(ci)
                if ci + 1 < len(p3chunks):
                    p3_pooled(ci + 1)
                p3_out(ci)
            S.flush()
    return nc


def _constants():
    ident = np.eye(128, dtype=np.float32)
    consts = dict(ident=ident)
    tp = np.arange(128)[:, None] - 8
    bandw = np.zeros((128, 4, 112), np.float32)
    for g, w in enumerate(POOL_W):
        t = np.arange(112)[None, :]
        inwin = (tp >= t - w // 2) & (tp <= t + w // 2 - 1)
        bandw[:, g, :] = inwin.astype(np.float32) - w * (tp == t)
    consts['bandw'] = bandw
    for n in (64, 32):
        bl = np.zeros((128, 4, n), np.float32)
        il = np.zeros((128, 4, n), np.float32)
        for g, w in enumerate(POOL_W):
            t = np.arange(n)[None, :]
            inwin = (tp >= t - w // 2) & (tp <= t + w // 2 - 1) & (tp <= n - 1)
            cnt = np.minimum(t + w // 2 - 1, n - 1) - (t - w // 2) + 1
            bl[:, g, :] = inwin.astype(np.float32) - cnt.astype(np.float32) * (tp == t)
            il[:, g, :] = 1.0 / cnt.astype(np.float32)
        consts['bandl%d' % n] = bl
        consts['invl%d' % n] = il
    cols = np.arange(64)
    cstart = np.clip(cols - 8, 0, 48)
    kc = np.arange(64)
    consts['mask01'] = ((kc[:, None] >= cstart[None, :]) & (kc[:, None] < cstart[None, :] + 16)).astype(np.float32)
    return consts


_CACHE = {}


def kernel(x_prompt, x_sample, meta_tokens, norm_g, w_in, w_pool, pool_scale, rpb, meta_bias, w_out, final_g):
    f = lambda a: np.ascontiguousarray(np.asarray(a, dtype=np.float32))
    x_prompt, x_sample, meta_tokens = f(x_prompt), f(x_sample), f(meta_tokens)
    norm_g, w_in, w_pool, pool_scale = f(norm_g)[0], f(w_in)[0], f(w_pool)[0], f(pool_scale)[0]
    rpb, meta_bias, w_out, final_g = f(rpb)[0], f(meta_bias)[0], f(w_out)[0], f(final_g)
    if 'nc' not in _CACHE:
        _CACHE['nc'] = build_program()
        _CACHE['const'] = _constants()
    nc = _CACHE['nc']
    C = _CACHE['const']
    hperm = np.array([2 * (hp % 8) + hp // 8 for hp in range(NH)])
    kc = np.arange(64)[:, None]
    qc = np.arange(64)[None, :]
    cidx = np.clip(kc - qc + 15, 0, 30)
    ridx = np.clip(14 - (np.arange(16) - 1), 0, 14)
    b0 = rpb[hperm][:, ridx][:, :, cidx]
    b0 = np.ascontiguousarray(b0.transpose(2, 0, 1, 3))
    mbT = np.ascontiguousarray(meta_bias[hperm].T)
    gT = np.ascontiguousarray(norm_g.reshape(KC, 128).T)
    psT = np.ascontiguousarray(pool_scale.reshape(KC, 128).T)
    fgb = np.ascontiguousarray(np.broadcast_to(final_g[None, :], (128, D)))
    metax = np.zeros((128, D), np.float32)
    metax[:16] = meta_tokens
    in_maps = []
    for i in range(NCORES):
        half = i % 2
        tok0 = 0 if half == 0 else 30 * 128
        xb = x_sample[i // 2][tok0:tok0 + TB * 128]
        xall = np.concatenate([x_prompt[i], xb, metax], axis=0)
        m = dict(xall=xall, w_in=w_in, w_pool=w_pool, w_out=w_out, gT=gT, psT=psT, fgb=fgb, mbT=mbT, b0=b0)
        m.update(C)
        in_maps.append(m)
    res = run_bass_kernel_spmd(nc, in_maps, core_ids=list(range(NCORES)))
    y_prompt = np.empty((8, 2048, D), np.float32)
    y_sample = np.empty((4, 8192, D), np.float32)
    for i in range(NCORES):
        y = res.results[i]['y']
        y_prompt[i] = y[:TA * 128]
        yb = y[TA * 128:]
        if i % 2 == 0:
            y_sample[i // 2][:4096] = yb[:4096]
        else:
            y_sample[i // 2][4096:] = yb[2 * 128:]
    return (y_prompt, y_sample)
```

```python
import re
import numpy as np
from contextlib import ExitStack
import concourse.bass as bass
import concourse.mybir as mybir
from concourse.bass_utils import run_bass_kernel_spmd

F32 = mybir.dt.float32
BF16 = mybir.dt.bfloat16
AF = mybir.ActivationFunctionType
ALU = mybir.AluOpType

NCORES = 8
D = 1024
KC = 8
NH = 16
HD = 64
VW = NH * 65
TA = 16
TB = 34
NT = TA + TB + 1
MT = TA + TB
EPS = 1e-6
POOL_W = (2, 4, 8, 16)
ENGS = ['sync', 'gpsimd', 'scalar', 'vector', 'tensor']


class Op:
    __slots__ = ('eng', 'fn', 'deps', 'needed', 'val', 'is_dma', 'sem', 'idx', 'ent')


class Sched:
    def __init__(self, nc, es):
        self.nc = nc
        self.esem = {e: es.enter_context(nc.semaphore('s_' + e)) for e in ENGS}
        self.ecnt = {e: 0 for e in ENGS}
        self.dsem = {}
        self.es = es
        self.reset()

    def reset(self):
        self.q = {e: [] for e in ENGS}
        self.lastw = {}
        self.readers = {}
        self.n = 0

    def _tok(self, o):
        return o.sem if o.is_dma else o.eng

    def op(self, eng, fn, reads=(), writes=(), dma=None):
        o = Op()
        o.eng = eng; o.fn = fn; o.needed = False; o.val = None; o.idx = self.n
        self.n += 1
        o.is_dma = dma is not None
        o.sem = None
        if o.is_dma:
            if dma not in self.dsem:
                self.dsem[dma] = [self.es.enter_context(self.nc.semaphore('d_' + re.sub('[^0-9a-zA-Z]+', '_', str(dma)))), 0]
            ent = self.dsem[dma]
            o.sem = ent[0]; o.ent = ent
        deps = {}

        def add(d):
            if d is None:
                return
            if (not d.is_dma) and (not o.is_dma) and d.eng == 'tensor' and eng == 'tensor':
                return
            k = id(d.sem) if d.is_dma else d.eng
            cur = deps.get(k)
            if cur is None or cur.idx < d.idx:
                deps[k] = d
        for r in reads:
            add(self.lastw.get(r))
        for r in writes:
            add(self.lastw.get(r))
            for d in self.readers.get(r, {}).values():
                add(d)
        o.deps = [(d, (d.ent[1] if d.is_dma else None)) for d in deps.values()]
        if o.is_dma:
            o.ent[1] += 16
            o.val = o.ent[1]
        for d, _ in o.deps:
            d.needed = True
        for r in reads:
            self.readers.setdefault(r, {})[(id(o.sem) if o.is_dma else eng)] = o
        for r in writes:
            self.lastw[r] = o
            self.readers[r] = {}
        self.q[eng].append(o)
        return o

    def flush(self):
        nc = self.nc
        for e in ENGS:
            for o in self.q[e]:
                if (not o.is_dma) and o.needed:
                    self.ecnt[e] += 1
                    o.val = self.ecnt[e]
        final_dma = [(ent[0], ent[1]) for ent in self.dsem.values() if ent[1] > 0]
        with nc.Block(no_gpsimd_drain=True) as block:
            for e in ENGS:
                ops = self.q[e]

                def body(eng, ops=ops, e=e):
                    waited = {}
                    for o in ops:
                        for d, dv in o.deps:
                            sem = d.sem if d.is_dma else self.esem[d.eng]
                            val = dv if d.is_dma else d.val
                            if waited.get(id(sem), 0) >= val:
                                continue
                            eng.wait_ge(sem, val)
                            waited[id(sem)] = val
                        ins = o.fn(eng)
                        if o.is_dma:
                            ins.then_inc(o.sem, 16)
                        elif o.needed:
                            ins.then_inc(self.esem[e], 1)
                    if e == 'sync':
                        for sem, val in final_dma:
                            if waited.get(id(sem), 0) < val:
                                eng.wait_ge(sem, val)
                getattr(block, e)(body)
        self.reset()


def seq_chunks(t0, ntiles):
    out = []
    t = 0
    while t < ntiles:
        n = min(4, ntiles - t)
        out.append((t0 + t, n))
        t += n
    return out


def build_program():
    nc = bass.Bass('TRN2', target_bir_lowering=False)
    dt = lambda name, shape, dtype, kind: nc.dram_tensor(name, shape, dtype, kind=kind).ap()
    xall = dt('xall', [NT * 128, D], F32, 'ExternalInput')
    w_in = dt('w_in', [D, 6144], F32, 'ExternalInput')
    w_pool = dt('w_pool', [4, 256, 256], F32, 'ExternalInput')
    w_out = dt('w_out', [2048, D], F32, 'ExternalInput')
    gT_d = dt('gT', [128, KC], F32, 'ExternalInput')
    psT_d = dt('psT', [128, KC], F32, 'ExternalInput')
    fgb_d = dt('fgb', [128, D], F32, 'ExternalInput')
    mbT_d = dt('mbT', [16, NH], F32, 'ExternalInput')
    b0_d = dt('b0', [64, NH, 16, 64], F32, 'ExternalInput')
    mask_d = dt('mask01', [64, 64], F32, 'ExternalInput')
    ident_d = dt('ident', [128, 128], F32, 'ExternalInput')
    bandw_d = dt('bandw', [128, 4, 112], F32, 'ExternalInput')
    bandl64_d = dt('bandl64', [128, 4, 64], F32, 'ExternalInput')
    bandl32_d = dt('bandl32', [128, 4, 32], F32, 'ExternalInput')
    invl64_d = dt('invl64', [128, 4, 64], F32, 'ExternalInput')
    invl32_d = dt('invl32', [128, 4, 32], F32, 'ExternalInput')
    y_d = dt('y', [(TA + TB) * 128, D], F32, 'ExternalOutput')
    hT_d = dt('hT_s', [NT * 128 * D], BF16, 'Internal')
    kT_d = dt('kT_s', [NT * 128, D], BF16, 'Internal')
    v_d = dt('v_s', [NT * 128, VW], BF16, 'Internal')
    u_d = dt('u_s', [NT * 128, D], BF16, 'Internal')
    wg_s = dt('wg_s', [128, KC, 2048], BF16, 'Internal')
    wo_s = dt('wo_s', [128, 16, D], BF16, 'Internal')
    at_d = dt('at_s', [NT * 128, D], BF16, 'Internal')
    w_in_v = w_in.rearrange('(kc p) n -> p kc n', p=128)
    w_out_v = w_out.rearrange('(kc p) n -> p kc n', p=128)
    w_pool_v = w_pool.rearrange('g (k p) d -> p (g k) d', p=128)
    kT_v = lambda t: kT_d[t * 128:(t + 1) * 128, :].rearrange('p (a b) -> p a b', a=KC)
    hT_v = lambda t0, nt: hT_d[t0 * 128 * D:(t0 + nt) * 128 * D].rearrange('(p k n) -> p k n', p=128, k=KC)

    with ExitStack() as top:
        S = Sched(nc, top)
        bank2 = [top.enter_context(nc.psum_tensor('bankpair%d' % b, [128, 1024], F32)) for b in range(4)]
        banks = []
        for b in range(4):
            banks.append(bank2[b][:, 0:512])
            banks.append(bank2[b][:, 512:1024])
        PS = lambda b: ('ps', b)
        sb = lambda es, name, shape, dtype: es.enter_context(nc.sbuf_tensor('sb_' + name, shape, dtype))

        ident = sb(top, 'ident', [128, 128], BF16)
        gT = sb(top, 'gTs', [128, KC], F32)
        epsb = sb(top, 'epsb', [128, 1], F32)
        Wp = sb(top, 'Wp', [128, 8, 256], BF16)
        bandw = sb(top, 'bandw', [128, 4, 112], BF16)
        bandl = {64: sb(top, 'bandl64', [128, 4, 64], BF16), 32: sb(top, 'bandl32', [128, 4, 32], BF16)}
        invl = {64: sb(top, 'invl64', [128, 4, 64], F32), 32: sb(top, 'invl32', [128, 4, 32], F32)}
        psc = sb(top, 'psc', [128, KC], F32)

        def evac(eng, dst, src, reads, writes):
            if eng == 'vector':
                S.op('vector', lambda e: e.tensor_copy(dst(), src()), reads=reads, writes=writes)
            else:
                S.op('scalar', lambda e: e.copy(dst(), src()), reads=reads, writes=writes)

        with ExitStack() as es12:
            Wq = sb(es12, 'Wq', [128, KC, 1024], BF16)
            EB = sb(es12, 'EB', [128, NH, 15, 64], BF16)
            E_m2 = sb(es12, 'E_m2', [128, NH, 128], BF16)
            E5 = sb(es12, 'E5', [80, NH, 128], BF16)
            EM = sb(es12, 'EM', [16, NH, 128], BF16)

            with ExitStack() as es:
                ctmp = sb(es, 'ctmp', [128, 128], F32)
                S.op('sync', lambda e: e.dma_start(out=ctmp[:], in_=ident_d), writes=['ctmp'], dma='c0')
                S.op('vector', lambda e: e.tensor_copy(ident[:], ctmp[:]), reads=['ctmp'], writes=['ident'])
                S.op('sync', lambda e: e.dma_start(out=gT[:], in_=gT_d), writes=['gT'], dma='c1')
                S.op('vector', lambda e: e.memset(epsb[:], EPS), writes=['epsb'])

                Wk = sb(es, 'Wk', [128, KC, 1024], BF16)
                Wvu = sb(es, 'Wvu', [128, KC, 2048], BF16)
                def p1_weights(xdeps):
                    for cb in range(4):
                        S.op('gpsimd', lambda e, cb=cb: e.dma_start(out=Wk[:, :, cb * 256:(cb + 1) * 256],
                                                                   in_=w_in_v[:, :, 3072 + cb * 256:3072 + (cb + 1) * 256]),
                             reads=(xdeps if cb == 0 else []), writes=[('Wk', cb)], dma=('wk', cb))
                    for cb in range(4):
                        src0 = 4096 if cb < 2 else 0
                        S.op('gpsimd', lambda e, cb=cb, src0=src0: e.dma_start(
                            out=Wvu[:, :, cb * 512:(cb + 1) * 512],
                            in_=w_in_v[:, :, src0 + (cb % 2) * 512:src0 + (cb % 2 + 1) * 512]),
                             writes=[('Wvu', cb)], dma=('wvu', cb))
                    S.op('gpsimd', lambda e: e.dma_start(out=bandw[:], in_=bandw_d), writes=['band'], dma='c7')
                    S.op('gpsimd', lambda e: e.dma_start(out=bandl[64][:], in_=bandl64_d), writes=['band'], dma='c7')
                    S.op('gpsimd', lambda e: e.dma_start(out=bandl[32][:], in_=bandl32_d), writes=['band'], dma='c7')
                    S.op('gpsimd', lambda e: e.dma_start(out=Wp[:], in_=w_pool_v), writes=['Wp'], dma='wp')

                S.op('sync', lambda e: e.dma_start(out=psc[:], in_=psT_d), writes=['psc'], dma='c6')
                S.op('vector', lambda e: e.tensor_scalar(out=psc[:], in0=psc[:], scalar1=0.5, scalar2=None, op0=ALU.mult),
                     reads=['psc'], writes=['psc'])
                S.op('sync', lambda e: e.dma_start(out=invl[64][:], in_=invl64_d), writes=['band'], dma='c8')
                S.op('sync', lambda e: e.dma_start(out=invl[32][:], in_=invl32_d), writes=['band'], dma='c8')

                NX = 4
                X = [sb(es, 'X%d' % i, [128, D], F32) for i in range(NX)]
                XN = [sb(es, 'XN%d' % i, [128, D], BF16) for i in range(2)]
                junk = sb(es, 'junk', [128, D], BF16)
                ssq = sb(es, 'ssq', [128, 8], F32)
                sd = sb(es, 'sd', [128, 8], F32)
                rstd = sb(es, 'rstd', [128, 8], F32)
                HT = [sb(es, 'HT%d' % i, [128, KC, 512], BF16) for i in range(2)]
                KTs = [sb(es, 'KTs%d' % i, [128, KC, 512], BF16) for i in range(2)]
                Vs = [sb(es, 'Vs%d' % i, [128, NH, 65], BF16) for i in range(4)]
                Us = [sb(es, 'Us%d' % i, [128, D], BF16) for i in range(4)]
                for i in range(4):
                    S.op('vector', lambda e, i=i: e.memset(Vs[i][:, :, 64:65], 2.0), writes=[('Vs', i)])

                chunks = seq_chunks(0, TA) + seq_chunks(TA, TB) + [(MT, 1)]
                st = dict(xcnt=0, pbank=2, vcnt=0, ev=0, scnt=0)

                def nb1():
                    b = st['pbank']; st['pbank'] = 2 + (b - 1) % 6
                    return b

                def ev1():
                    st['ev'] += 1
                    return 'vector' if st['ev'] % 2 == 0 else 'scalar'

                tile_sc = {}

                tile_xs = {}

                def p1_load(ci, j):
                    t0, nt = chunks[ci]
                    t = t0 + j
                    xs = st['xcnt'] % NX; st['xcnt'] += 1
                    tile_xs[(ci, j)] = xs
                    S.op('sync', lambda e, xs=xs, t=t: e.dma_start(out=X[xs][:], in_=xall[t * 128:(t + 1) * 128, :]),
                         writes=[('X', xs)], dma=('X', xs))

                def p1_front(ci, j):
                    if (ci, j) not in tile_xs:
                        p1_load(ci, j)
                    xs = tile_xs[(ci, j)]
                    sc = st['scnt'] % 8; st['scnt'] += 1
                    tile_sc[(ci, j)] = sc
                    S.op('scalar', lambda e, xs=xs, sc=sc: e.activation(out=junk[:], in_=X[xs][:], func=AF.Square,
                                                                       accum_out=ssq[:, sc:sc + 1]),
                         reads=[('X', xs)], writes=['junk', ('ssq', sc)])
                    S.op('scalar', lambda e, sc=sc: e.activation(out=sd[:, sc:sc + 1], in_=ssq[:, sc:sc + 1], func=AF.Sqrt,
                                                                bias=epsb[:], scale=1.0 / D),
                         reads=[('ssq', sc), 'epsb'], writes=[('sd', sc)])
                    S.op('vector', lambda e, sc=sc: e.reciprocal(rstd[:, sc:sc + 1], sd[:, sc:sc + 1]),
                         reads=[('sd', sc)], writes=[('rstd', sc)])
                    xn = sc % 2
                    S.op('vector', lambda e, xs=xs, xn=xn, sc=sc: e.tensor_scalar(
                        out=XN[xn][:], in0=X[xs][:], scalar1=rstd[:, sc:sc + 1], scalar2=None, op0=ALU.mult),
                         reads=[('X', xs), ('rstd', sc)], writes=[('XN', xn)])

                def p1_T(ci, j):
                    sc = tile_sc[(ci, j)]
                    xn = sc % 2
                    tb = 0
                    tpb = bank2[tb]
                    for kc in range(KC):
                        S.op('tensor', lambda e, tpb=tpb, xn=xn, kc=kc: e.matmul(
                            tpb[:, kc * 128:(kc + 1) * 128], XN[xn][:, kc * 128:(kc + 1) * 128], ident[:],
                            start=True, stop=True),
                             reads=[('XN', xn), 'ident'], writes=[PS(2 * tb), PS(2 * tb + 1)])

                def p1_HTevac(ci, j):
                    sc = tile_sc[(ci, j)]
                    hs = ci % 2
                    tb = 0
                    tpb = bank2[tb]
                    S.op('vector', lambda e, tpb=tpb, hs=hs, j=j: e.tensor_tensor(
                        out=HT[hs][:, :, j * 128:(j + 1) * 128],
                        in0=tpb[:, 0:1024].rearrange('p (a b) -> p a b', a=KC),
                        in1=gT[:, :].unsqueeze(2).to_broadcast([128, KC, 128]), op=ALU.mult),
                         reads=[PS(2 * tb), PS(2 * tb + 1), 'gT'], writes=[('HT', hs)])

                def p1_K(ci):
                    t0, nt = chunks[ci]
                    N = nt * 128
                    hs = ci % 2
                    for fc in range(KC):
                        pb = nb1()
                        for kc in range(KC):
                            S.op('tensor', lambda e, pb=pb, fc=fc, kc=kc, hs=hs, N=N: e.matmul(
                                banks[pb][:, 0:N], Wk[:, kc, fc * 128:(fc + 1) * 128], HT[hs][:, kc, 0:N],
                                start=(kc == 0), stop=(kc == KC - 1)),
                                 reads=[('HT', hs), ('Wk', fc // 2)], writes=[PS(pb)])
                        evac(ev1(), lambda fc=fc, hs=hs, N=N: KTs[hs][:, fc, 0:N], lambda pb=pb, N=N: banks[pb][:, 0:N],
                             [PS(pb)], [('KTs', hs)])
                    for j in range(nt):
                        t = t0 + j
                        S.op('gpsimd', lambda e, hs=hs, t=t, j=j: e.dma_start(out=kT_v(t), in_=KTs[hs][:, :, j * 128:(j + 1) * 128]),
                             reads=[('KTs', hs)], writes=[('kT_d', t)], dma=('sK', hs))

                def p1_VU(ci, j):
                    t0, nt = chunks[ci]
                    N = nt * 128
                    hs = ci % 2
                    t = t0 + j
                    vs = st['vcnt'] % 4; st['vcnt'] += 1
                    for cg in range(4):
                        pb = nb1()
                        for kc in range(KC):
                            S.op('tensor', lambda e, pb=pb, cg=cg, kc=kc, hs=hs, j=j: e.matmul(
                                banks[pb][:, :], HT[hs][:, kc, j * 128:(j + 1) * 128], Wvu[:, kc, cg * 512:(cg + 1) * 512],
                                start=(kc == 0), stop=(kc == KC - 1)),
                                 reads=[('HT', hs), ('Wvu', cg)], writes=[PS(pb)])
                        if cg < 2:
                            evac(ev1(), lambda vs=vs, cg=cg: Vs[vs][:, cg * 8:(cg + 1) * 8, 0:64],
                                 lambda pb=pb: banks[pb][:, :].rearrange('p (a b) -> p a b', a=8), [PS(pb)], [('Vs', vs)])
                        else:
                            evac(ev1(), lambda vs=vs, cg=cg: Us[vs][:, (cg - 2) * 512:(cg - 1) * 512],
                                 lambda pb=pb: banks[pb][:, :], [PS(pb)], [('Us', vs)])
                    S.op('gpsimd', lambda e, vs=vs, t=t: e.dma_start(out=v_d[t * 128:(t + 1) * 128, :],
                                                                   in_=Vs[vs][:].rearrange('p a b -> p (a b)')),
                         reads=[('Vs', vs)], writes=[('v_d', t)], dma=('sV', vs))
                    S.op('gpsimd', lambda e, vs=vs, t=t: e.dma_start(out=u_d[t * 128:(t + 1) * 128, :], in_=Us[vs][:]),
                         reads=[('Us', vs)], writes=[('u_d', t)], dma=('sU', vs))
                    if j == nt - 1:
                        S.op('gpsimd', lambda e, hs=hs, t0=t0, nt=nt, N=N: e.dma_start(out=hT_v(t0, nt), in_=HT[hs][:, :, 0:N]),
                             reads=[('HT', hs)], writes=[('hT_d', t0)], dma=('sH', hs))

                msk = sb(es, 'msk', [128, 64], F32)
                mbs = sb(es, 'mbs', [80, NH], F32)
                mbe = sb(es, 'mbe', [80, NH], F32)
                EBf = sb(es, 'EBf', [128, 4, 15, 64], F32)

                def p1_tables(hg):
                    if hg == 0:
                        S.op('sync', lambda e: e.dma_start(out=msk[0:64, :], in_=mask_d), writes=['msk'], dma='c2')
                        S.op('sync', lambda e: e.dma_start(out=msk[64:128, :], in_=mask_d), writes=['msk'], dma='c2')
                        S.op('sync', lambda e: e.dma_start(out=mbs[0:16, :], in_=mbT_d), writes=['mbs'], dma='c3')
                        S.op('sync', lambda e: e.dma_start(out=mbs[64:80, :], in_=mbT_d), writes=['mbs'], dma='c3')
                    S.op('sync', lambda e, hg=hg: e.dma_start(out=EBf[0:64], in_=b0_d[:, hg * 4:(hg + 1) * 4, 1:16, :]),
                         writes=['EBf'], dma='c4')
                    S.op('sync', lambda e, hg=hg: e.dma_start(out=EBf[64:128], in_=b0_d[:, hg * 4:(hg + 1) * 4, 0:15, :]),
                         writes=['EBf'], dma='c4')
                    S.op('scalar', lambda e: e.activation(out=EBf[:], in_=EBf[:], func=AF.Exp), reads=['EBf'], writes=['EBf'])
                    S.op('vector', lambda e, hg=hg: e.tensor_tensor(
                        out=EB[:, hg * 4:(hg + 1) * 4, :, :].rearrange('p a b c -> p (a b) c'),
                        in0=EBf[:].rearrange('p a b c -> p (a b) c'),
                        in1=msk[:, :].unsqueeze(1).to_broadcast([128, 60, 64]), op=ALU.mult),
                         reads=['EBf', 'msk'], writes=['EB'])
                    if hg == 3:
                        S.op('scalar', lambda e: e.activation(out=mbe[0:16, :], in_=mbs[0:16, :], func=AF.Exp), reads=['mbs'], writes=['mbe'])
                        S.op('scalar', lambda e: e.activation(out=mbe[64:80, :], in_=mbs[64:80, :], func=AF.Exp), reads=['mbs'], writes=['mbe'])
                        S.op('vector', lambda e: e.tensor_copy(E_m2[:].rearrange('p h (a b) -> p h a b', a=2), EB[:, :, 11:13, :]),
                             reads=['EB'], writes=['E_m2'])
                        S.op('vector', lambda e: e.memset(E_m2[0:64, :, 64:128], 0.0), writes=['E_m2'])
                        S.op('vector', lambda e: e.memset(E5[0:64, :, 0:64], 0.0), writes=['E5'])
                        S.op('vector', lambda e: e.tensor_copy(E5[0:64, :, 64:128], EB[0:64, :, 4, :]), reads=['EB'], writes=['E5'])
                        S.op('vector', lambda e: e.tensor_copy(E5[64:80, :, :], mbe[64:80, :].unsqueeze(2).to_broadcast([16, NH, 128])),
                             reads=['mbe'], writes=['E5'])
                        S.op('vector', lambda e: e.tensor_copy(EM[0:16, :, :], mbe[0:16, :].unsqueeze(2).to_broadcast([16, NH, 128])),
                             reads=['mbe'], writes=['EM'])
                        for cb in range(4):
                            S.op('gpsimd', lambda e, cb=cb: e.dma_start(out=Wq[:, :, cb * 256:(cb + 1) * 256],
                                                                       in_=w_in_v[:, :, 2048 + cb * 256:2048 + (cb + 1) * 256]),
                                 writes=['Wq'], dma='wq')

                for j in range(chunks[0][1]):
                    p1_load(0, j)
                p1_weights([('X', tile_xs[(0, j)]) for j in range(chunks[0][1])])
                for j in range(chunks[0][1]):
                    p1_front(0, j)
                    p1_T(0, j)
                    p1_HTevac(0, j)
                for ci in range(len(chunks)):
                    ntc = chunks[ci][1]
                    ntn = chunks[ci + 1][1] if ci + 1 < len(chunks) else 0
                    if ntn > 0:
                        p1_front(ci + 1, 0)
                    p1_K(ci)
                    for j in range(max(ntc, ntn)):
                        if j < ntn:
                            p1_T(ci + 1, j)
                        if j + 1 < ntn:
                            p1_front(ci + 1, j + 1)
                        if j < ntn:
                            p1_HTevac(ci + 1, j)
                        if j < ntc:
                            p1_VU(ci, j)
                    if 1 <= ci <= 4:
                        p1_tables(ci - 1)
                S.flush()

            with ExitStack() as es:
                KM = sb(es, 'KM', [128, KC, 16], BF16)
                VM0 = sb(es, 'VM0', [16, VW], BF16)
                HT2 = [sb(es, 'HT2_%d' % i, [128, KC, 512], BF16) for i in range(2)]
                _t0, _nt = seq_chunks(0, TA)[0]
                S.op('sync', lambda e: e.dma_start(out=HT2[0][:, :, 0:_nt * 128], in_=hT_v(_t0, _nt)),
                     reads=[('hT_d', _t0)], writes=[('HT2', 0)], dma=('HT2', 0))
                S.op('sync', lambda e: e.dma_start(out=KM[:], in_=kT_v(MT)[:, :, 0:16]), writes=['KM'], dma='c5')
                S.op('sync', lambda e: e.dma_start(out=VM0[:], in_=v_d[MT * 128:MT * 128 + 16, :]), writes=['VM0'], dma='c5')
                KXZ = sb(es, 'KXZ', [128, KC, 80], BF16)
                VXZ = sb(es, 'VXZ', [80, VW], BF16)
                S.op('vector', lambda e: e.memset(KXZ[:, :, 0:64], 0.0), writes=['KXZa'])
                S.op('vector', lambda e: e.memset(VXZ[0:64, :], 0.0), writes=['VXZa'])
                S.op('sync', lambda e: e.dma_start(out=KXZ[:, :, 64:80], in_=kT_v(MT)[:, :, 0:16]), writes=['KXZb'], dma='c5')
                S.op('sync', lambda e: e.dma_start(out=VXZ[64:80, :], in_=v_d[MT * 128:MT * 128 + 16, :]), writes=['VXZb'], dma='c5')
                RK = 8
                RX = 4
                KR = [sb(es, 'KR%d' % i, [128, KC, 128], BF16) for i in range(RK)]
                VR = [sb(es, 'VR%d' % i, [128, VW], BF16) for i in range(RK)]
                KX = [sb(es, 'KX%d' % i, [128, KC, 80], BF16) for i in range(RX)]
                VX = [sb(es, 'VX%d' % i, [80, VW], BF16) for i in range(RX)]
                for i in range(RX):
                    S.op('sync', lambda e, i=i: e.dma_start(out=KX[i][:, :, 64:80], in_=kT_v(MT)[:, :, 0:16]),
                         writes=[('KXm', i)], dma='c5')
                    S.op('sync', lambda e, i=i: e.dma_start(out=VX[i][64:80, :], in_=v_d[MT * 128:MT * 128 + 16, :]),
                         writes=[('VXm', i)], dma='c5')
                QT = [sb(es, 'QT%d' % i, [128, KC, 512], BF16) for i in range(2)]
                def p2_convert(cb):
                    if cb < 8:
                        src0 = 1024 if cb < 4 else 5120
                        S.op('gpsimd', lambda e, cb=cb, src0=src0: e.dma_start(
                            out=wg_s[:, :, cb * 256:(cb + 1) * 256],
                            in_=w_in_v[:, :, src0 + (cb % 4) * 256:src0 + (cb % 4 + 1) * 256]),
                             writes=[('wg_s', cb)], dma=('cvg', cb % 2))
                    else:
                        c2 = cb - 8
                        S.op('gpsimd', lambda e, c2=c2: e.dma_start(out=wo_s[:, c2 * 4:(c2 + 1) * 4, :], in_=w_out_v[:, c2 * 4:(c2 + 1) * 4, :]),
                             writes=[('wo_s', c2)], dma=('cvo', c2 % 2))
                PT = [sb(es, 'PT%d' % i, [128, 5, 2, 512], BF16) for i in range(2)]
                ATT = [sb(es, 'ATT%d' % i, [128, D], BF16) for i in range(2)]
                rec = sb(es, 'rec', [128, 8], F32)

                st2 = dict(qb=6, sbank=0, ev=0)
                seqs = [(0, TA), (TA, TB)]
                p2chunks = []
                for si, (s0, sn) in enumerate(seqs):
                    for (t0, nt) in seq_chunks(s0, sn):
                        p2chunks.append((si, t0, nt))
                loaded_k = [set(), set()]
                loaded_x = [set(), set()]

                def need_k(si, tl):
                    s0, sn = seqs[si]
                    if tl < 0 or tl >= sn or tl in loaded_k[si]:
                        return
                    loaded_k[si].add(tl)
                    t = s0 + tl
                    sl = t % RK
                    S.op('sync', lambda e, sl=sl, t=t: e.dma_start(out=KR[sl][:], in_=kT_v(t)),
                         reads=[('kT_d', t)], writes=[('KR', sl)], dma=('KR', sl))
                    S.op('sync', lambda e, sl=sl, t=t: e.dma_start(out=VR[sl][:], in_=v_d[t * 128:(t + 1) * 128, :]),
                         reads=[('v_d', t)], writes=[('VR', sl)], dma=('VR', sl))

                def need_x(si, tl):
                    s0, sn = seqs[si]
                    if tl < 0 or tl >= sn or tl in loaded_x[si]:
                        return
                    loaded_x[si].add(tl)
                    t = s0 + tl
                    sl = t % RX
                    S.op('sync', lambda e, sl=sl, t=t: e.dma_start(out=KX[sl][:, :, 0:64], in_=kT_v(t)[:, :, 0:64]),
                         reads=[('kT_d', t)], writes=[('KX', sl)], dma=('KX', sl))
                    S.op('sync', lambda e, sl=sl, t=t: e.dma_start(out=VX[sl][0:64, :], in_=v_d[t * 128:t * 128 + 64, :]),
                         reads=[('v_d', t)], writes=[('VX', sl)], dma=('VX', sl))

                def tile_blocks(si, tl):
                    s0, sn = seqs[si]
                    if tl == 0:
                        kts, deltas, special = [0, 1, 2, 3], [0, 1, 2, 3], 'M'
                    elif tl == 1:
                        kts, deltas, special = [0, 1, 2, 3], [-1, 0, 1, 2], 'M'
                    elif tl == sn - 2:
                        kts, deltas, special = [sn - 4, sn - 3, sn - 2, sn - 1], [-2, -1, 0, 1], 'M'
                    elif tl == sn - 1:
                        kts, deltas, special = [sn - 4, sn - 3, sn - 2, sn - 1], [-3, -2, -1, 0], 'M'
                    else:
                        kts, deltas, special = [tl - 2, tl - 1, tl, tl + 1], [-2, -1, 0, 1], 'X'
                    return kts, deltas, special

                def ensure_tile(si, tl):
                    s0, sn = seqs[si]
                    if tl < 0 or tl >= sn:
                        return
                    kts, deltas, special = tile_blocks(si, tl)
                    for kt in kts:
                        need_k(si, kt)
                    if special == 'X':
                        need_x(si, tl + 2)

                def p2_load(ci):
                    si, t0, nt = p2chunks[ci]
                    qs = ci % 2
                    N = nt * 128
                    S.op('sync', lambda e, qs=qs, t0=t0, nt=nt, N=N: e.dma_start(out=HT2[qs][:, :, 0:N], in_=hT_v(t0, nt)),
                         reads=[('hT_d', t0)], writes=[('HT2', qs)], dma=('HT2', qs))

                qstate = {}

                def p2_qmm(ci, fc, kcs):
                    si, t0, nt = p2chunks[ci]
                    qs = ci % 2
                    N = nt * 128
                    if (ci, fc) not in qstate:
                        pb = st2['qb']; st2['qb'] = 6 + (pb - 5) % 2
                        qstate[(ci, fc)] = pb
                    pb = qstate[(ci, fc)]
                    for kc in kcs:
                        S.op('tensor', lambda e, pb=pb, fc=fc, kc=kc, qs=qs, N=N: e.matmul(
                            banks[pb][:, 0:N], Wq[:, kc, fc * 128:(fc + 1) * 128], HT2[qs][:, kc, 0:N],
                            start=(kc == 0), stop=(kc == KC - 1)),
                             reads=[('HT2', qs), 'Wq'], writes=[PS(pb)])
                    if kcs and kcs[-1] == KC - 1:
                        evac('scalar' if fc % 2 == 0 else 'vector', lambda fc=fc, qs=qs, N=N: QT[qs][:, fc, 0:N],
                             lambda pb=pb, N=N: banks[pb][:, 0:N], [PS(pb)], [('QT', qs)])

                def p2_qpiece(ci, fc):
                    p2_qmm(ci, fc, list(range(KC)))

                units = []
                for ci, (si, t0, nt) in enumerate(p2chunks):
                    for j in range(nt):
                        for G in range(2):
                            units.append((ci, j, G))

                def unit_blocks(u):
                    ci, j, G = u
                    si, t0, nt = p2chunks[ci]
                    s0, sn = seqs[si]
                    tl = t0 - s0 + j
                    kts, deltas, special = tile_blocks(si, tl)
                    blocks = []
                    for kt, dl in zip(kts, deltas):
                        sl = (s0 + kt) % RK
                        std_m2 = (special == 'X' and dl == -2)
                        blocks.append(dict(nk=128, K=KR[sl], V=VR[sl], kres=[('KR', sl)], vres=[('VR', sl)],
                                           E=('m2' if std_m2 else 'eb'), i0=7 - 2 * dl))
                    if special == 'X':
                        sl = (s0 + tl + 2) % RX
                        blocks.append(dict(nk=80, K=KX[sl], V=VX[sl], kres=[('KX', sl), ('KXm', sl)],
                                           vres=[('VX', sl), ('VXm', sl)], E='e5', i0=0))
                    else:
                        blocks.append(dict(nk=80, K=KXZ, V=VXZ, kres=['KXZa', 'KXZb'], vres=['VXZa', 'VXZb'], E='e5', i0=0))
                    return blocks, tl, si

                def p2_qk(ui, b):
                    u = units[ui]
                    ci, j, G = u
                    qs = ci % 2
                    ps = ui % 2
                    blocks, tl, si = unit_blocks(u)
                    if G == 0 and b == 0:
                        ensure_tile(si, tl)
                        ensure_tile(si, tl + 1)
                    blk = blocks[b]
                    nk = blk['nk']
                    sbk = st2['sbank']; st2['sbank'] = (sbk + 2) % 4
                    bl = [sbk, sbk + 1]
                    for jj in range(4):
                        fc = 4 * G + jj
                        for par in range(2):
                            S.op('tensor', lambda e, blk=blk, nk=nk, par=par, fc=fc, jj=jj, bk=bl[par], qs=qs, j=j: e.matmul(
                                banks[bk][0:nk, jj * 128:(jj + 1) * 128],
                                blk['K'][par * 64:(par + 1) * 64, fc, 0:nk],
                                QT[qs][par * 64:(par + 1) * 64, fc, j * 128:(j + 1) * 128],
                                start=True, stop=True),
                                 reads=blk['kres'] + [('QT', qs)], writes=[PS(bl[par])])
                    for par in range(2):
                        S.op('scalar', lambda e, nk=nk, b=b, par=par, ps=ps, bk=bl[par]: e.activation(
                            out=PT[ps][0:nk, b, par, :], in_=banks[bk][0:nk, :], func=AF.Exp, scale=0.125),
                             reads=[PS(bl[par])], writes=[('PT', ps, b, par)])
                        hsl = slice(par * 8 + 4 * G, par * 8 + 4 * G + 4)
                        if blk['E'] == 'eb':
                            i0 = blk['i0']
                            ein = lambda hsl=hsl, i0=i0: EB[:, hsl, i0:i0 + 2, :]
                            pin = lambda ps=ps, b=b, par=par: PT[ps][:, b, par, :].rearrange('p (a b c) -> p a b c', a=4, b=2)
                        elif blk['E'] == 'm2':
                            ein = lambda hsl=hsl: E_m2[:, hsl, :]
                            pin = lambda ps=ps, b=b, par=par: PT[ps][:, b, par, :].rearrange('p (a b) -> p a b', a=4)
                        elif blk['E'] == 'e5':
                            ein = lambda hsl=hsl: E5[0:80, hsl, :]
                            pin = lambda ps=ps, b=b, par=par: PT[ps][0:80, b, par, :].rearrange('p (a b) -> p a b', a=4)
                        else:
                            ein = lambda hsl=hsl: EM[0:16, hsl, :]
                            pin = lambda ps=ps, b=b, par=par: PT[ps][0:16, b, par, :].rearrange('p (a b) -> p a b', a=4)
                        S.op('vector', lambda e, ein=ein, pin=pin: e.tensor_tensor(out=pin(), in0=pin(), in1=ein(), op=ALU.mult),
                             reads=[('PT', ps, b, par)], writes=[('PT', ps, b, par)])

                def p2_pv(ui, par, jj):
                    u = units[ui]
                    ci, j, G = u
                    ps = ui % 2
                    blocks, tl, si = unit_blocks(u)
                    ab = 4 + par
                    fc = 4 * G + jj
                    h = 2 * fc + par
                    for b, blk in enumerate(blocks):
                        nk = blk['nk']
                        S.op('tensor', lambda e, blk=blk, nk=nk, b=b, par=par, jj=jj, h=h, ab=ab, ps=ps: e.matmul(
                            banks[ab][:, jj * 65:(jj + 1) * 65],
                            PT[ps][0:nk, b, par, jj * 128:(jj + 1) * 128],
                            blk['V'][0:nk, h * 65:(h + 1) * 65],
                            start=(b == 0), stop=(b == len(blocks) - 1)),
                             reads=blk['vres'] + [('PT', ps, b, par)], writes=[PS(ab)])

                def p2_norm(ui, par):
                    u = units[ui]
                    ci, j, G = u
                    si_, t0, nt = p2chunks[ci]
                    t = t0 + j
                    at = t % 2
                    ab = 4 + par
                    accv = lambda ab=ab: banks[ab][:, 0:260].rearrange('p (a b) -> p a b', a=4)
                    rs = slice(par * 4, par * 4 + 4)
                    S.op('vector', lambda e, accv=accv, rs=rs: e.reciprocal(rec[:, rs], accv()[:, :, 64]),
                         reads=[PS(ab)], writes=[('rec', par)])
                    outv = lambda at=at, G=G, par=par: ATT[at][:, :].rearrange('p (a b c) -> p a b c', a=8, b=2)[:, 4 * G:4 * G + 4, par, :]
                    S.op('vector', lambda e, accv=accv, outv=outv, rs=rs: e.tensor_tensor(
                        out=outv(), in0=accv()[:, :, 0:64],
                        in1=rec[:, rs].unsqueeze(2).to_broadcast([128, 4, 64]), op=ALU.mult),
                         reads=[PS(ab), ('rec', par)], writes=[('ATT', at)])
                    if G == 1 and par == 1:
                        S.op('gpsimd', lambda e, at=at, t=t: e.dma_start(out=at_d[t * 128:(t + 1) * 128, :], in_=ATT[at][:]),
                             reads=[('ATT', at)], writes=[('at_d', t)], dma=('sA', at))

                def p2_prev_slot(pu, b):
                    if pu < 0:
                        return
                    if b == 1:
                        p2_pv(pu, 0, 0); p2_pv(pu, 0, 1)
                    elif b == 2:
                        p2_pv(pu, 0, 2); p2_pv(pu, 0, 3); p2_norm(pu, 0)
                    elif b == 3:
                        p2_pv(pu, 1, 0); p2_pv(pu, 1, 1)
                    elif b == 4:
                        p2_pv(pu, 1, 2); p2_pv(pu, 1, 3); p2_norm(pu, 1)

                for fc in range(KC):
                    p2_qpiece(0, fc)
                if len(p2chunks) > 1:
                    p2_load(1)
                uic = {}
                for ui, u in enumerate(units):
                    ci, j, G = u
                    nt = p2chunks[ci][2]
                    k = uic.get(ci, 0); uic[ci] = k + 1
                    ppu = (KC + 2 * nt - 1) // (2 * nt)
                    if 4 <= ui < 16:
                        p2_convert(ui - 4)
                    QSPLIT = [[0, 1, 2], [3], [4, 5], [6], [7]]
                    for b in range(5):
                        p2_qk(ui, b)
                        if ci + 1 < len(p2chunks):
                            if ppu == 1:
                                if k < KC:
                                    p2_qmm(ci + 1, k, QSPLIT[b])
                            elif b == 0:
                                for fc in range(k * ppu, min(KC, (k + 1) * ppu)):
                                    p2_qpiece(ci + 1, fc)
                            if b == 0 and k == 2 * nt - 1 and ci + 2 < len(p2chunks):
                                p2_load(ci + 2)
                        if b > 0:
                            p2_prev_slot(ui - 1, b)
                for b in range(1, 5):
                    p2_prev_slot(len(units) - 1, b)
                S.flush()

        with ExitStack() as es:
            Wg = sb(es, 'Wg', [128, KC, 2048], BF16)
            Wo = sb(es, 'Wo', [128, 16, 1024], BF16)
            fgb = sb(es, 'fgb', [128, D], F32)
            S.op('sync', lambda e: e.dma_start(out=fgb[:], in_=fgb_d), writes=['fgb'], dma='c6b')

            HT3 = [sb(es, 'HT3_%d' % i, [128, KC, 512], BF16) for i in range(2)]
            UW = [sb(es, 'UW%d' % i, [128, 5, D], BF16) for i in range(2)]

            def windows(N):
                out = []
                o = 0
                while o < N:
                    out.append((o, min(112, N - o)))
                    o += 112
                return out
            AB = [sb(es, 'AB%d' % i, [128, 4, D], BF16) for i in range(2)]
            XR = [sb(es, 'XR%d' % i, [128, D], F32) for i in range(3)]
            PLT = sb(es, 'PLT', [128, KC, 512], BF16)
            MIX = sb(es, 'MIX', [128, 16, 512], BF16)
            TH = [sb(es, 'TH%d' % i, [128, 512], F32) for i in range(2)]
            SG = [sb(es, 'SG%d' % i, [128, 512], F32) for i in range(2)]
            Y = [sb(es, 'Y%d' % i, [128, D], F32) for i in range(2)]
            O = [sb(es, 'O%d' % i, [128, D], F32) for i in range(2)]
            junk3 = sb(es, 'junk3', [128, D], BF16)
            ssq3 = sb(es, 'ssq3', [128, 2], F32)
            sd3 = sb(es, 'sd3', [128, 2], F32)
            rs3 = sb(es, 'rs3', [128, 2], F32)

            st3 = dict(pbk=0, tcount=0, xrc=0, ycount=0)

            def nextbank():
                b = st3['pbk']; st3['pbk'] = (b + 1) % 8
                return b

            p3chunks = []
            for (s0, sn) in [(0, TA), (TA, TB)]:
                for (t0, nt) in seq_chunks(s0, sn):
                    p3chunks.append((s0, sn, t0, nt))

            def p3_load(ci):
                s0, sn, t0, nt = p3chunks[ci]
                N = nt * 128
                cs = ci % 2
                first = (t0 == s0)
                last = (t0 + nt == s0 + sn)
                S.op('sync', lambda e, cs=cs, t0=t0, nt=nt, N=N: e.dma_start(out=HT3[cs][:, :, 0:N], in_=hT_v(t0, nt)),
                     reads=[('hT_d', t0)], writes=[('HT3', cs)], dma=('HT3', cs))
                S0, S1 = s0 * 128, (s0 + sn) * 128
                for k, (o, n) in enumerate(windows(N)):
                    r0 = t0 * 128 + o - 8
                    if r0 < S0:
                        S.op('sync', lambda e, cs=cs, k=k: e.dma_start(out=UW[cs][0:8, k, :], in_=u_d[MT * 128 + 8:MT * 128 + 16, :]),
                             reads=[('u_d', MT)], writes=[('UWa', cs, k)], dma=('UW', cs))
                        S.op('sync', lambda e, cs=cs, k=k, S0=S0: e.dma_start(out=UW[cs][8:128, k, :], in_=u_d[S0:S0 + 120, :]),
                             reads=[('u_d', s0)], writes=[('UW', cs, k)], dma=('UW', cs))
                    else:
                        avail = 128
                        S.op('sync', lambda e, cs=cs, k=k, r0=r0, avail=avail: e.dma_start(
                            out=UW[cs][0:avail, k, :], in_=u_d[r0:r0 + avail, :]),
                             reads=[('u_d', tt) for tt in range(r0 // 128, (r0 + avail - 1) // 128 + 1)],
                             writes=[('UW', cs, k), ('UWa', cs, k)], dma=('UW', cs))
                S.op('sync', lambda e, cs=cs, t0=t0, nt=nt: e.dma_start(
                    out=AB[cs][:, 0:nt, :], in_=at_d[t0 * 128:(t0 + nt) * 128, :].rearrange('(a p) f -> p a f', p=128)),
                     reads=[('at_d', tt) for tt in range(t0, t0 + nt)], writes=[('AB', cs)], dma=('AB', cs))

            def p3_pooled(ci):
                s0, sn, t0, nt = p3chunks[ci]
                N = nt * 128
                cs = ci % 2
                last = (t0 + nt == s0 + sn)
                wins = windows(N)
                for cc in range(KC):
                    g = cc // 2
                    wdt = float(POOL_W[g])
                    pb = nextbank()
                    for k, (o, n) in enumerate(wins):
                        lastw = last and (k == len(wins) - 1)
                        bsrc = bandl[n] if lastw else bandw
                        S.op('tensor', lambda e, pb=pb, k=k, o=o, n=n, cs=cs, cc=cc, g=g, bsrc=bsrc: e.matmul(
                            banks[pb][:, o:o + n], UW[cs][:, k, cc * 128:(cc + 1) * 128], bsrc[:, g, 0:n],
                            start=True, stop=True), reads=[('UW', cs, k), ('UWa', cs, k), 'band'], writes=[PS(pb)])
                    o_l, n_l = wins[-1]
                    nint = o_l if last else N
                    if nint > 0:
                        S.op('vector', lambda e, pb=pb, cc=cc, nint=nint, wdt=wdt: e.tensor_scalar(
                            out=PLT[:, cc, 0:nint], in0=banks[pb][:, 0:nint], scalar1=1.0 / wdt, scalar2=None, op0=ALU.mult),
                             reads=[PS(pb)], writes=[('PLT', cc)])
                    if last:
                        S.op('vector', lambda e, pb=pb, cc=cc, g=g, o_l=o_l, n_l=n_l: e.tensor_tensor(
                            out=PLT[:, cc, o_l:o_l + n_l], in0=banks[pb][:, o_l:o_l + n_l],
                            in1=invl[n_l][:, g, :], op=ALU.mult), reads=[PS(pb), 'band'], writes=[('PLT', cc)])

            def p3_gates(ci):
                s0, sn, t0, nt = p3chunks[ci]
                N = nt * 128
                cs = ci % 2
                for dc in range(KC):
                    g = dc // 2
                    pg = nextbank()
                    for kc in range(KC):
                        S.op('tensor', lambda e, pg=pg, dc=dc, kc=kc, cs=cs, N=N: e.matmul(
                            banks[pg][:, 0:N], Wg[:, kc, dc * 128:(dc + 1) * 128], HT3[cs][:, kc, 0:N],
                            start=(kc == 0), stop=(kc == KC - 1)), reads=[('HT3', cs), ('Wg', dc // 2)], writes=[PS(pg)])
                    pm = nextbank()
                    for k2 in range(2):
                        S.op('tensor', lambda e, pm=pm, dc=dc, k2=k2, g=g, N=N: e.matmul(
                            banks[pm][:, 0:N], Wp[:, g * 2 + k2, (dc % 2) * 128:(dc % 2 + 1) * 128], PLT[:, g * 2 + k2, 0:N],
                            start=(k2 == 0), stop=(k2 == 1)), reads=[('PLT', g * 2 + k2), 'Wp'], writes=[PS(pm)])
                    ts = st3['tcount'] % 2; st3['tcount'] += 1
                    S.op('scalar', lambda e, ts=ts, pg=pg, N=N: e.activation(out=TH[ts][:, 0:N], in_=banks[pg][:, 0:N],
                                                                            func=AF.Tanh, scale=0.5),
                         reads=[PS(pg)], writes=[('TH', ts)])
                    S.op('vector', lambda e, ts=ts, pg=pg, N=N: e.scalar_tensor_tensor(
                        out=SG[ts][:, 0:N], in0=TH[ts][:, 0:N], scalar=1.0, in1=banks[pg][:, 0:N],
                        op0=ALU.add, op1=ALU.mult), reads=[('TH', ts), PS(pg)], writes=[('SG', ts)])
                    S.op('vector', lambda e, ts=ts, pm=pm, dc=dc, N=N: e.scalar_tensor_tensor(
                        out=MIX[:, dc, 0:N], in0=banks[pm][:, 0:N], scalar=psc[:, dc:dc + 1], in1=SG[ts][:, 0:N],
                        op0=ALU.mult, op1=ALU.mult), reads=[('SG', ts), PS(pm), 'psc'], writes=[('MIX', dc)])
                for fc in range(KC):
                    pg = nextbank()
                    for kc in range(KC):
                        S.op('tensor', lambda e, pg=pg, fc=fc, kc=kc, cs=cs, N=N: e.matmul(
                            banks[pg][:, 0:N], Wg[:, kc, 1024 + fc * 128:1024 + (fc + 1) * 128], HT3[cs][:, kc, 0:N],
                            start=(kc == 0), stop=(kc == KC - 1)), reads=[('HT3', cs), ('Wg', 4 + fc // 2)], writes=[PS(pg)])
                    pa = nextbank()
                    pab = banks[pa]
                    for j in range(nt):
                        S.op('tensor', lambda e, pab=pab, j=j, fc=fc, cs=cs: e.matmul(
                            pab[:, j * 128:(j + 1) * 128], AB[cs][:, j, fc * 128:(fc + 1) * 128], ident[:],
                            start=True, stop=True),
                             reads=[('AB', cs), 'ident'], writes=[PS(pa)])
                    ts = st3['tcount'] % 2; st3['tcount'] += 1
                    S.op('scalar', lambda e, ts=ts, pg=pg, N=N: e.activation(out=TH[ts][:, 0:N], in_=banks[pg][:, 0:N],
                                                                            func=AF.Tanh, scale=0.5),
                         reads=[PS(pg)], writes=[('TH', ts)])
                    S.op('vector', lambda e, ts=ts, pg=pg, N=N: e.scalar_tensor_tensor(
                        out=SG[ts][:, 0:N], in0=TH[ts][:, 0:N], scalar=1.0, in1=banks[pg][:, 0:N],
                        op0=ALU.add, op1=ALU.mult), reads=[('TH', ts), PS(pg)], writes=[('SG', ts)])
                    S.op('vector', lambda e, ts=ts, pab=pab, fc=fc, N=N: e.tensor_tensor(
                        out=MIX[:, 8 + fc, 0:N], in0=pab[:, 0:N], in1=SG[ts][:, 0:N], op=ALU.mult),
                         reads=[('SG', ts), PS(pa)], writes=[('MIX', 8 + fc)])
            def p3_out(ci):
                s0, sn, t0, nt = p3chunks[ci]
                for j in range(nt):
                    t = t0 + j
                    xr = st3['xrc'] % 3; st3['xrc'] += 1
                    S.op('sync', lambda e, xr=xr, t=t: e.dma_start(out=XR[xr][:], in_=xall[t * 128:(t + 1) * 128, :]),
                         writes=[('XR', xr)], dma=('XR', xr))
                    ys = st3['ycount'] % 2; st3['ycount'] += 1
                    for cg in range(2):
                        po = nextbank()
                        for kc in range(16):
                            S.op('tensor', lambda e, po=po, kc=kc, cg=cg, j=j: e.matmul(
                                banks[po][:, :], MIX[:, kc, j * 128:(j + 1) * 128], Wo[:, kc, cg * 512:(cg + 1) * 512],
                                start=(kc == 0), stop=(kc == 15)), reads=[('MIX', kc), ('Wo', kc // 4)], writes=[PS(po)])
                        S.op('vector', lambda e, po=po, cg=cg, ys=ys, xr=xr: e.tensor_tensor(
                            out=Y[ys][:, cg * 512:(cg + 1) * 512], in0=banks[po][:, :], in1=XR[xr][:, cg * 512:(cg + 1) * 512],
                            op=ALU.add), reads=[PS(po), ('XR', xr)], writes=[('Y', ys)])
                    S.op('scalar', lambda e, ys=ys: e.activation(out=junk3[:], in_=Y[ys][:], func=AF.Square,
                                                                accum_out=ssq3[:, ys:ys + 1]),
                         reads=[('Y', ys)], writes=['junk3', ('ssq3', ys)])
                    S.op('scalar', lambda e, ys=ys: e.activation(out=sd3[:, ys:ys + 1], in_=ssq3[:, ys:ys + 1], func=AF.Sqrt,
                                                                bias=epsb[:], scale=1.0 / D),
                         reads=[('ssq3', ys), 'epsb'], writes=[('sd3', ys)])
                    S.op('vector', lambda e, ys=ys: e.reciprocal(rs3[:, ys:ys + 1], sd3[:, ys:ys + 1]),
                         reads=[('sd3', ys)], writes=[('rs3', ys)])
                    S.op('vector', lambda e, ys=ys: e.scalar_tensor_tensor(
                        out=O[ys][:], in0=Y[ys][:], scalar=rs3[:, ys:ys + 1], in1=fgb[:], op0=ALU.mult, op1=ALU.mult),
                         reads=[('Y', ys), ('rs3', ys), 'fgb'], writes=[('O', ys)])
                    S.op('gpsimd', lambda e, ys=ys, t=t: e.dma_start(out=y_d[t * 128:(t + 1) * 128, :], in_=O[ys][:]),
                         reads=[('O', ys)], writes=[('y_d', t)], dma=('sO', ys))

            p3_load(0)
            for cb in range(8):
                S.op('sync', lambda e, cb=cb: e.dma_start(out=Wg[:, :, cb * 256:(cb + 1) * 256], in_=wg_s[:, :, cb * 256:(cb + 1) * 256]),
                     writes=[('Wg', cb)], dma=('wg', cb))
            for cb in range(4):
                S.op('sync', lambda e, cb=cb: e.dma_start(out=Wo[:, cb * 4:(cb + 1) * 4, :], in_=wo_s[:, cb * 4:(cb + 1) * 4, :]),
                     writes=[('Wo', cb)], dma=('wo', cb))
            p3_pooled(0)
            for ci in range(len(p3chunks)):
                if ci + 1 < len(p3chunks):
                    p3_load(ci + 1)
                p3_gates(ci)
                if ci + 1 < len(p3chunks):
                    p3_pooled(ci + 1)
                p3_out(ci)
            S.flush()
    return nc


def _constants():
    ident = np.eye(128, dtype=np.float32)
    consts = dict(ident=ident)
    tp = np.arange(128)[:, None] - 8
    bandw = np.zeros((128, 4, 112), np.float32)
    for g, w in enumerate(POOL_W):
        t = np.arange(112)[None, :]
        inwin = (tp >= t - w // 2) & (tp <= t + w // 2 - 1)
        bandw[:, g, :] = inwin.astype(np.float32) - w * (tp == t)
    consts['bandw'] = bandw
    for n in (64, 32):
        bl = np.zeros((128, 4, n), np.float32)
        il = np.zeros((128, 4, n), np.float32)
        for g, w in enumerate(POOL_W):
            t = np.arange(n)[None, :]
            inwin = (tp >= t - w // 2) & (tp <= t + w // 2 - 1) & (tp <= n - 1)
            cnt = np.minimum(t + w // 2 - 1, n - 1) - (t - w // 2) + 1
            bl[:, g, :] = inwin.astype(np.float32) - cnt.astype(np.float32) * (tp == t)
            il[:, g, :] = 1.0 / cnt.astype(np.float32)
        consts['bandl%d' % n] = bl
        consts['invl%d' % n] = il
    cols = np.arange(64)
    cstart = np.clip(cols - 8, 0, 48)
    kc = np.arange(64)
    consts['mask01'] = ((kc[:, None] >= cstart[None, :]) & (kc[:, None] < cstart[None, :] + 16)).astype(np.float32)
    return consts


_CACHE = {}


def kernel(x_prompt, x_sample, meta_tokens, norm_g, w_in, w_pool, pool_scale, rpb, meta_bias, w_out, final_g):
    f = lambda a: np.ascontiguousarray(np.asarray(a, dtype=np.float32))
    x_prompt, x_sample, meta_tokens = f(x_prompt), f(x_sample), f(meta_tokens)
    norm_g, w_in, w_pool, pool_scale = f(norm_g)[0], f(w_in)[0], f(w_pool)[0], f(pool_scale)[0]
    rpb, meta_bias, w_out, final_g = f(rpb)[0], f(meta_bias)[0], f(w_out)[0], f(final_g)
    if 'nc' not in _CACHE:
        _CACHE['nc'] = build_program()
        _CACHE['const'] = _constants()
    nc = _CACHE['nc']
    C = _CACHE['const']
    hperm = np.array([2 * (hp % 8) + hp // 8 for hp in range(NH)])
    kc = np.arange(64)[:, None]
    qc = np.arange(64)[None, :]
    cidx = np.clip(kc - qc + 15, 0, 30)
    ridx = np.clip(14 - (np.arange(16) - 1), 0, 14)
    b0 = rpb[hperm][:, ridx][:, :, cidx]
    b0 = np.ascontiguousarray(b0.transpose(2, 0, 1, 3))
    mbT = np.ascontiguousarray(meta_bias[hperm].T)
    gT = np.ascontiguousarray(norm_g.reshape(KC, 128).T)
    psT = np.ascontiguousarray(pool_scale.reshape(KC, 128).T)
    fgb = np.ascontiguousarray(np.broadcast_to(final_g[None, :], (128, D)))
    metax = np.zeros((128, D), np.float32)
    metax[:16] = meta_tokens
    in_maps = []
    for i in range(NCORES):
        half = i % 2
        tok0 = 0 if half == 0 else 30 * 128
        xb = x_sample[i // 2][tok0:tok0 + TB * 128]
        xall = np.concatenate([x_prompt[i], xb, metax], axis=0)
        m = dict(xall=xall, w_in=w_in, w_pool=w_pool, w_out=w_out, gT=gT, psT=psT, fgb=fgb, mbT=mbT, b0=b0)
        m.update(C)
        in_maps.append(m)
    res = run_bass_kernel_spmd(nc, in_maps, core_ids=list(range(NCORES)))
    y_prompt = np.empty((8, 2048, D), np.float32)
    y_sample = np.empty((4, 8192, D), np.float32)
    for i in range(NCORES):
        y = res.results[i]['y']
        y_prompt[i] = y[:TA * 128]
        yb = y[TA * 128:]
        if i % 2 == 0:
            y_sample[i // 2][:4096] = yb[:4096]
        else:
            y_sample[i // 2][4096:] = yb[2 * 128:]
    return (y_prompt, y_sample)
```
